# Optimizing a Trainium2 kernel written in Bass

```python
import math
import jax, jax.numpy as jnp
from jax import lax
import numpy as np

D_MODEL = 1024
BATCH = 8
SEQ = 2048
DEPTH = 4

CTX_LEN = 256
GRID_W = 64
N_EVEN = (DEPTH + 1) // 2
N_ODD = DEPTH // 2

A_HEADS = 4
A_DQK = 64
A_DV = 2 * A_DQK
A_WIDTH = A_HEADS * A_DV
B_HEADS = 4
B_DK = 64
B_DV = 128
B_WIDTH = B_HEADS * B_DV
B_GATE_RANK = 16
B_GATE_NORM = 16.0
C_HEADS = 8
C_DH = D_MODEL // C_HEADS
C_WIDTH = C_HEADS * C_DH

CHUNK = 64
Q_BLOCK = 128
ROPE_BASE = 10000.0
EPS = 1e-6
D_FF = ((8 * D_MODEL + 3 * 256 - 1) // (3 * 256)) * 256

EVEN_SIZES = (2 * A_HEADS * A_DQK, 2 * A_HEADS * A_DQK, A_WIDTH,
              B_HEADS * B_DK, B_HEADS * B_DK, B_WIDTH, B_WIDTH, 2 * B_GATE_RANK)
EVEN_COLS = sum(EVEN_SIZES)
EVEN_SPLITS = tuple(int(s) for s in np.cumsum(EVEN_SIZES)[:-1])
ODD_COLS = 5 * C_WIDTH

kernel_name = "hybrid_diffattn_gla_hgrn2_prefix_dit"

F32 = jnp.float32


def rms_norm(x, gain):
    xf = x.astype(F32)
    y = xf * lax.rsqrt(jnp.mean(xf * xf, axis=-1, keepdims=True) + EPS)
    return (y * gain.astype(F32)).astype(x.dtype)


def axial_rope_tables(rows):
    row_ids = jnp.repeat(jnp.arange(rows), GRID_W).astype(F32)
    col_ids = jnp.tile(jnp.arange(GRID_W), rows).astype(F32)
    n_axis = A_DQK // 2
    inv = ROPE_BASE ** (-jnp.arange(0, n_axis, 2, dtype=F32) / n_axis)
    ang = jnp.concatenate([row_ids[:, None] * inv, col_ids[:, None] * inv], axis=-1)
    return jnp.cos(ang), jnp.sin(ang)


def apply_axial_rope(x, cos, sin):
    def rot(u, cs, sn):
        cs = cs[:, None, None, :]
        sn = sn[:, None, None, :]
        u1, u2 = jnp.split(u, 2, axis=-1)
        return jnp.concatenate([u1 * cs - u2 * sn, u2 * cs + u1 * sn], axis=-1)
    xr, xc = jnp.split(x, 2, axis=-1)
    cr, cc = jnp.split(cos, 2, axis=-1)
    sr, sc = jnp.split(sin, 2, axis=-1)
    return jnp.concatenate([rot(xr, cr, sr), rot(xc, cc, sc)], axis=-1).astype(x.dtype)


def diff_attend(q, k, v, lam):
    s = jnp.einsum('bqhcd,bkhcd->bhcqk', q, k).astype(F32) * (A_DQK ** -0.5)
    p = jax.nn.softmax(s, axis=-1)
    a = (p[:, :, 0] - lam * p[:, :, 1]).astype(v.dtype)
    return jnp.einsum('bhqk,bkhe->bqhe', a, v)


def chunked_gated_scan(q, k, v, logf, s0):
    Bn, L, H, _ = q.shape
    dv = v.shape[-1]
    n = L // CHUNK

    def to_chunks(a):
        return jnp.moveaxis(a.reshape(Bn, n, CHUNK, H, a.shape[-1]), 1, 0)

    mask = jnp.tril(jnp.ones((CHUNK, CHUNK), dtype=bool))[None, :, :, None, None]

    def step(S, inp):
        qc, kc, vc, gc = inp
        qf, kf, vf = qc.astype(F32), kc.astype(F32), vc.astype(F32)
        b = jnp.cumsum(gc.astype(F32), axis=1)
        o_inter = jnp.einsum('bthd,bhde->bthe', qf * jnp.exp(b), S)
        rel = b[:, :, None] - b[:, None, :]
        dec = jnp.exp(jnp.where(mask, rel, -jnp.inf))
        att = jnp.einsum('bthd,bshd,btshd->bhts', qf, kf, dec)
        o_intra = jnp.einsum('bhts,bshe->bthe', att, vf)
        b_last = b[:, -1]
        k_dec = kf * jnp.exp(b_last[:, None] - b)
        S_new = jnp.exp(b_last)[..., None] * S + jnp.einsum('bshd,bshe->bhde', k_dec, vf)
        return S_new, (o_inter + o_intra).astype(v.dtype)

    S, o = lax.scan(step, s0.astype(F32), (to_chunks(q), to_chunks(k), to_chunks(v), to_chunks(logf)))
    return jnp.moveaxis(o, 0, 1).reshape(Bn, L, H, dv), S


def bidir_scan(qc, kfc, gfc, kbc, gbc, vc, ql, kfl, gfl, kbl, gbl, vl, need_ctx):
    Bn, _, H, dk = qc.shape
    s0 = jnp.zeros((Bn, H, dk, vc.shape[-1]), F32)
    fl = lambda a: jnp.flip(a, axis=1)
    oc_f, sc_f = chunked_gated_scan(qc, kfc, vc, gfc, s0)
    ol_f, _ = chunked_gated_scan(ql, kfl, vl, gfl, sc_f)
    oc_b, sc_b = chunked_gated_scan(fl(qc), fl(kbc), fl(vc), fl(gbc), s0)
    ol_b, _ = chunked_gated_scan(fl(ql), fl(kbl), fl(vl), fl(gbl), sc_b)
    ol = ol_f + fl(ol_b)
    oc = oc_f + fl(oc_b) if need_ctx else None
    return oc, ol


def prep_even(p, qk_gain, w_gate_up, b_gate_up):
    Bn, L = p.shape[:2]
    aq, ak, av, bq, bk, bv, bg, blr = jnp.split(p, EVEN_SPLITS, axis=-1)
    aq = rms_norm(aq.reshape(Bn, L, A_HEADS, 2, A_DQK), qk_gain[0])
    ak = rms_norm(ak.reshape(Bn, L, A_HEADS, 2, A_DQK), qk_gain[1])
    av = av.reshape(Bn, L, A_HEADS, A_DV)
    bq = bq.reshape(Bn, L, B_HEADS, B_DK) * (B_DK ** -0.5)
    bk = bk.reshape(Bn, L, B_HEADS, B_DK)
    bv = bv.reshape(Bn, L, B_HEADS, B_DV)
    lr_f, lr_b = jnp.split(blr, 2, axis=-1)
    gk_f = (jax.nn.log_sigmoid((lr_f @ w_gate_up[0] + b_gate_up[0]).astype(F32)) / B_GATE_NORM).reshape(Bn, L, B_HEADS, B_DK)
    gk_b = (jax.nn.log_sigmoid((lr_b @ w_gate_up[1] + b_gate_up[1]).astype(F32)) / B_GATE_NORM).reshape(Bn, L, B_HEADS, B_DK)
    return aq, ak, av, (bq, bk, gk_f, bk, gk_b, bv), bg


def even_mixer(pc, pl, rope_cos, rope_sin, qk_gain, lam, sub_gain, lambda_init,
               w_gate_up, b_gate_up, gla_gain, need_ctx):
    qc, kc, vc, bc, bgc = prep_even(pc, qk_gain, w_gate_up, b_gate_up)
    ql, kl, vl, bl, bgl = prep_even(pl, qk_gain, w_gate_up, b_gate_up)
    ql = apply_axial_rope(ql, rope_cos, rope_sin)
    kl = apply_axial_rope(kl, rope_cos, rope_sin)
    k_all = jnp.concatenate([kc, kl], axis=1)
    v_all = jnp.concatenate([vc, vl], axis=1)
    Bn, T = ql.shape[:2]
    nb = T // Q_BLOCK
    q_blocks = jnp.moveaxis(ql.reshape(Bn, nb, Q_BLOCK, A_HEADS, 2, A_DQK), 1, 0)
    o_blocks = lax.map(lambda qb: diff_attend(qb, k_all, v_all, lam), q_blocks)
    oa_l = jnp.moveaxis(o_blocks, 0, 1).reshape(Bn, T, A_HEADS, A_DV)

    def post_a(o):
        return (rms_norm(o, sub_gain) * (1.0 - lambda_init)).reshape(o.shape[0], o.shape[1], A_WIDTH)

    def post_b(o, g):
        return (rms_norm(o, gla_gain) * jax.nn.silu(g).reshape(o.shape)).reshape(o.shape[0], o.shape[1], B_WIDTH)

    ob_c, ob_l = bidir_scan(*bc, *bl, need_ctx)
    ol = jnp.concatenate([post_a(oa_l), post_b(ob_l, bgl)], axis=-1)
    oc = None
    if need_ctx:
        oa_c = diff_attend(qc, kc, vc, lam)
        oc = jnp.concatenate([post_a(oa_c), post_b(ob_c, bgc)], axis=-1)
    return oc, ol


def prep_odd(p, lb_f, lb_b):
    Bn, L = p.shape[:2]
    hd = lambda a: a.reshape(Bn, L, C_HEADS, C_DH)
    q, ff, fb, i, g = jnp.split(p, 5, axis=-1)
    q = hd(jax.nn.silu(q)) * (C_DH ** -0.5)
    f_f = lb_f + (1.0 - lb_f) * jax.nn.sigmoid(ff.astype(F32))
    f_b = lb_b + (1.0 - lb_b) * jax.nn.sigmoid(fb.astype(F32))
    return (q, hd(1.0 - f_f), hd(jnp.log(f_f)), hd(1.0 - f_b), hd(jnp.log(f_b)), hd(i)), g


def odd_mixer(pc, pl, lb_f, lb_b, out_gain, need_ctx):
    sc, gc = prep_odd(pc, lb_f, lb_b)
    sl, gl = prep_odd(pl, lb_f, lb_b)
    oc, ol = bidir_scan(*sc, *sl, need_ctx)

    def post(o, g):
        return (rms_norm(o, out_gain) * jax.nn.silu(g).reshape(o.shape)).reshape(o.shape[0], o.shape[1], C_WIDTH)

    return (post(oc, gc) if need_ctx else None), post(ol, gl)


def swiglu(h, w_in, w_out):
    gate, up = jnp.split(h @ w_in, 2, axis=-1)
    return (jax.nn.silu(gate) * up) @ w_out


def setup_inputs(seed: int = 0) -> dict:
    key = jax.random.key(seed)
    ks = jax.random.split(key, 24)
    D = D_MODEL
    nrm = lambda k, shape, scale: jax.random.normal(k, shape, jnp.float32) * scale
    return {
        "x": nrm(ks[0], (BATCH, SEQ, D), 1.0),
        "c": nrm(ks[1], (BATCH, D), 1.0),
        "ctx": nrm(ks[2], (BATCH, CTX_LEN, D), 1.0),
        "c_ctx": nrm(ks[3], (D,), 1.0),
        "w_ada": nrm(ks[4], (DEPTH, D, 6 * D), 0.5 * D ** -0.5),
        "b_ada": nrm(ks[5], (DEPTH, 6 * D), 0.02),
        "norm1_gain": 1.0 + nrm(ks[6], (DEPTH, D), 0.02),
        "norm2_gain": 1.0 + nrm(ks[7], (DEPTH, D), 0.02),
        "w_in_even": nrm(ks[8], (N_EVEN, D, EVEN_COLS), D ** -0.5),
        "qk_gain_a": 1.0 + nrm(ks[9], (N_EVEN, 2, A_DQK), 0.02),
        "lambda_a": nrm(ks[10], (N_EVEN, 4, A_DQK), 0.1),
        "subln_gain_a": 1.0 + nrm(ks[11], (N_EVEN, A_DV), 0.02),
        "w_gate_up_b": nrm(ks[12], (N_EVEN, 2, B_GATE_RANK, B_HEADS * B_DK), B_GATE_RANK ** -0.5),
        "b_gate_up_b": nrm(ks[13], (N_EVEN, 2, B_HEADS * B_DK), 0.01),
        "onorm_gain_b": 1.0 + nrm(ks[14], (N_EVEN, B_DV), 0.02),
        "w_out_even": nrm(ks[15], (N_EVEN, A_WIDTH + B_WIDTH, D), (A_WIDTH + B_WIDTH) ** -0.5),
        "w_in_odd": nrm(ks[16], (N_ODD, D, ODD_COLS), D ** -0.5),
        "lb_raw_c": nrm(ks[17], (2, DEPTH, C_WIDTH), 0.5),
        "onorm_gain_c": 1.0 + nrm(ks[18], (N_ODD, C_DH), 0.02),
        "w_out_odd": nrm(ks[19], (N_ODD, C_WIDTH, D), C_WIDTH ** -0.5),
        "w_ffn_in": nrm(ks[20], (DEPTH, D, 2 * D_FF), D ** -0.5),
        "w_ffn_out": nrm(ks[21], (DEPTH, D_FF, D), D_FF ** -0.5),
    }


def reference(x, c, ctx, c_ctx, w_ada, b_ada, norm1_gain, norm2_gain, w_in_even, qk_gain_a,
              lambda_a, subln_gain_a, w_gate_up_b, b_gate_up_b, onorm_gain_b, w_out_even,
              w_in_odd, lb_raw_c, onorm_gain_c, w_out_odd, w_ffn_in, w_ffn_out):
    rows = x.shape[1] // GRID_W
    rope_cos, rope_sin = axial_rope_tables(rows)
    lb_p = jax.nn.softmax(lb_raw_c.astype(F32), axis=1)
    lower_bounds = jnp.cumsum(lb_p, axis=1) - lb_p[:, :1]
    z = ctx
    sc = jax.nn.silu(c)
    scc = jax.nn.silu(c_ctx)
    for l in range(DEPTH):
        need_ctx = l < DEPTH - 1
        mod_l = [m[..., None, :] for m in jnp.split(sc @ w_ada[l] + b_ada[l], 6, axis=-1)]
        mod_c = [m[..., None, :] for m in jnp.split(scc @ w_ada[l] + b_ada[l], 6, axis=-1)]
        hl = rms_norm(x, norm1_gain[l]) * (1.0 + mod_l[1]) + mod_l[0]
        hc = rms_norm(z, norm1_gain[l]) * (1.0 + mod_c[1]) + mod_c[0]
        if l % 2 == 0:
            j = l // 2
            lambda_init = 0.8 - 0.6 * math.exp(-0.3 * l)
            lq1, lk1, lq2, lk2 = lambda_a[j].astype(F32)
            lam = jnp.exp(jnp.sum(lq1 * lk1)) - jnp.exp(jnp.sum(lq2 * lk2)) + lambda_init
            oc, ol = even_mixer(hc @ w_in_even[j], hl @ w_in_even[j], rope_cos, rope_sin,
                                qk_gain_a[j], lam, subln_gain_a[j], lambda_init,
                                w_gate_up_b[j], b_gate_up_b[j], onorm_gain_b[j], need_ctx)
            w_out = w_out_even[j]
        else:
            j = l // 2
            oc, ol = odd_mixer(hc @ w_in_odd[j], hl @ w_in_odd[j], lower_bounds[0, l],
                               lower_bounds[1, l], onorm_gain_c[j], need_ctx)
            w_out = w_out_odd[j]
        x = x + mod_l[2] * (ol @ w_out)
        hl = rms_norm(x, norm2_gain[l]) * (1.0 + mod_l[4]) + mod_l[3]
        x = x + mod_l[5] * swiglu(hl, w_ffn_in[l], w_ffn_out[l])
        if need_ctx:
            z = z + mod_c[2] * (oc @ w_out)
            hc = rms_norm(z, norm2_gain[l]) * (1.0 + mod_c[4]) + mod_c[3]
            z = z + mod_c[5] * swiglu(hc, w_ffn_in[l], w_ffn_out[l])
    return x
```

```python
import contextlib
import math
import numpy as np
import concourse.bass as bass
import concourse.mybir as mybir
from concourse.bass_utils import run_bass_kernel_spmd

F32 = mybir.dt.float32
BF16 = mybir.dt.bfloat16
ALU = mybir.AluOpType
AF = mybir.ActivationFunctionType

ENGS = ("pe", "act", "dve", "pool", "sp")
SELF_SYNC = {"pe": False, "act": True, "dve": True, "pool": True, "sp": False}

D = 1024
T = 2304
NT = 18
CTX = 256
TGS = [(0, 256), (256, 512), (768, 512), (1280, 512), (1792, 512)]
DEPTH = 4
DFF = 2816
EPS = 1e-6
NS = 6
SLOT = 2048


class Prog:
    def __init__(self, nc):
        self.nc = nc
        self.ops = {e: [] for e in ENGS}
        self.hoisted = {e: [] for e in ENGS}
        self.count = {e: 0 for e in ENGS}
        self.waited = {e: {} for e in ENGS}
        self.last_w = {}
        self.readers = {}
        self.dma_count = {}
        self.dma_sems = []
        self.used = {e: set() for e in ENGS}

    def _deps(self, reads, writes):
        deps = []
        for k in reads:
            deps.extend(self.last_w.get(k, ()))
        for k in writes:
            deps.extend(self.last_w.get(k, ()))
            deps.extend(self.readers.get(k, ()))
        return deps

    def _waits(self, eng, deps, cache=True):
        need = {}
        for (s, v) in deps:
            if s in ENGS:
                if s == eng and not SELF_SYNC[eng]:
                    continue
            else:
                v = self.dma_count[s]
            if v > need.get(s, 0):
                need[s] = v
        waits = []
        for s, v in need.items():
            if (not cache) or self.waited[eng].get(s, 0) < v:
                if cache:
                    self.waited[eng][s] = v
                waits.append((s, v))
                if s in ENGS:
                    self.used[s].add(v)
        return waits

    def _record(self, tok, reads, writes):
        for k in writes:
            self.last_w[k] = [tok]
            self.readers[k] = []
        for k in reads:
            if k not in writes:
                self.readers.setdefault(k, []).append(tok)

    def alias(self, new_keys, old_keys):
        toks = []
        for k in old_keys:
            toks.extend(self.last_w.get(k, ()))
            toks.extend(self.readers.get(k, ()))
        toks = list(set(toks))
        for k in new_keys:
            self.last_w[k] = list(toks)
            self.readers[k] = []

    def op(self, eng, fn, reads=(), writes=()):
        waits = self._waits(eng, self._deps(reads, writes))
        self.count[eng] += 1
        tok = (eng, self.count[eng])
        self.ops[eng].append((waits, fn, tok))
        self._record(tok, reads, writes)
        return tok

    def dma(self, eng, sem, fn, reads=(), writes=(), hoist_pos=None):
        if sem not in self.dma_count:
            self.dma_count[sem] = 0
            self.dma_sems.append(sem)
        waits = self._waits(eng, self._deps(reads, writes), cache=(hoist_pos is None))
        self.dma_count[sem] += 16
        tok = (sem, self.dma_count[sem])
        if hoist_pos is None:
            self.ops[eng].append((waits, fn, tok))
        else:
            self.hoisted[eng].append((hoist_pos, len(self.hoisted[eng]), (waits, fn, tok)))
        self._record(tok, reads, writes)
        return tok

    def pos(self, eng):
        return len(self.ops[eng])

    def finish(self, eng="sp"):
        waits = []
        for s in self.dma_sems:
            waits.append((s, self.dma_count[s]))
        for e in ENGS:
            if e != eng and self.count[e] > 0:
                waits.append((e, self.count[e]))
                self.used[e].add(self.count[e])
        self.ops[eng].append((waits, None, None))

    def emit(self):
        nc = self.nc
        with contextlib.ExitStack() as st:
            sems = {}
            for e in ENGS:
                sems[e] = st.enter_context(nc.semaphore("s_" + e))
            for s in self.dma_sems:
                sems[s] = st.enter_context(nc.semaphore("d_" + s))
            rank = {e: {v: i + 1 for i, v in enumerate(sorted(self.used[e]))} for e in ENGS}
            block = st.enter_context(nc.Block())

            def run(e, h):
                hoist = sorted(self.hoisted[e], key=lambda x: (x[0], x[1]))
                hi = 0
                base = self.ops[e]
                seq = []
                for i, o in enumerate(base):
                    while hi < len(hoist) and hoist[hi][0] <= i:
                        seq.append(hoist[hi][2])
                        hi += 1
                    seq.append(o)
                while hi < len(hoist):
                    seq.append(hoist[hi][2])
                    hi += 1
                for waits, fn, tok in seq:
                    for (s, v) in waits:
                        h.wait_ge(sems[s], rank[s][v] if s in ENGS else v)
                    if fn is None:
                        continue
                    ins = fn(h)
                    if tok[0] in ENGS:
                        if tok[1] in rank[tok[0]]:
                            ins.then_inc(sems[tok[0]], 1)
                    else:
                        ins.then_inc(sems[tok[0]], 16)

            @block.tensor
            def _(h):
                run("pe", h)

            @block.scalar
            def _(h):
                run("act", h)

            @block.vector
            def _(h):
                run("dve", h)

            @block.gpsimd
            def _(h):
                run("pool", h)

            @block.sync
            def _(h):
                run("sp", h)


def lambda_init(l):
    return 0.8 - 0.6 * math.exp(-0.3 * l)


def host_consts():
    c = {}
    c["identf"] = np.eye(128, dtype=np.float32)
    c["ones"] = np.ones((128, 128), np.float32)
    p = np.arange(128)
    c["blk64"] = (p[:, None] // 64 == p[None, :] // 64).astype(np.float32)
    Rm = np.zeros((128, 128), np.float32)
    for q in range(128):
        w = (q % 64) % 32
        if w < 16:
            Rm[q + 16, q] = -1.0
        else:
            Rm[q - 16, q] = 1.0
    c["rotm"] = Rm
    s = p[:, None]
    t = p[None, :]
    same = (s // 64 == t // 64)
    c["maskf"] = (same & (s <= t)).astype(np.float32)
    c["maskb"] = (same & (s >= t)).astype(np.float32)
    tt = np.arange(512)
    c["rmf"] = np.broadcast_to((tt % 64 != 0).astype(np.float32), (128, 512)).copy()
    c["rmb"] = np.broadcast_to((tt % 64 != 63).astype(np.float32), (128, 512)).copy()
    tok = np.arange(2048)
    row = (tok // 64).astype(np.float32)
    col = (tok % 64).astype(np.float32)
    inv = (np.float32(10000.0) ** (-np.arange(0, 32, 2, dtype=np.float32) / np.float32(32))).astype(np.float32)
    cosT = np.zeros((128, 2048), np.float32)
    sinT = np.zeros((128, 2048), np.float32)
    for q in range(128):
        d = q % 64
        axis = d // 32
        j = d % 16
        ang = ((row if axis == 0 else col) * inv[j]).astype(np.float32)
        cosT[q] = np.cos(ang).astype(np.float32)
        sinT[q] = np.sin(ang).astype(np.float32)
    c["cosT"] = cosT
    c["sinT"] = sinT
    return c


CONST_BF = ("ones", "blk64", "rotm", "maskf", "maskb", "identb")


def build_program(cfg):
    nlayers = cfg.get("nlayers", DEPTH)
    do_att = cfg.get("att", True)
    do_gla = cfg.get("gla", True)
    do_odd = cfg.get("odd", True)
    do_ffn = cfg.get("ffn", True)

    nc = bass.Bass("TRN2", target_bir_lowering=False)
    dram = {}

    def din(name, shape):
        dram[name] = nc.dram_tensor(name, list(shape), F32, kind="ExternalInput").ap()
        return dram[name]

    xin = din("xin", [T, D])
    y = nc.dram_tensor("y", [2048, D], F32, kind="ExternalOutput").ap()
    w_ada = din("w_ada", [DEPTH, D, 6 * D])
    w_in_even = din("w_in_even", [2, D, 3104])
    w_out_even = din("w_out_even", [2, D, D])
    w_in_odd = din("w_in_odd", [2, D, 5120])
    w_out_odd = din("w_out_odd", [2, D, D])
    w_ffn_in = din("w_ffn_in", [DEPTH, D, 2 * DFF])
    w_ffn_out = din("w_ffn_out", [DEPTH, DFF, D])
    for nm in ("identf", "ones", "blk64", "rotm", "maskf", "maskb"):
        din(nm, [128, 128])
    din("rmf", [128, 512])
    din("rmb", [128, 512])
    din("cosT", [128, 2048])
    din("sinT", [128, 2048])
    din("cvec", [128, 16])
    din("badaT", [128, 4 * 48])
    din("g1T", [128, 32])
    din("g2T", [128, 32])
    din("qkg", [128, 4])
    din("lamb", [128, 2 * 256])
    din("sublnT", [128, 2])
    din("wgu", [32, 2 * 2 * 256])
    din("bgu", [64, 2 * 2 * 4])
    din("glagT", [128, 2])
    din("lbraw", [128, 2 * 4 * 8])
    din("ognT", [128, 2])

    P = Prog(nc)
    st = contextlib.ExitStack()

    def sb(name, shape, dt):
        return st.enter_context(nc.sbuf_tensor("s_" + name, list(shape), dt))

    xT = sb("xT", [128, 8, T], F32)
    hT = sb("hT", [128, 8, T], BF16)
    wring = sb("wring", [128, NS, SLOT], BF16)
    ps = [st.enter_context(nc.psum_tensor(f"ps{i}", [128, 512], F32)) for i in range(8)]
    psb = [p_[:].bitcast(BF16) for p_ in ps]

    cst = {}
    cst["identf"] = sb("c_identf", [128, 128], F32)
    for nm in ("ones", "blk64", "rotm", "maskf", "maskb", "identb"):
        cst[nm] = sb("c_" + nm, [128, 128], BF16)
    cst["rmf"] = sb("c_rmf", [128, 512], BF16)
    cst["rmb"] = sb("c_rmb", [128, 512], BF16)
    cvec = sb("cvec", [128, 8, 2], F32)
    scT = sb("scT", [128, 8, 2], BF16)
    badaT = sb("badaT", [128, 4, 48, 1], F32)
    g1T = sb("g1T", [128, 4, 8, 1], F32)
    g2T = sb("g2T", [128, 4, 8, 1], F32)
    modT = sb("modT", [128, 4, 48, 2], F32)
    A1 = sb("A1", [128, 4, 8, 2], F32)
    A2 = sb("A2", [128, 4, 8, 2], F32)
    qkg = sb("qkg", [128, 2, 2], F32)
    lamb = sb("lamb", [128, 2, 4, 64], F32)
    lamw = sb("lamw", [128, 2, 2, 64], F32)
    lams = sb("lams", [128, 2, 4], F32)
    nlam = sb("nlam", [128, 2], F32)
    sublnT = sb("sublnT", [128, 2], F32)
    wgu = sb("wgu", [32, 2, 2, 256], BF16)
    bgu = sb("bgu", [64, 2, 2, 4], F32)
    glagT = sb("glagT", [128, 2], F32)
    lbraw = sb("lbraw", [128, 2, 4, 8], F32)
    lbe = sb("lbe", [128, 2, 4, 8], F32)
    lbs = sb("lbs", [128, 2, 8], F32)
    lbT = sb("lbT", [128, 2, 2, 8], F32)
    omT = sb("omT", [128, 2, 2, 8], F32)
    ognT = sb("ognT", [128, 2], F32)

    ARENA = 32000
    arena = sb("arena", [128, ARENA], BF16)
    arena_f = arena[:].bitcast(F32)

    class Arena:
        def __init__(self):
            self.off = 0
            self.keys = []
            self.old_keys = []

        def reset(self):
            best = {}
            toks = list(getattr(self, "summary", []))
            for k in self.keys:
                toks.extend(P.last_w.get(k, ()))
                toks.extend(P.readers.get(k, ()))
            for (s_, v_) in toks:
                if v_ > best.get(s_, 0):
                    best[s_] = v_
            self.summary = list(best.items())
            self.keys = []
            self.off = 0

        def alloc(self, name, free_shape, dt, n=1):
            size = int(np.prod(free_shape))
            outs = []
            for i in range(n):
                if dt == F32:
                    if self.off % 2:
                        self.off += 1
                    a = arena_f[:, self.off // 2: self.off // 2 + size]
                    self.off += 2 * size
                else:
                    a = arena[:, self.off: self.off + size]
                    self.off += size
                assert self.off <= ARENA, (name, self.off)
                if len(free_shape) == 2:
                    a = a.rearrange("p (a b) -> p a b", b=free_shape[1])
                self.uid = getattr(self, "uid", 0) + 1
                key = (name, i, self.uid)
                P.last_w[key] = list(getattr(self, "summary", []))
                P.readers[key] = []
                self.keys.append(key)
                outs.append((a, key))
            return outs

    AR = Arena()

    def subkeys(base, n=5):
        for g_ in range(n):
            k_ = (base, g_)
            P.last_w[k_] = list(P.last_w.get(base, ()))
            P.readers[k_] = []
            AR.keys.append(k_)

    class Rot:
        def __init__(self, items):
            self.items = items
            self.i = 0

        def next(self):
            it = self.items[self.i % len(self.items)]
            self.i += 1
            return it

    def mm(out, lhsT, rhs, start, stop, reads, writes):
        P.op("pe", lambda h: h.matmul(out, lhsT=lhsT, rhs=rhs, start=start, stop=stop), reads, writes)

    def tr(out, in_, ident, reads, writes):
        P.op("pe", lambda h: h.transpose(out, in_, ident), reads, writes)

    def act(out, in_, func, reads, writes, scale=1.0, bias=0.0, accum_out=None):
        if accum_out is None:
            P.op("act", lambda h: h.activation(out=out, in_=in_, func=func, bias=bias, scale=scale), reads, writes)
        else:
            P.op("act", lambda h: h.activation(out=out, in_=in_, func=func, bias=bias, scale=scale,
                                               accum_out=accum_out), reads, writes)

    def tt(eng, out, in0, in1, op, reads, writes):
        P.op(eng, lambda h: h.tensor_tensor(out=out, in0=in0, in1=in1, op=op), reads, writes)

    def ts(eng, out, in0, s1, s2, op0, op1, reads, writes):
        if s2 is None:
            P.op(eng, lambda h: h.tensor_scalar(out=out, in0=in0, scalar1=s1, scalar2=None, op0=op0), reads, writes)
        else:
            P.op(eng, lambda h: h.tensor_scalar(out=out, in0=in0, scalar1=s1, scalar2=s2, op0=op0, op1=op1),
                 reads, writes)

    def stt(out, in0, scalar, in1, op0, op1, reads, writes):
        P.op("dve", lambda h: h.scalar_tensor_tensor(out=out, in0=in0, scalar=scalar, in1=in1, op0=op0, op1=op1),
             reads, writes)

    def cp(eng, out, in_, reads, writes):
        if eng == "act":
            P.op("act", lambda h: h.activation(out=out, in_=in_, func=AF.Copy), reads, writes)
        else:
            P.op(eng, lambda h: h.tensor_copy(out=out, in_=in_), reads, writes)

    def psk(b):
        return ("ps", b)

    class WRing:
        def __init__(self):
            self.n = 0
            self.rel_pos = [0] * NS

        def load(self, parts):
            s = self.n % NS
            self.n += 1
            key = ("w", s)
            for (off, shape, src) in parts:
                size = int(np.prod(shape))
                dst = wring[:, s, off:off + size]
                if len(shape) == 2:
                    dst = dst.rearrange("p (a b) -> p a b", b=shape[1])
                P.dma("pool", f"w{s}", lambda h, dst=dst, src=src: h.dma_start(out=dst, in_=src),
                      writes=[key], hoist_pos=self.rel_pos[s])
            return s, key

        def release(self, s):
            self.rel_pos[s] = P.pos("pool")

        def view(self, s, off, shape):
            size = int(np.prod(shape))
            a = wring[:, s, off:off + size]
            if len(shape) == 2:
                a = a.rearrange("p (a b) -> p a b", b=shape[1])
            return a

    WR = WRing()

    def wcols(w2d, c0, ncols):
        return w2d.rearrange("(k p) n -> p k n", p=128)[:, :, c0:c0 + ncols]

    def wrows(w2d, r0, nrows):
        return w2d[r0:r0 + nrows, :].rearrange("(a p) n -> p a n", p=128)

    def load_small(dst, name, bfcast=False):
        src = dram[name]
        d2 = dst[:]
        if len(d2.shape) > 2:
            names = "abcdef"[: len(d2.shape) - 1]
            pat = "p " + " ".join(names) + " -> p (" + " ".join(names) + ")"
            d2 = d2.rearrange(pat)
        if bfcast:
            P.dma("pool", "cstp", lambda h: h.dma_start(out=d2, in_=src[:, :]), writes=[name])
        else:
            P.dma("sp", "cst", lambda h: h.dma_start(out=d2, in_=src[:, :]), writes=[name])

    load_small(cst["identf"], "identf")
    for nm in ("ones", "blk64", "rotm", "maskf", "maskb"):
        load_small(cst[nm], nm, bfcast=True)
    P.dma("pool", "cstp", lambda h: h.dma_start(out=cst["identb"][:], in_=dram["identf"][:, :]), writes=["identb"])
    load_small(cst["rmf"], "rmf", bfcast=True)
    load_small(cst["rmb"], "rmb", bfcast=True)
    for t_, nm in ((cvec, "cvec"), (badaT, "badaT"), (g1T, "g1T"), (g2T, "g2T"), (qkg, "qkg"), (lamb, "lamb"),
                   (sublnT, "sublnT"), (bgu, "bgu"), (glagT, "glagT"), (lbraw, "lbraw"), (ognT, "ognT")):
        load_small(t_, nm)
    load_small(wgu, "wgu", bfcast=True)

    act(scT[:], cvec[:], AF.Silu, ["cvec"], ["scT"])
    for j in range(2):
        tt("dve", lamw[:, j, :, :], lamb[:, j, 0:4:2, :], lamb[:, j, 1:4:2, :], ALU.mult, ["lamb"], [("lamw", j)])
        for i in range(2):
            P.op("dve", lambda h, j=j, i=i: h.reduce_sum(out=lams[:, j, i:i + 1], in_=lamw[:, j, i, :],
                                                          axis=mybir.AxisListType.X),
                 [("lamw", j)], [("lams", j, i)])
        act(lams[:, j, 2:4], lams[:, j, 0:2], AF.Exp, [("lams", j, 0), ("lams", j, 1)], [("lams", j, 2)])
        stt(nlam[:, j:j + 1], lams[:, j, 3:4], -lambda_init(2 * j), lams[:, j, 2:3], ALU.add, ALU.subtract,
            [("lams", j, 2)], [("nlam", j)])
        ts("dve", sublnT[:, j:j + 1], sublnT[:, j:j + 1], 1.0 - lambda_init(2 * j), None, ALU.mult, None,
           ["sublnT"], ["sublnT"])
    ts("dve", bgu[:], bgu[:], -1.0, None, ALU.mult, None, ["bgu"], ["bgu"])
    act(lbe[:], lbraw[:], AF.Exp, ["lbraw"], ["lbe"])
    for dr in range(2):
        tt("dve", lbs[:, dr, :], lbe[:, dr, 0, :], lbe[:, dr, 1, :], ALU.add, ["lbe"], [("lbs", dr)])
        tt("dve", lbs[:, dr, :], lbs[:, dr, :], lbe[:, dr, 2, :], ALU.add, ["lbe", ("lbs", dr)], [("lbs", dr)])
        tt("dve", lbs[:, dr, :], lbs[:, dr, :], lbe[:, dr, 3, :], ALU.add, ["lbe", ("lbs", dr)], [("lbs", dr)])
        P.op("dve", lambda h, dr=dr: h.reciprocal(out=lbs[:, dr, :], in_=lbs[:, dr, :]), [("lbs", dr)], [("lbs", dr)])
        tt("dve", lbT[:, dr, 0, :], lbe[:, dr, 1, :], lbs[:, dr, :], ALU.mult, ["lbe", ("lbs", dr)], [("lbT", dr, 0)])
        tt("dve", lbT[:, dr, 1, :], lbe[:, dr, 1, :], lbe[:, dr, 2, :], ALU.add, ["lbe"], [("lbT", dr, 1)])
        tt("dve", lbT[:, dr, 1, :], lbT[:, dr, 1, :], lbe[:, dr, 3, :], ALU.add, ["lbe", ("lbT", dr, 1)], [("lbT", dr, 1)])
        tt("dve", lbT[:, dr, 1, :], lbT[:, dr, 1, :], lbs[:, dr, :], ALU.mult, [("lbT", dr, 1), ("lbs", dr)],
           [("lbT", dr, 1)])
        for j in range(2):
            ts("dve", omT[:, dr, j, :], lbT[:, dr, j, :], -1.0, 1.0, ALU.mult, ALU.add, [("lbT", dr, j)], [("omT", dr, j)])

    AR.reset()
    xt_bufs = Rot(AR.alloc("xtile", [1024], F32, 3))
    evac = Rot(["act", "dve"])
    for i in range(NT):
        xt, xk = xt_bufs.next()
        P.dma("sp", f"xt{i % 3}", lambda h, xt=xt, i=i: h.dma_start(out=xt, in_=xin[i * 128:(i + 1) * 128, :]),
              writes=[xk])
        for half in range(2):
            b = (2 * i + half) % 4
            for q in range(4):
                k = half * 4 + q
                tr(ps[b][:, q * 128:(q + 1) * 128], xt[:, k * 128:(k + 1) * 128], cst["identf"][:],
                   [xk, "identf"], [psk(b)])
            e = evac.next()
            cp(e, xT[:, half * 4:half * 4 + 4, i * 128:(i + 1) * 128],
               ps[b][:, :].rearrange("p (a b) -> p a b", b=128), [psk(b)], [("xT", None)])

    def tg_of_tile(i):
        return 0 if i < 2 else 1 + (i - 2) // 4

    def xk(g, d):
        return ("xT", g, d)

    def xkall(g):
        return [("xT", g, d) for d in range(8)]

    for g in range(5):
        P.alias(xkall(g), [("xT", None)])

    def ada_layer(l):
        b = 7
        for blk in range(24):
            s, key = WR.load([(0, [8, 256], wcols(w_ada[l], blk * 256, 256))])
            wv = WR.view(s, 0, [8, 256])
            for sub in range(2):
                f = blk * 2 + sub
                for k in range(8):
                    mm(ps[b][:, 2 * f:2 * f + 2], wv[:, k, sub * 128:(sub + 1) * 128], scT[:, k, :],
                       k == 0, k == 7, [key, "scT"], [psk(b)])
            WR.release(s)
        tt("dve", modT[:, l, :, :], ps[b][:, 0:96].rearrange("p (a b) -> p a b", b=2),
           badaT[:, l, :, 0:1].to_broadcast([128, 48, 2]), ALU.add, [psk(b), "badaT"], [("modT", l)])
        for (A, gT, m, nm) in ((A1, g1T, 1, "A1"), (A2, g2T, 4, "A2")):
            stt(A[:, l, :, :], modT[:, l, m * 8:(m + 1) * 8, :], 1.0,
                gT[:, l, :, 0:1].to_broadcast([128, 8, 2]), ALU.add, ALU.mult,
                [("modT", l), "g1T" if m == 1 else "g2T"], [(nm, l)])


    def modcol(l, m, k, g):
        c = 1 if g == 0 else 0
        return modT[:, l, m * 8 + k, c:c + 1]

    def norm_phase(l, which):
        AR.reset()
        sqb = Rot(AR.alloc("sq", [512], BF16, 3))
        lnb = Rot(AR.alloc("lnv", [512], F32, 2))
        rsb = Rot(AR.alloc("rstd", [512], F32, 2))
        tmb = Rot(AR.alloc("ntmp", [512], F32, 3))
        A = A1 if which == 1 else A2
        Ak = ("A1", l) if which == 1 else ("A2", l)
        mshift = 0 if which == 1 else 3
        sqe = Rot(["act", "act", "dve"])
        for g, (t0, n) in enumerate(TGS):
            b = g % 2
            c = 1 if g == 0 else 0
            for k in range(8):
                sq, sk = sqb.next()
                e = sqe.next()
                if e == "act":
                    act(sq[:, :n], xT[:, k, t0:t0 + n], AF.Square, [xk(g, k)], [sk])
                else:
                    tt("dve", sq[:, :n], xT[:, k, t0:t0 + n], xT[:, k, t0:t0 + n], ALU.mult, [xk(g, k)], [sk])
                mm(ps[b][:, :n], cst["ones"][:], sq[:, :n], k == 0, k == 7, [sk, "ones"], [psk(b)])
            lnv, lk = lnb.next()
            rstd, rk = rsb.next()
            act(lnv[:, :n], ps[b][:, :n], AF.Ln, [psk(b)], [lk], scale=1.0 / D, bias=EPS)
            act(rstd[:, :n], lnv[:, :n], AF.Exp, [lk], [rk], scale=-0.5)
            for k in range(8):
                tmp, tk = tmb.next()
                stt(tmp[:, :n], xT[:, k, t0:t0 + n], A[:, l, k, c:c + 1], rstd[:, :n], ALU.mult, ALU.mult,
                    [xk(g, k), Ak, rk], [tk])
                act(hT[:, k, t0:t0 + n], tmp[:, :n], AF.Identity, [tk, ("modT", l)], [("hT", g)],
                    bias=modcol(l, mshift, k, g))

    def ffn_phase(l):
        AR.reset()
        actb = Rot(AR.alloc("actT", [2, T], BF16, 2))
        sgb = Rot(AR.alloc("sg", [512], F32, 3))
        gb = Rot([0, 1])
        ub = Rot([2, 3])
        ob = Rot([4, 5, 6, 7])
        nsub = [0]

        def out_units(prev):
            (aT, ak, wo, ko, so_) = prev
            for g, (t0, n) in enumerate(TGS):
                for d in range(8):
                    def unit(g=g, t0=t0, n=n, d=d):
                        b = ob.next()
                        for j in range(2):
                            mm(ps[b][:, :n], wo[:, j, d * 128:(d + 1) * 128], aT[:, j, t0:t0 + n], j == 0, j == 1,
                               [ko, (ak, g)], [psk(b)])
                        stt(xT[:, d, t0:t0 + n], ps[b][:, :n], modcol(l, 5, d, g), xT[:, d, t0:t0 + n],
                            ALU.mult, ALU.add, [psk(b), ("modT", l), xk(g, d)], [xk(g, d)])
                    yield unit

        prev = None
        for grp in range(12):
            pend = list(out_units(prev)) if prev is not None else []
            if grp < 11:
                sg_, kg = WR.load([(0, [8, 256], wcols(w_ffn_in[l], grp * 256, 256))])
                su_, ku = WR.load([(0, [8, 256], wcols(w_ffn_in[l], DFF + grp * 256, 256))])
                so_, ko = WR.load([(0, [2, 1024], wrows(w_ffn_out[l], grp * 256, 256))])
                wg = WR.view(sg_, 0, [8, 256])
                wu = WR.view(su_, 0, [8, 256])
                wo = WR.view(so_, 0, [2, 1024])
                aT, ak = actb.next()
                if nsub[0] < 2:
                    subkeys(ak)
                    nsub[0] += 1
                for j in range(2):
                    for g, (t0, n) in enumerate(TGS):
                        bg_ = gb.next()
                        bu_ = ub.next()
                        for k in range(8):
                            mm(ps[bg_][:, :n], wg[:, k, j * 128:(j + 1) * 128], hT[:, k, t0:t0 + n], k == 0, k == 7,
                               [kg, ("hT", g)], [psk(bg_)])
                        for k in range(8):
                            mm(ps[bu_][:, :n], wu[:, k, j * 128:(j + 1) * 128], hT[:, k, t0:t0 + n], k == 0, k == 7,
                               [ku, ("hT", g)], [psk(bu_)])
                        sg, sk = sgb.next()
                        act(sg[:, :n], ps[bg_][:, :n], AF.Silu, [psk(bg_)], [sk])
                        tt("dve", aT[:, j, t0:t0 + n], sg[:, :n], ps[bu_][:, :n], ALU.mult, [sk, psk(bu_)], [(ak, g)])
                        for _ in range(4):
                            if pend:
                                pend.pop(0)()
                WR.release(sg_)
                WR.release(su_)
            while pend:
                pend.pop(0)()
            if prev is not None:
                WR.release(prev[4])
            prev = (aT, ak, wo, ko, so_) if grp < 11 else None

    def outproj_tg(l, wo_view, wkey, oT, okey, g, banks):
        t0, n = TGS[g]
        for d in range(8):
            b = banks.next()
            mm(ps[b][:, :n], wo_view[:, d * 128:(d + 1) * 128], oT[:, :n], True, True, [wkey, okey], [psk(b)])
            stt(xT[:, d, t0:t0 + n], ps[b][:, :n], modcol(l, 2, d, g), xT[:, d, t0:t0 + n], ALU.mult, ALU.add,
                [psk(b), ("modT", l), xk(g, d)], [xk(g, d)])

    def scan_head(l, kind, h, wviews, wkeys):
        j = l // 2
        dk = 128 if kind == "hgrn" else 64
        qconst = math.log(128 ** -0.5) if kind == "hgrn" else math.log(64 ** -0.5)
        AR.reset()
        vtok, vk = AR.alloc("vtok", [NT, 128], BF16, 1)[0]
        oacc, ok_ = AR.alloc("oacc", [T], F32, 1)[0]
        subkeys(ok_)
        DB = []
        for dr in range(2):
            d_ = {}
            d_["tmpF"] = {nm: Rot(AR.alloc(f"t{dr}_" + nm, [512], F32, 1)) for nm in ("a", "b", "c", "d", "e", "f")}
            d_["opB"] = {nm: Rot(AR.alloc(f"o{dr}_" + nm, [512], BF16, 1)) for nm in ("qh", "kh", "qb", "kd")}
            d_["kdt"] = Rot(AR.alloc(f"kdtok{dr}", [4, 128], BF16, 1))
            d_["attm"] = Rot(AR.alloc(f"attm{dr}", [128], BF16, 2))
            d_["S"] = AR.alloc(f"S{dr}", [128], F32, 1)[0]
            d_["Sb"] = Rot(AR.alloc(f"Sb{dr}", [128], BF16, 2))
            d_["c0"] = Rot(AR.alloc(f"c0{dr}", [8], F32, 1))
            d_["lr"] = Rot(AR.alloc(f"lrT{dr}", [512], BF16, 1))
            d_["pA"] = 0 if dr == 0 else 2
            d_["pB"] = 1 if dr == 0 else 3
            d_["ob"] = 5 if dr == 0 else 6
            d_["ab"] = 4 if dr == 0 else 7
            DB.append(d_)
        postF = {nm: Rot(AR.alloc("p_" + nm, [512], F32, 1)) for nm in ("a", "b", "c")}
        sqb = Rot(AR.alloc("psq", [512], BF16, 1))
        oTb = Rot(AR.alloc("oT", [512], BF16, 2))

        def pq(b, q):
            return ("ps", b, q)

        vb = Rot([0, 1, 2, 3])
        for i0 in range(0, NT, 4):
            nt_ = min(4, NT - i0)
            b = vb.next()
            for q in range(nt_):
                i = i0 + q
                for k in range(8):
                    mm(ps[b][:, q * 128:(q + 1) * 128], hT[:, k, i * 128:(i + 1) * 128], wviews["v"][:, k, :],
                       k == 0, k == 7, [("hT", tg_of_tile(i)), wkeys["v"]], [psk(b)])
            cp("act", vtok[:, i0:i0 + nt_, :], ps[b][:, :nt_ * 128].rearrange("p (a b) -> p a b", b=128),
               [psk(b)], [vk])

        def prep_a(dr, g):
            D_ = DB[dr]
            tmpF = D_["tmpF"]
            bA, bB = D_["pA"], D_["pB"]
            t0, n = TGS[g]
            nch = n // 64
            rd = [("hT", g)]
            for k in range(8):
                mm(ps[bA][:dk, :n], wviews["q"][:, k, :], hT[:, k, t0:t0 + n], k == 0, k == 7, rd + [wkeys["q"]], [psk(bA)])
            qf, qk_ = tmpF["a"].next()
            if kind == "hgrn":
                act(qf[:dk, :n], ps[bA][:dk, :n], AF.Sigmoid, [psk(bA)], [qk_])
                tt("dve", qf[:dk, :n], qf[:dk, :n], ps[bA][:dk, :n], ALU.mult, [qk_, psk(bA)], [qk_])
            else:
                cp("act", qf[:dk, :n], ps[bA][:dk, :n], [psk(bA)], [qk_])
            yield
            kf, kk_ = tmpF["b"].next()
            gl, gk_ = tmpF["c"].next()
            if kind == "hgrn":
                wn = "gf" if dr == 0 else "gb"
                for k in range(8):
                    mm(ps[bB][:dk, :n], wviews[wn][:, k, :], hT[:, k, t0:t0 + n], k == 0, k == 7, rd + [wkeys[wn]], [psk(bB)])
                act(kf[:dk, :n], ps[bB][:dk, :n], AF.Sigmoid, [psk(bB)], [kk_])
                ts("dve", kf[:dk, :n], kf[:dk, :n], omT[:, dr, j, h:h + 1], lbT[:, dr, j, h:h + 1], ALU.mult, ALU.add,
                   [kk_, ("omT", dr, j), ("lbT", dr, j)], [kk_])
                yield
                act(gl[:dk, :n], kf[:dk, :n], AF.Ln, [kk_], [gk_])
                act(kf[:dk, :n], kf[:dk, :n], AF.Identity, [kk_], [kk_], scale=-1.0, bias=1.0)
            else:
                for k in range(8):
                    mm(ps[bB][:dk, :n], wviews["k"][:, k, :], hT[:, k, t0:t0 + n], k == 0, k == 7, rd + [wkeys["k"]], [psk(bB)])
                cp("act", kf[:dk, :n], ps[bB][:dk, :n], [psk(bB)], [kk_])
                yield
                for k in range(8):
                    mm(ps[bA][:32, :n], wviews["lr"][:, k, :], hT[:, k, t0:t0 + n], k == 0, k == 7, rd + [wkeys["lr"]], [psk(bA)])
                lr, lrk = D_["lr"].next()
                cp("dve", lr[:32, :n], ps[bA][:32, :n], [psk(bA)], [lrk])
                mm(ps[bB][:dk, :n], wgu[:, j, dr, h * 64:(h + 1) * 64], lr[:32, :n], True, True, [lrk, "wgu"], [psk(bB)])
                act(gl[:dk, :n], ps[bB][:dk, :n], AF.Exp, [psk(bB), "bgu"], [gk_], scale=-1.0, bias=bgu[:, j, dr, h:h + 1])
                yield
                act(gl[:dk, :n], gl[:dk, :n], AF.Ln, [gk_], [gk_], bias=1.0)
                ts("dve", gl[:dk, :n], gl[:dk, :n], -1.0 / 16.0, None, ALU.mult, None, [gk_], [gk_])
            yield
            bb, bk_ = tmpF["d"].next()
            if dr == 0:
                P.op("dve", lambda h_: h_.tensor_tensor_scan(out=bb[:dk, :n], data0=cst["rmf"][:dk, :n], data1=gl[:dk, :n],
                                                             initial=0.0, op0=ALU.mult, op1=ALU.add),
                     [gk_, "rmf"], [bk_])
            else:
                P.op("dve", lambda h_: h_.tensor_tensor_scan(out=bb[:dk, :n][:, ::-1],
                                                             data0=cst["rmb"][:dk, :n][:, ::-1],
                                                             data1=gl[:dk, :n][:, ::-1],
                                                             initial=0.0, op0=ALU.mult, op1=ALU.add),
                     [gk_, "rmb"], [bk_])
            yield
            b3 = bb[:dk, :n].rearrange("p (c t) -> p c t", t=64)
            mid = 31 if dr == 0 else 32
            rr, rk_ = tmpF["c"].next()
            tt("dve", rr[:dk, :n].rearrange("p (c t) -> p c t", t=64), b3,
               b3[:, :, mid:mid + 1].to_broadcast([dk, nch, 64]), ALU.subtract, [bk_], [rk_])
            ts("dve", rr[:dk, :n], rr[:dk, :n], 40.0, -40.0, ALU.min, ALU.max, [rk_], [rk_])
            yield
            ep, epk = tmpF["e"].next()
            act(ep[:dk, :n], rr[:dk, :n], AF.Exp, [rk_], [epk], bias=qconst)
            en, enk = tmpF["f"].next()
            act(en[:dk, :n], rr[:dk, :n], AF.Exp, [rk_], [enk], scale=-1.0)
            yield
            return dict(qf=(qf, qk_), kf=(kf, kk_), bb=(bb, bk_), ep=(ep, epk), en=(en, enk))

        def prep_b(dr, g, pa):
            D_ = DB[dr]
            tmpF, opB = D_["tmpF"], D_["opB"]
            bA = D_["pA"]
            t0, n = TGS[g]
            nch = n // 64
            qf, qk_ = pa["qf"]
            kf, kk_ = pa["kf"]
            bb, bk_ = pa["bb"]
            ep, epk = pa["ep"]
            en, enk = pa["en"]
            b3 = bb[:dk, :n].rearrange("p (c t) -> p c t", t=64)
            last = 63 if dr == 0 else 0
            qh, qhk = opB["qh"].next()
            tt("dve", qh[:dk, :n], qf[:dk, :n], ep[:dk, :n], ALU.mult, [qk_, epk], [qhk])
            kh, khk = opB["kh"].next()
            tt("dve", kh[:dk, :n], kf[:dk, :n], en[:dk, :n], ALU.mult, [kk_, enk], [khk])
            eb, ebk = tmpF["e"].next()
            act(eb[:dk, :n], bb[:dk, :n], AF.Exp, [bk_], [ebk], bias=qconst)
            dd, ddk = tmpF["f"].next()
            tt("dve", dd[:dk, :n].rearrange("p (c t) -> p c t", t=64), b3,
               b3[:, :, last:last + 1].to_broadcast([dk, nch, 64]), ALU.subtract, [bk_], [ddk])
            act(dd[:dk, :n], dd[:dk, :n], AF.Exp, [ddk], [ddk], scale=-1.0)
            c0, c0k = D_["c0"].next()
            act(c0[:dk, :nch], b3[:, :, last], AF.Exp, [bk_], [c0k])
            qb, qbk = opB["qb"].next()
            tt("dve", qb[:dk, :n], qf[:dk, :n], eb[:dk, :n], ALU.mult, [qk_, ebk], [qbk])
            kd, kdk = opB["kd"].next()
            tt("dve", kd[:dk, :n], kf[:dk, :n], dd[:dk, :n], ALU.mult, [kk_, ddk], [kdk])
            kt, ktk = D_["kdt"].next()
            ntile = n // 128
            for q in range(ntile):
                tr(psb[bA][:, q * 128:q * 128 + dk], kd[:dk, q * 128:(q + 1) * 128], cst["identb"][:dk, :dk],
                   [kdk, "identb"], [psk(bA)])
            cp("dve", kt[:, :ntile, :dk], psb[bA][:, :ntile * 128].rearrange("p (a b) -> p a b", b=128)[:, :, :dk],
               [psk(bA)], [ktk])
            return dict(qh=(qh, qhk), kh=(kh, khk), qb=(qb, qbk), kt=(kt, ktk), c0=(c0, c0k))

        def scan_tile(dr, g, i, pr, state):
            D_ = DB[dr]
            Sf, Sk = D_["S"]
            t0, n = TGS[g]
            q = i - t0 // 128
            cols = slice(q * 128, (q + 1) * 128)
            qh, qhk = pr["qh"]
            kh, khk = pr["kh"]
            qb, qbk = pr["qb"]
            kt, ktk = pr["kt"]
            c0, c0k = pr["c0"]
            ab = D_["ab"]
            ac = slice(0, 128)
            uc = slice(256, 384)
            mm(ps[ab][:, ac], kh[:dk, cols], qh[:dk, cols], True, True, [khk, qhk], [psk(ab)])
            am, amk = D_["attm"].next()
            mk = "maskf" if dr == 0 else "maskb"
            tt("dve", am[:, :], ps[ab][:, ac], cst[mk][:], ALU.mult, [psk(ab), mk], [amk])
            bo = D_["ob"]
            mm(ps[bo][:, 0:128], vtok[:, i, :], am[:, :], True, False, [vk, amk], [psk(bo)])
            order = (0, 1) if dr == 0 else (1, 0)
            for ci, cch in enumerate(order):
                ccols = slice(q * 128 + cch * 64, q * 128 + (cch + 1) * 64)
                rows = slice(cch * 64, (cch + 1) * 64)
                sbf, sbk = state["sb"]
                mm(ps[bo][:, cch * 64:(cch + 1) * 64], sbf[:dk, :], qb[:dk, ccols], False, ci == 1,
                   [sbk, qbk], [psk(bo)])
                mm(ps[ab][:dk, uc], kt[rows, q, :dk], vtok[rows, i, :], True, True, [ktk, vk], [psk(ab)])
                chunk_idx = q * 2 + cch
                stt(Sf[:dk, :], Sf[:dk, :], c0[:dk, chunk_idx:chunk_idx + 1], ps[ab][:dk, uc], ALU.mult, ALU.add,
                    [Sk, c0k, psk(ab)], [Sk])
                nsb, nsk = D_["Sb"].next()
                cp("act", nsb[:dk, :], Sf[:dk, :], [Sk], [nsk])
                state["sb"] = (nsb, nsk)
                if ci == 0:
                    yield

        def tiles_of(g):
            t0, n = TGS[g]
            return list(range(t0 // 128, (t0 + n) // 128))

        gain_ap = (ognT if kind == "hgrn" else glagT)[:, j:j + 1]
        gain_key = "ognT" if kind == "hgrn" else "glagT"
        visited = set()
        done_g = {g: 0 for g in range(5)}

        def post(g, dr):
            D_ = DB[dr]
            bA, bB = D_["pA"], D_["pB"]
            t0, n = TGS[g]
            sq, sk = sqb.next()
            act(sq[:, :n], oacc[:, t0:t0 + n], AF.Square, [(ok_, g)], [sk])
            mm(ps[bA][:, :n], cst["ones"][:], sq[:, :n], True, True, [sk, "ones"], [psk(bA)])
            ra, rak = postF["a"].next()
            act(ra[:, :n], ps[bA][:, :n], AF.Ln, [psk(bA)], [rak], scale=1.0 / 128, bias=EPS)
            act(ra[:, :n], ra[:, :n], AF.Exp, [rak], [rak], scale=-0.5)
            for k in range(8):
                mm(ps[bB][:, :n], wviews["g"][:, k, :], hT[:, k, t0:t0 + n], k == 0, k == 7, [("hT", g), wkeys["g"]], [psk(bB)])
            sg, sgk = postF["b"].next()
            act(sg[:, :n], ps[bB][:, :n], AF.Silu, [psk(bB)], [sgk])
            t1, t1k = postF["c"].next()
            stt(t1[:, :n], oacc[:, t0:t0 + n], gain_ap, ra[:, :n], ALU.mult, ALU.mult, [(ok_, g), gain_key, rak], [t1k])
            oT, oTk = oTb.next()
            tt("dve", oT[:, :n], t1[:, :n], sg[:, :n], ALU.mult, [t1k, sgk], [oTk])
            outproj_tg(l, wviews["wo"], wkeys["wo"], oT, oTk, g, Rot([bA, bB]))

        def run_gen(gen):
            try:
                while True:
                    next(gen)
            except StopIteration as e_:
                return e_.value

        def step_gen(gen_box):
            if gen_box[2]:
                return
            try:
                next(gen_box[0])
            except StopIteration as e_:
                gen_box[1] = e_.value
                gen_box[2] = True

        def dirgen(dr, groups):
            D_ = DB[dr]
            Sf, Sk = D_["S"]
            P.op("pool", lambda h_: h_.memset(Sf[:, :], 0.0), [], [Sk])
            sb0, sbk0 = D_["Sb"].next()
            P.op("pool", lambda h_: h_.memset(sb0[:, :], 0.0), [], [sbk0])
            state = {"sb": (sb0, sbk0)}
            box = [prep_a(dr, groups[0]), None, False]
            while not box[2]:
                step_gen(box)
                yield
            for gi, g in enumerate(groups):
                tiles = tiles_of(g)
                if dr == 1:
                    tiles = tiles[::-1]
                pr = prep_b(dr, g, box[1])
                yield
                if gi + 1 < len(groups):
                    box = [prep_a(dr, groups[gi + 1]), None, False]
                for i in tiles:
                    for _ in scan_tile(dr, g, i, pr, state):
                        step_gen(box)
                        yield
                    bo = D_["ob"]
                    if i not in visited:
                        visited.add(i)
                        cp("act", oacc[:, i * 128:(i + 1) * 128], ps[bo][:, 0:128], [psk(bo)], [(ok_, g)])
                    else:
                        tt("dve", oacc[:, i * 128:(i + 1) * 128], ps[bo][:, 0:128], oacc[:, i * 128:(i + 1) * 128],
                           ALU.add, [psk(bo), (ok_, g)], [(ok_, g)])
                    done_g[g] += 1
                    step_gen(box)
                    if done_g[g] == 2 * len(tiles):
                        post(g, dr)
                    yield
                while not box[2]:
                    step_gen(box)
                    yield

        gens = [dirgen(0, [0, 1, 2, 3, 4]), dirgen(1, [0, 4, 3, 2, 1])]
        alive = [True, True]
        while any(alive):
            for dr in range(2):
                if alive[dr]:
                    try:
                        next(gens[dr])
                    except StopIteration:
                        alive[dr] = False

    def odd_mixer(l):
        j = l // 2
        W = w_in_odd[j]
        for h in range(8):
            s1, k1 = WR.load([(0, [8, 128], wcols(W, 0 * 1024 + h * 128, 128)),
                              (1024, [8, 128], wcols(W, 1 * 1024 + h * 128, 128))])
            s2, k2 = WR.load([(0, [8, 128], wcols(W, 2 * 1024 + h * 128, 128)),
                              (1024, [8, 128], wcols(W, 3 * 1024 + h * 128, 128))])
            s3, k3 = WR.load([(0, [8, 128], wcols(W, 4 * 1024 + h * 128, 128)),
                              (1024, [1024], w_out_odd[j][h * 128:(h + 1) * 128, :])])
            wv = dict(q=WR.view(s1, 0, [8, 128]), gf=WR.view(s1, 1024, [8, 128]),
                      gb=WR.view(s2, 0, [8, 128]), v=WR.view(s2, 1024, [8, 128]),
                      g=WR.view(s3, 0, [8, 128]), wo=WR.view(s3, 1024, [1024]))
            wk = dict(q=k1, gf=k1, gb=k2, v=k2, g=k3, wo=k3)
            scan_head(l, "hgrn", h, wv, wk)
            WR.release(s1)
            WR.release(s2)
            WR.release(s3)

    def gla_mixer(l):
        j = l // 2
        W = w_in_even[j]
        for h in range(4):
            s1, k1 = WR.load([(0, [8, 64], wcols(W, 1536 + h * 64, 64)),
                              (512, [8, 64], wcols(W, 1792 + h * 64, 64)),
                              (1024, [8, 32], wcols(W, 3072, 32))])
            s2, k2 = WR.load([(0, [8, 128], wcols(W, 2048 + h * 128, 128)),
                              (1024, [8, 128], wcols(W, 2560 + h * 128, 128))])
            s3, k3 = WR.load([(0, [1024], w_out_even[j][512 + h * 128:512 + (h + 1) * 128, :])])
            wv = dict(q=WR.view(s1, 0, [8, 64]), k=WR.view(s1, 512, [8, 64]), lr=WR.view(s1, 1024, [8, 32]),
                      v=WR.view(s2, 0, [8, 128]), g=WR.view(s2, 1024, [8, 128]), wo=WR.view(s3, 0, [1024]))
            wk = dict(q=k1, k=k1, lr=k1, v=k2, g=k2, wo=k3)
            scan_head(l, "gla", h, wv, wk)
            WR.release(s1)
            WR.release(s2)
            WR.release(s3)

    def att_mixer(l):
        j = l // 2
        W = w_in_even[j]
        AR.reset()
        kvb = []
        for i_ in range(2):
            kT_, kTk_ = AR.alloc(f"kT{i_}", [T], BF16, 1)[0]
            subkeys(kTk_)
            vt_, vk_ = AR.alloc(f"vtokA{i_}", [NT, 132], BF16, 1)[0]
            P.op("pool", lambda h_, vt_=vt_: h_.memset(vt_[:, :, 128:132], 1.0), [], [vk_])
            kvb.append((kT_, kTk_, vt_, vk_))
        qTb = Rot(AR.alloc("qT", [512], BF16, 2))
        PTb = Rot(AR.alloc("PT", [512], BF16, 4))
        sqb = Rot(AR.alloc("asq", [512], BF16, 2))
        rawb = Rot(AR.alloc("araw", [512], F32, 2))
        rsb = Rot(AR.alloc("arstd", [512], F32, 2))
        qgf = Rot(AR.alloc("aqg", [512], F32, 2))
        qgb = Rot(AR.alloc("aqgb", [512], BF16, 2))
        t1b = Rot(AR.alloc("at1", [512], F32, 1))
        t2b = Rot(AR.alloc("at2", [512], F32, 1))
        cosb = Rot(AR.alloc("cos", [512], F32, 2))
        sinb = Rot(AR.alloc("sin", [512], F32, 2))
        ptb = Rot(AR.alloc("pt", [128], F32, 2))
        pob = Rot(AR.alloc("po", [128], F32, 4))
        ponb = Rot(AR.alloc("pon", [128], BF16, 4))
        stb = Rot(AR.alloc("stat", [8], F32, 4))
        oTb = Rot(AR.alloc("oTa", [512], BF16, 2))
        sqj = Rot(AR.alloc("sqjunk", [128], F32, 1))
        rope_cnt = [0]
        MB = 7

        def qk_prep(wview, wkey, qk, g, out, outkey, outcols):
            t0, n = TGS[g]
            for k in range(8):
                mm(ps[MB][:, :n], wview[:, k, :], hT[:, k, t0:t0 + n], k == 0, k == 7, [("hT", g), wkey], [psk(MB)])
            raw, rwk = rawb.next()
            cp("act", raw[:, :n], ps[MB][:, :n], [psk(MB)], [rwk])
            sq, sk = sqb.next()
            act(sq[:, :n], ps[MB][:, :n], AF.Square, [psk(MB)], [sk])
            yield
            mm(ps[MB][:, :n], cst["blk64"][:], sq[:, :n], True, True, [sk, "blk64"], [psk(MB)])
            rs, rk = rsb.next()
            act(rs[:, :n], ps[MB][:, :n], AF.Ln, [psk(MB)], [rk], scale=1.0 / 64, bias=EPS)
            act(rs[:, :n], rs[:, :n], AF.Exp, [rk], [rk], scale=-0.5)
            if g == 0:
                stt(out[:, outcols], raw[:, :n], qkg[:, j, qk:qk + 1], rs[:, :n], ALU.mult, ALU.mult,
                    [rwk, "qkg", rk], [outkey])
                return
            qg, qgk = qgf.next()
            stt(qg[:, :n], raw[:, :n], qkg[:, j, qk:qk + 1], rs[:, :n], ALU.mult, ALU.mult, [rwk, "qkg", rk], [qgk])
            qb_, qbk = qgb.next()
            cp("act", qb_[:, :n], qg[:, :n], [qgk], [qbk])
            cs, ck = cosb.next()
            sn, snk = sinb.next()
            l0 = t0 - CTX
            r = rope_cnt[0]
            rope_cnt[0] += 1
            P.dma("sp", f"rp{r % 2}", lambda h_: h_.dma_start(out=cs[:, :n], in_=dram["cosT"][:, l0:l0 + n]), writes=[ck])
            P.dma("sp", f"rp{r % 2}", lambda h_: h_.dma_start(out=sn[:, :n], in_=dram["sinT"][:, l0:l0 + n]), writes=[snk])
            t1, t1k = t1b.next()
            tt("dve", t1[:, :n], qg[:, :n], cs[:, :n], ALU.mult, [qgk, ck], [t1k])
            yield
            mm(ps[MB][:, :n], cst["rotm"][:], qb_[:, :n], True, True, [qbk, "rotm"], [psk(MB)])
            t2, t2k = t2b.next()
            tt("dve", t2[:, :n], ps[MB][:, :n], sn[:, :n], ALU.mult, [psk(MB), snk], [t2k])
            tt("dve", out[:, outcols], t1[:, :n], t2[:, :n], ALU.add, [t1k, t2k], [outkey])

        HW = {}

        def load_head(h):
            sa, ka = WR.load([(0, [8, 128], wcols(W, h * 128, 128)), (1024, [8, 128], wcols(W, 512 + h * 128, 128))])
            sb_, kb = WR.load([(0, [8, 128], wcols(W, 1024 + h * 128, 128)),
                               (1024, [1024], w_out_even[j][h * 128:(h + 1) * 128, :])])
            HW[h] = dict(sa=sa, sb=sb_, ka=ka, kb=kb, wq=WR.view(sa, 0, [8, 128]), wk=WR.view(sa, 1024, [8, 128]),
                         wv=WR.view(sb_, 0, [8, 128]), wo=WR.view(sb_, 1024, [1024]))

        def k_unit(h, g):
            kT_, kTk_, vt_, vk_ = kvb[h % 2]
            t0, n = TGS[g]
            return lambda: qk_prep(HW[h]["wk"], HW[h]["ka"], 1, g, kT_, (kTk_, g), slice(t0, t0 + n))

        def v_unit(h, i0):
            kT_, kTk_, vt_, vk_ = kvb[h % 2]

            def f():
                nt_ = min(4, NT - i0)
                for q in range(nt_):
                    i = i0 + q
                    for k in range(8):
                        mm(ps[MB][:, q * 128:(q + 1) * 128], hT[:, k, i * 128:(i + 1) * 128], HW[h]["wv"][:, k, :],
                           k == 0, k == 7, [("hT", tg_of_tile(i)), HW[h]["kb"]], [psk(MB)])
                cp("dve", vt_[:, i0:i0 + nt_, 0:128], ps[MB][:, :nt_ * 128].rearrange("p (a b) -> p a b", b=128),
                   [psk(MB)], [vk_])
                return
                yield
            return f

        qready = {}

        def q_unit(h, g):
            def f():
                qT, qTk = qTb.next()
                qready[(h, g)] = (qT, qTk)
                yield from qk_prep(HW[h]["wq"], HW[h]["ka"], 0, g, qT, qTk, slice(0, TGS[g][1]))
            return f

        bg = []

        def bg_add(unit):
            bg.append(unit())

        def drain(nu=1):
            for _ in range(nu):
                while bg:
                    try:
                        next(bg[0])
                        break
                    except StopIteration:
                        bg.pop(0)

        def flush():
            while bg:
                try:
                    next(bg[0])
                except StopIteration:
                    bg.pop(0)

        def run_now(unit):
            for _ in unit():
                pass

        load_head(0)
        for g in range(5):
            run_now(k_unit(0, g))
        for i0 in range(0, NT, 4):
            run_now(v_unit(0, i0))
        run_now(q_unit(0, 0))
        for h in range(4):
            kT, kTk, vtok, vk = kvb[h % 2]
            if h + 1 < 4:
                load_head(h + 1)
            plan = {0: [], 1: [], 2: [], 3: [], 4: []}
            plan[0].append(q_unit(h, 1))
            plan[1].append(q_unit(h, 2))
            plan[2].append(q_unit(h, 3))
            plan[3].append(q_unit(h, 4))
            if h + 1 < 4:
                for g_ in range(3):
                    plan[1].append(k_unit(h + 1, g_))
                for g_ in range(3, 5):
                    plan[2].append(k_unit(h + 1, g_))
                vs_ = list(range(0, NT, 4))
                for i0 in vs_[:2]:
                    plan[2].append(v_unit(h + 1, i0))
                for i0 in vs_[2:]:
                    plan[3].append(v_unit(h + 1, i0))
                plan[4].append(q_unit(h + 1, 0))
            for g in range(5):
                t0, n = TGS[g]
                nq = n // 128
                for u_ in plan[g]:
                    bg_add(u_)
                qT, qTk = qready[(h, g)]
                ktiles = list(range(2)) if g == 0 else list(range(NT))
                sbanks = Rot([4, 5, 6])
                seq = [(c, kt) for c in range(2) for kt in ktiles]
                pend = []
                for idx, (c, kt) in enumerate(seq):
                    bs = sbanks.next()
                    mm(ps[bs][:, :n], kT[c * 64:(c + 1) * 64, kt * 128:(kt + 1) * 128], qT[c * 64:(c + 1) * 64, :n],
                       True, True, [(kTk, tg_of_tile(kt)), qTk], [psk(bs)])
                    pt, ptk = PTb.next()
                    act(pt[:, :n], ps[bs][:, :n], AF.Exp, [psk(bs)], [ptk], scale=0.125)
                    pend.append((c, kt, pt, ptk))
                    if len(pend) > 2:
                        emit_av(pend.pop(0), nq, ktiles, vtok, vk)
                    if g > 0 and idx % 2 == 1:
                        drain(1)
                while pend:
                    emit_av(pend.pop(0), nq, ktiles, vtok, vk)
                flush()
                items = []
                for qt in range(nq):
                    acc = ps[qt]
                    sts, stk = stb.next()
                    P.op("dve", lambda h_, acc=acc, sts=sts: h_.reciprocal(out=sts[:, 0:1], in_=acc[:, 128:129]),
                         [psk(qt)], [stk])
                    P.op("dve", lambda h_, acc=acc, sts=sts: h_.reciprocal(out=sts[:, 1:2], in_=acc[:, 384:385]),
                         [psk(qt), stk], [stk])
                    tt("dve", sts[:, 2:3], sts[:, 1:2], nlam[:, j:j + 1], ALU.mult, [stk, ("nlam", j)], [stk])
                    t_, tk_ = ptb.next()
                    ts("dve", t_[:, :], acc[:, 256:384], sts[:, 2:3], None, ALU.mult, None, [psk(qt), stk], [tk_])
                    o_, ok2 = pob.next()
                    stt(o_[:, :], acc[:, 0:128], sts[:, 0:1], t_[:, :], ALU.mult, ALU.add, [psk(qt), stk, tk_], [ok2])
                    items.append((o_, ok2, sts, stk))

                def post_rest(items=items, g=g, n=n, h=h):
                    t0_ = TGS[g][0]
                    oT, oTk = oTb.next()
                    ons = []
                    for (o_, ok2, sts, stk) in items:
                        jk, jkk = sqj.next()
                        act(jk[:, :], o_[:, :], AF.Square, [ok2], [jkk, stk], accum_out=sts[:, 3:4])
                        act(sts[:, 4:5], sts[:, 3:4], AF.Ln, [stk], [stk], scale=1.0 / 128, bias=EPS)
                        act(sts[:, 5:6], sts[:, 4:5], AF.Exp, [stk], [stk], scale=-0.5)
                        on_, onk = ponb.next()
                        ts("dve", on_[:, :], o_[:, :], sts[:, 5:6], None, ALU.mult, None, [ok2, stk], [onk])
                        ons.append((on_, onk))
                    yield
                    for qt, (on_, onk) in enumerate(ons):
                        tr(psb[MB][:, qt * 128:(qt + 1) * 128], on_[:, :], cst["identb"][:], [onk, "identb"], [psk(MB)])
                    P.op("act", lambda h_, oT=oT, n=n: h_.activation(out=oT[:, :n], in_=psb[MB][:, :n], func=AF.Copy,
                                                                  scale=sublnT[:, j:j + 1]),
                         [psk(MB), "sublnT"], [oTk])
                    yield
                    wo_ = HW[h]["wo"]
                    for d in range(8):
                        mm(ps[MB][:, :n], wo_[:, d * 128:(d + 1) * 128], oT[:, :n], True, True, [HW[h]["kb"], oTk], [psk(MB)])
                        stt(xT[:, d, t0_:t0_ + n], ps[MB][:, :n], modcol(l, 2, d, g), xT[:, d, t0_:t0_ + n], ALU.mult, ALU.add,
                            [psk(MB), ("modT", l), xk(g, d)], [xk(g, d)])
                        yield

                bg_add(post_rest)
            flush()
            WR.release(HW[h]["sa"])
            WR.release(HW[h]["sb"])

    def emit_av(pend, nq, ktiles, vtok, vk):
        c, kt, pt, ptk = pend
        for qt in range(nq):
            mm(ps[qt][:, c * 256:c * 256 + 129], pt[:, qt * 128:(qt + 1) * 128], vtok[:, kt, 0:129],
               kt == ktiles[0], kt == ktiles[-1], [ptk, vk], [psk(qt)])

    ada_layer(0)
    for l in range(nlayers):
        norm_phase(l, 1)
        if l % 2 == 0:
            if do_att:
                att_mixer(l)
            if do_gla:
                gla_mixer(l)
        else:
            if do_odd:
                odd_mixer(l)
        norm_phase(l, 2)
        if l + 1 < nlayers:
            ada_layer(l + 1)
        if do_ffn:
            ffn_phase(l)

    AR.reset()
    ob = Rot(AR.alloc("otile", [1024], F32, 3))
    evac = Rot(["act", "dve"])
    cnt = 0
    for i in range(2, NT):
        ot, okk = ob.next()
        g = tg_of_tile(i)
        for half in range(2):
            b = (cnt) % 4
            cnt += 1
            for q in range(4):
                k = half * 4 + q
                tr(ps[b][:, q * 128:(q + 1) * 128], xT[:, k, i * 128:(i + 1) * 128], cst["identf"][:],
                   [xk(g, k), "identf"], [psk(b)])
            cp(evac.next(), ot[:, half * 512:(half + 1) * 512], ps[b][:, :], [psk(b)], [okk])
        P.dma("sp", f"out{i % 3}", lambda h_, ot=ot, i=i: h_.dma_start(out=y[(i - 2) * 128:(i - 1) * 128, :], in_=ot),
              reads=[okk], writes=[("y", i)])
    P.finish("sp")
    if cfg.get("verbose"):
        print("ops per engine:", {e: len(P.ops[e]) + len(P.hoisted[e]) for e in ENGS}, "arena", AR.off)
    P.emit()
    st.close()
    return nc


def host_layout(inp, b):
    f = lambda a: np.ascontiguousarray(a, dtype=np.float32)
    m = {}
    m["xin"] = f(np.concatenate([inp["ctx"][b], inp["x"][b]], axis=0))
    for nm in ("w_ada", "w_in_even", "w_out_even", "w_in_odd", "w_out_odd", "w_ffn_in", "w_ffn_out"):
        m[nm] = inp[nm]
    fm = lambda v: np.asarray(v).reshape(-1, 128).T
    cv = np.stack([fm(inp["c"][b]), fm(inp["c_ctx"])], axis=-1)
    m["cvec"] = f(cv.reshape(128, 16))
    m["badaT"] = f(np.stack([fm(inp["b_ada"][l]) for l in range(4)], axis=1).reshape(128, 4 * 48))
    m["g1T"] = f(np.stack([fm(inp["norm1_gain"][l]) for l in range(4)], axis=1).reshape(128, 32))
    m["g2T"] = f(np.stack([fm(inp["norm2_gain"][l]) for l in range(4)], axis=1).reshape(128, 32))
    qk = np.asarray(inp["qk_gain_a"])
    m["qkg"] = f(np.tile(qk.transpose(2, 0, 1), (2, 1, 1)).reshape(128, 4))
    m["lamb"] = f(np.broadcast_to(np.asarray(inp["lambda_a"]).reshape(1, 512), (128, 512)))
    m["sublnT"] = f(np.asarray(inp["subln_gain_a"]).T)
    wg = np.asarray(inp["w_gate_up_b"])
    wgp = np.zeros((32, 2, 2, 256), np.float32)
    for j in range(2):
        for dr in range(2):
            wgp[dr * 16:(dr + 1) * 16, j, dr, :] = wg[j, dr]
    m["wgu"] = f(wgp.reshape(32, 1024))
    bg = np.asarray(inp["b_gate_up_b"]).reshape(2, 2, 4, 64)
    m["bgu"] = f(bg.transpose(3, 0, 1, 2).reshape(64, 16))
    m["glagT"] = f(np.asarray(inp["onorm_gain_b"]).T)
    lb = np.asarray(inp["lb_raw_c"]).reshape(2, 4, 8, 128)
    m["lbraw"] = f(lb.transpose(3, 0, 1, 2).reshape(128, 64))
    m["ognT"] = f(np.asarray(inp["onorm_gain_c"]).T)
    return m


_CACHE = {}


def run(inputs, cfg=None, trace=False, ncores=8):
    cfg = cfg or {}
    key = tuple(sorted(cfg.items()))
    if key not in _CACHE:
        _CACHE[key] = build_program(cfg)
    nc = _CACHE[key]
    inp = {k: np.asarray(v) for k, v in inputs.items()}
    consts = host_consts()
    in_maps = []
    for b in range(ncores):
        m = host_layout(inp, b)
        m.update(consts)
        in_maps.append(m)
    res = run_bass_kernel_spmd(nc, in_maps, core_ids=list(range(ncores)), trace=trace)
    out = np.stack([np.asarray(r["y"]) for r in res.results], axis=0).astype(np.float32)
    return out, res


def kernel(**inputs):
    out, _ = run(inputs)
    return out
```

```python
import contextlib
import math
import numpy as np
import concourse.bass as bass
import concourse.mybir as mybir
from concourse.bass_utils import run_bass_kernel_spmd

F32 = mybir.dt.float32
BF16 = mybir.dt.bfloat16
ALU = mybir.AluOpType
AF = mybir.ActivationFunctionType

ENGS = ("pe", "act", "dve", "pool", "sp")
SELF_SYNC = {"pe": False, "act": True, "dve": True, "pool": True, "sp": False}

D = 1024
T = 2304
NT = 18
CTX = 256
TGS = [(0, 256), (256, 512), (768, 512), (1280, 512), (1792, 512)]
DEPTH = 4
DFF = 2816
EPS = 1e-6
NS = 6
SLOT = 2048


class Prog:
    def __init__(self, nc):
        self.nc = nc
        self.ops = {e: [] for e in ENGS}
        self.hoisted = {e: [] for e in ENGS}
        self.count = {e: 0 for e in ENGS}
        self.waited = {e: {} for e in ENGS}
        self.last_w = {}
        self.readers = {}
        self.dma_count = {}
        self.dma_sems = []
        self.used = {e: set() for e in ENGS}

    def _deps(self, reads, writes):
        deps = []
        for k in reads:
            deps.extend(self.last_w.get(k, ()))
        for k in writes:
            deps.extend(self.last_w.get(k, ()))
            deps.extend(self.readers.get(k, ()))
        return deps

    def _waits(self, eng, deps, cache=True):
        need = {}
        for (s, v) in deps:
            if s in ENGS:
                if s == eng and not SELF_SYNC[eng]:
                    continue
            else:
                v = self.dma_count[s]
            if v > need.get(s, 0):
                need[s] = v
        waits = []
        for s, v in need.items():
            if (not cache) or self.waited[eng].get(s, 0) < v:
                if cache:
                    self.waited[eng][s] = v
                waits.append((s, v))
                if s in ENGS:
                    self.used[s].add(v)
        return waits

    def _record(self, tok, reads, writes):
        for k in writes:
            self.last_w[k] = [tok]
            self.readers[k] = []
        for k in reads:
            if k not in writes:
                self.readers.setdefault(k, []).append(tok)

    def alias(self, new_keys, old_keys):
        toks = []
        for k in old_keys:
            toks.extend(self.last_w.get(k, ()))
            toks.extend(self.readers.get(k, ()))
        toks = list(set(toks))
        for k in new_keys:
            self.last_w[k] = list(toks)
            self.readers[k] = []

    def op(self, eng, fn, reads=(), writes=()):
        waits = self._waits(eng, self._deps(reads, writes))
        self.count[eng] += 1
        tok = (eng, self.count[eng])
        self.ops[eng].append((waits, fn, tok))
        self._record(tok, reads, writes)
        return tok

    def dma(self, eng, sem, fn, reads=(), writes=(), hoist_pos=None):
        if sem not in self.dma_count:
            self.dma_count[sem] = 0
            self.dma_sems.append(sem)
        waits = self._waits(eng, self._deps(reads, writes), cache=(hoist_pos is None))
        self.dma_count[sem] += 16
        tok = (sem, self.dma_count[sem])
        if hoist_pos is None:
            self.ops[eng].append((waits, fn, tok))
        else:
            self.hoisted[eng].append((hoist_pos, len(self.hoisted[eng]), (waits, fn, tok)))
        self._record(tok, reads, writes)
        return tok

    def pos(self, eng):
        return len(self.ops[eng])

    def finish(self, eng="sp"):
        waits = []
        for s in self.dma_sems:
            waits.append((s, self.dma_count[s]))
        for e in ENGS:
            if e != eng and self.count[e] > 0:
                waits.append((e, self.count[e]))
                self.used[e].add(self.count[e])
        self.ops[eng].append((waits, None, None))

    def emit(self):
        nc = self.nc
        with contextlib.ExitStack() as st:
            sems = {}
            for e in ENGS:
                sems[e] = st.enter_context(nc.semaphore("s_" + e))
            for s in self.dma_sems:
                sems[s] = st.enter_context(nc.semaphore("d_" + s))
            rank = {e: {v: i + 1 for i, v in enumerate(sorted(self.used[e]))} for e in ENGS}
            block = st.enter_context(nc.Block())

            def run(e, h):
                hoist = sorted(self.hoisted[e], key=lambda x: (x[0], x[1]))
                hi = 0
                base = self.ops[e]
                seq = []
                for i, o in enumerate(base):
                    while hi < len(hoist) and hoist[hi][0] <= i:
                        seq.append(hoist[hi][2])
                        hi += 1
                    seq.append(o)
                while hi < len(hoist):
                    seq.append(hoist[hi][2])
                    hi += 1
                for waits, fn, tok in seq:
                    for (s, v) in waits:
                        h.wait_ge(sems[s], rank[s][v] if s in ENGS else v)
                    if fn is None:
                        continue
                    ins = fn(h)
                    if tok[0] in ENGS:
                        if tok[1] in rank[tok[0]]:
                            ins.then_inc(sems[tok[0]], 1)
                    else:
                        ins.then_inc(sems[tok[0]], 16)

            @block.tensor
            def _(h):
                run("pe", h)

            @block.scalar
            def _(h):
                run("act", h)

            @block.vector
            def _(h):
                run("dve", h)

            @block.gpsimd
            def _(h):
                run("pool", h)

            @block.sync
            def _(h):
                run("sp", h)


def lambda_init(l):
    return 0.8 - 0.6 * math.exp(-0.3 * l)


def host_consts():
    c = {}
    c["identf"] = np.eye(128, dtype=np.float32)
    c["ones"] = np.ones((128, 128), np.float32)
    p = np.arange(128)
    c["blk64"] = (p[:, None] // 64 == p[None, :] // 64).astype(np.float32)
    Rm = np.zeros((128, 128), np.float32)
    for q in range(128):
        w = (q % 64) % 32
        if w < 16:
            Rm[q + 16, q] = -1.0
        else:
            Rm[q - 16, q] = 1.0
    c["rotm"] = Rm
    s = p[:, None]
    t = p[None, :]
    same = (s // 64 == t // 64)
    c["maskf"] = (same & (s <= t)).astype(np.float32)
    c["maskb"] = (same & (s >= t)).astype(np.float32)
    tt = np.arange(512)
    c["rmf"] = np.broadcast_to((tt % 64 != 0).astype(np.float32), (128, 512)).copy()
    c["rmb"] = np.broadcast_to((tt % 64 != 63).astype(np.float32), (128, 512)).copy()
    tok = np.arange(2048)
    row = (tok // 64).astype(np.float32)
    col = (tok % 64).astype(np.float32)
    inv = (np.float32(10000.0) ** (-np.arange(0, 32, 2, dtype=np.float32) / np.float32(32))).astype(np.float32)
    cosT = np.zeros((128, 2048), np.float32)
    sinT = np.zeros((128, 2048), np.float32)
    for q in range(128):
        d = q % 64
        axis = d // 32
        j = d % 16
        ang = ((row if axis == 0 else col) * inv[j]).astype(np.float32)
        cosT[q] = np.cos(ang).astype(np.float32)
        sinT[q] = np.sin(ang).astype(np.float32)
    c["cosT"] = cosT
    c["sinT"] = sinT
    return c


CONST_BF = ("ones", "blk64", "rotm", "maskf", "maskb", "identb")


def build_program(cfg):
    nlayers = cfg.get("nlayers", DEPTH)
    do_att = cfg.get("att", True)
    do_gla = cfg.get("gla", True)
    do_odd = cfg.get("odd", True)
    do_ffn = cfg.get("ffn", True)

    nc = bass.Bass("TRN2", target_bir_lowering=False)
    dram = {}

    def din(name, shape):
        dram[name] = nc.dram_tensor(name, list(shape), F32, kind="ExternalInput").ap()
        return dram[name]

    xin = din("xin", [T, D])
    y = nc.dram_tensor("y", [2048, D], F32, kind="ExternalOutput").ap()
    w_ada = din("w_ada", [DEPTH, D, 6 * D])
    w_in_even = din("w_in_even", [2, D, 3104])
    w_out_even = din("w_out_even", [2, D, D])
    w_in_odd = din("w_in_odd", [2, D, 5120])
    w_out_odd = din("w_out_odd", [2, D, D])
    w_ffn_in = din("w_ffn_in", [DEPTH, D, 2 * DFF])
    w_ffn_out = din("w_ffn_out", [DEPTH, DFF, D])
    for nm in ("identf", "ones", "blk64", "rotm", "maskf", "maskb"):
        din(nm, [128, 128])
    din("rmf", [128, 512])
    din("rmb", [128, 512])
    din("cosT", [128, 2048])
    din("sinT", [128, 2048])
    din("cvec", [128, 16])
    din("badaT", [128, 4 * 48])
    din("g1T", [128, 32])
    din("g2T", [128, 32])
    din("qkg", [128, 4])
    din("lamb", [128, 2 * 256])
    din("sublnT", [128, 2])
    din("wgu", [32, 2 * 2 * 256])
    din("bgu", [64, 2 * 2 * 4])
    din("glagT", [128, 2])
    din("lbraw", [128, 2 * 4 * 8])
    din("ognT", [128, 2])

    P = Prog(nc)
    st = contextlib.ExitStack()

    def sb(name, shape, dt):
        return st.enter_context(nc.sbuf_tensor("s_" + name, list(shape), dt))

    xT = sb("xT", [128, 8, T], F32)
    hT = sb("hT", [128, 8, T], BF16)
    wring = sb("wring", [128, NS, SLOT], BF16)
    ps = [st.enter_context(nc.psum_tensor(f"ps{i}", [128, 512], F32)) for i in range(8)]
    psb = [p_[:].bitcast(BF16) for p_ in ps]

    cst = {}
    cst["identf"] = sb("c_identf", [128, 128], F32)
    for nm in ("ones", "blk64", "rotm", "maskf", "maskb", "identb"):
        cst[nm] = sb("c_" + nm, [128, 128], BF16)
    cst["rmf"] = sb("c_rmf", [128, 512], BF16)
    cst["rmb"] = sb("c_rmb", [128, 512], BF16)
    cvec = sb("cvec", [128, 8, 2], F32)
    scT = sb("scT", [128, 8, 2], BF16)
    badaT = sb("badaT", [128, 4, 48, 1], F32)
    g1T = sb("g1T", [128, 4, 8, 1], F32)
    g2T = sb("g2T", [128, 4, 8, 1], F32)
    modT = sb("modT", [128, 4, 48, 2], F32)
    A1 = sb("A1", [128, 4, 8, 2], F32)
    A2 = sb("A2", [128, 4, 8, 2], F32)
    qkg = sb("qkg", [128, 2, 2], F32)
    lamb = sb("lamb", [128, 2, 4, 64], F32)
    lamw = sb("lamw", [128, 2, 2, 64], F32)
    lams = sb("lams", [128, 2, 4], F32)
    nlam = sb("nlam", [128, 2], F32)
    sublnT = sb("sublnT", [128, 2], F32)
    wgu = sb("wgu", [32, 2, 2, 256], BF16)
    bgu = sb("bgu", [64, 2, 2, 4], F32)
    glagT = sb("glagT", [128, 2], F32)
    lbraw = sb("lbraw", [128, 2, 4, 8], F32)
    lbe = sb("lbe", [128, 2, 4, 8], F32)
    lbs = sb("lbs", [128, 2, 8], F32)
    lbT = sb("lbT", [128, 2, 2, 8], F32)
    omT = sb("omT", [128, 2, 2, 8], F32)
    ognT = sb("ognT", [128, 2], F32)

    ARENA = 32000
    arena = sb("arena", [128, ARENA], BF16)
    arena_f = arena[:].bitcast(F32)

    class Arena:
        def __init__(self):
            self.off = 0
            self.keys = []
            self.old_keys = []

        def reset(self):
            best = {}
            toks = list(getattr(self, "summary", []))
            for k in self.keys:
                toks.extend(P.last_w.get(k, ()))
                toks.extend(P.readers.get(k, ()))
            for (s_, v_) in toks:
                if v_ > best.get(s_, 0):
                    best[s_] = v_
            self.summary = list(best.items())
            self.keys = []
            self.off = 0

        def alloc(self, name, free_shape, dt, n=1):
            size = int(np.prod(free_shape))
            outs = []
            for i in range(n):
                if dt == F32:
                    if self.off % 2:
                        self.off += 1
                    a = arena_f[:, self.off // 2: self.off // 2 + size]
                    self.off += 2 * size
                else:
                    a = arena[:, self.off: self.off + size]
                    self.off += size
                assert self.off <= ARENA, (name, self.off)
                if len(free_shape) == 2:
                    a = a.rearrange("p (a b) -> p a b", b=free_shape[1])
                self.uid = getattr(self, "uid", 0) + 1
                key = (name, i, self.uid)
                P.last_w[key] = list(getattr(self, "summary", []))
                P.readers[key] = []
                self.keys.append(key)
                outs.append((a, key))
            return outs

    AR = Arena()

    def subkeys(base, n=5):
        for g_ in range(n):
            k_ = (base, g_)
            P.last_w[k_] = list(P.last_w.get(base, ()))
            P.readers[k_] = []
            AR.keys.append(k_)

    class Rot:
        def __init__(self, items):
            self.items = items
            self.i = 0

        def next(self):
            it = self.items[self.i % len(self.items)]
            self.i += 1
            return it

    def mm(out, lhsT, rhs, start, stop, reads, writes):
        P.op("pe", lambda h: h.matmul(out, lhsT=lhsT, rhs=rhs, start=start, stop=stop), reads, writes)

    def tr(out, in_, ident, reads, writes):
        P.op("pe", lambda h: h.transpose(out, in_, ident), reads, writes)

    def act(out, in_, func, reads, writes, scale=1.0, bias=0.0, accum_out=None):
        if accum_out is None:
            P.op("act", lambda h: h.activation(out=out, in_=in_, func=func, bias=bias, scale=scale), reads, writes)
        else:
            P.op("act", lambda h: h.activation(out=out, in_=in_, func=func, bias=bias, scale=scale,
                                               accum_out=accum_out), reads, writes)

    def tt(eng, out, in0, in1, op, reads, writes):
        P.op(eng, lambda h: h.tensor_tensor(out=out, in0=in0, in1=in1, op=op), reads, writes)

    def ts(eng, out, in0, s1, s2, op0, op1, reads, writes):
        if s2 is None:
            P.op(eng, lambda h: h.tensor_scalar(out=out, in0=in0, scalar1=s1, scalar2=None, op0=op0), reads, writes)
        else:
            P.op(eng, lambda h: h.tensor_scalar(out=out, in0=in0, scalar1=s1, scalar2=s2, op0=op0, op1=op1),
                 reads, writes)

    def stt(out, in0, scalar, in1, op0, op1, reads, writes):
        P.op("dve", lambda h: h.scalar_tensor_tensor(out=out, in0=in0, scalar=scalar, in1=in1, op0=op0, op1=op1),
             reads, writes)

    def cp(eng, out, in_, reads, writes):
        if eng == "act":
            P.op("act", lambda h: h.activation(out=out, in_=in_, func=AF.Copy), reads, writes)
        else:
            P.op(eng, lambda h: h.tensor_copy(out=out, in_=in_), reads, writes)

    def psk(b):
        return ("ps", b)

    class WRing:
        def __init__(self):
            self.n = 0
            self.rel_pos = [0] * NS

        def load(self, parts):
            s = self.n % NS
            self.n += 1
            key = ("w", s)
            for (off, shape, src) in parts:
                size = int(np.prod(shape))
                dst = wring[:, s, off:off + size]
                if len(shape) == 2:
                    dst = dst.rearrange("p (a b) -> p a b", b=shape[1])
                P.dma("pool", f"w{s}", lambda h, dst=dst, src=src: h.dma_start(out=dst, in_=src),
                      writes=[key], hoist_pos=self.rel_pos[s])
            return s, key

        def release(self, s):
            self.rel_pos[s] = P.pos("pool")

        def view(self, s, off, shape):
            size = int(np.prod(shape))
            a = wring[:, s, off:off + size]
            if len(shape) == 2:
                a = a.rearrange("p (a b) -> p a b", b=shape[1])
            return a

    WR = WRing()

    def wcols(w2d, c0, ncols):
        return w2d.rearrange("(k p) n -> p k n", p=128)[:, :, c0:c0 + ncols]

    def wrows(w2d, r0, nrows):
        return w2d[r0:r0 + nrows, :].rearrange("(a p) n -> p a n", p=128)

    def load_small(dst, name, bfcast=False):
        src = dram[name]
        d2 = dst[:]
        if len(d2.shape) > 2:
            names = "abcdef"[: len(d2.shape) - 1]
            pat = "p " + " ".join(names) + " -> p (" + " ".join(names) + ")"
            d2 = d2.rearrange(pat)
        if bfcast:
            P.dma("pool", "cstp", lambda h: h.dma_start(out=d2, in_=src[:, :]), writes=[name])
        else:
            P.dma("sp", "cst", lambda h: h.dma_start(out=d2, in_=src[:, :]), writes=[name])

    load_small(cst["identf"], "identf")
    for nm in ("ones", "blk64", "rotm", "maskf", "maskb"):
        load_small(cst[nm], nm, bfcast=True)
    P.dma("pool", "cstp", lambda h: h.dma_start(out=cst["identb"][:], in_=dram["identf"][:, :]), writes=["identb"])
    load_small(cst["rmf"], "rmf", bfcast=True)
    load_small(cst["rmb"], "rmb", bfcast=True)
    for t_, nm in ((cvec, "cvec"), (badaT, "badaT"), (g1T, "g1T"), (g2T, "g2T"), (qkg, "qkg"), (lamb, "lamb"),
                   (sublnT, "sublnT"), (bgu, "bgu"), (glagT, "glagT"), (lbraw, "lbraw"), (ognT, "ognT")):
        load_small(t_, nm)
    load_small(wgu, "wgu", bfcast=True)

    act(scT[:], cvec[:], AF.Silu, ["cvec"], ["scT"])
    for j in range(2):
        tt("dve", lamw[:, j, :, :], lamb[:, j, 0:4:2, :], lamb[:, j, 1:4:2, :], ALU.mult, ["lamb"], [("lamw", j)])
        for i in range(2):
            P.op("dve", lambda h, j=j, i=i: h.reduce_sum(out=lams[:, j, i:i + 1], in_=lamw[:, j, i, :],
                                                          axis=mybir.AxisListType.X),
                 [("lamw", j)], [("lams", j, i)])
        act(lams[:, j, 2:4], lams[:, j, 0:2], AF.Exp, [("lams", j, 0), ("lams", j, 1)], [("lams", j, 2)])
        stt(nlam[:, j:j + 1], lams[:, j, 3:4], -lambda_init(2 * j), lams[:, j, 2:3], ALU.add, ALU.subtract,
            [("lams", j, 2)], [("nlam", j)])
        ts("dve", sublnT[:, j:j + 1], sublnT[:, j:j + 1], 1.0 - lambda_init(2 * j), None, ALU.mult, None,
           ["sublnT"], ["sublnT"])
    ts("dve", bgu[:], bgu[:], -1.0, None, ALU.mult, None, ["bgu"], ["bgu"])
    act(lbe[:], lbraw[:], AF.Exp, ["lbraw"], ["lbe"])
    for dr in range(2):
        tt("dve", lbs[:, dr, :], lbe[:, dr, 0, :], lbe[:, dr, 1, :], ALU.add, ["lbe"], [("lbs", dr)])
        tt("dve", lbs[:, dr, :], lbs[:, dr, :], lbe[:, dr, 2, :], ALU.add, ["lbe", ("lbs", dr)], [("lbs", dr)])
        tt("dve", lbs[:, dr, :], lbs[:, dr, :], lbe[:, dr, 3, :], ALU.add, ["lbe", ("lbs", dr)], [("lbs", dr)])
        P.op("dve", lambda h, dr=dr: h.reciprocal(out=lbs[:, dr, :], in_=lbs[:, dr, :]), [("lbs", dr)], [("lbs", dr)])
        tt("dve", lbT[:, dr, 0, :], lbe[:, dr, 1, :], lbs[:, dr, :], ALU.mult, ["lbe", ("lbs", dr)], [("lbT", dr, 0)])
        tt("dve", lbT[:, dr, 1, :], lbe[:, dr, 1, :], lbe[:, dr, 2, :], ALU.add, ["lbe"], [("lbT", dr, 1)])
        tt("dve", lbT[:, dr, 1, :], lbT[:, dr, 1, :], lbe[:, dr, 3, :], ALU.add, ["lbe", ("lbT", dr, 1)], [("lbT", dr, 1)])
        tt("dve", lbT[:, dr, 1, :], lbT[:, dr, 1, :], lbs[:, dr, :], ALU.mult, [("lbT", dr, 1), ("lbs", dr)],
           [("lbT", dr, 1)])
        for j in range(2):
            ts("dve", omT[:, dr, j, :], lbT[:, dr, j, :], -1.0, 1.0, ALU.mult, ALU.add, [("lbT", dr, j)], [("omT", dr, j)])

    AR.reset()
    xt_bufs = Rot(AR.alloc("xtile", [1024], F32, 3))
    evac = Rot(["act", "dve"])
    for i in range(NT):
        xt, xk = xt_bufs.next()
        P.dma("sp", f"xt{i % 3}", lambda h, xt=xt, i=i: h.dma_start(out=xt, in_=xin[i * 128:(i + 1) * 128, :]),
              writes=[xk])
        for half in range(2):
            b = (2 * i + half) % 4
            for q in range(4):
                k = half * 4 + q
                tr(ps[b][:, q * 128:(q + 1) * 128], xt[:, k * 128:(k + 1) * 128], cst["identf"][:],
                   [xk, "identf"], [psk(b)])
            e = evac.next()
            cp(e, xT[:, half * 4:half * 4 + 4, i * 128:(i + 1) * 128],
               ps[b][:, :].rearrange("p (a b) -> p a b", b=128), [psk(b)], [("xT", None)])

    def tg_of_tile(i):
        return 0 if i < 2 else 1 + (i - 2) // 4

    def xk(g, d):
        return ("xT", g, d)

    def xkall(g):
        return [("xT", g, d) for d in range(8)]

    for g in range(5):
        P.alias(xkall(g), [("xT", None)])

    def ada_layer(l):
        b = 7
        for blk in range(24):
            s, key = WR.load([(0, [8, 256], wcols(w_ada[l], blk * 256, 256))])
            wv = WR.view(s, 0, [8, 256])
            for sub in range(2):
                f = blk * 2 + sub
                for k in range(8):
                    mm(ps[b][:, 2 * f:2 * f + 2], wv[:, k, sub * 128:(sub + 1) * 128], scT[:, k, :],
                       k == 0, k == 7, [key, "scT"], [psk(b)])
            WR.release(s)
        tt("dve", modT[:, l, :, :], ps[b][:, 0:96].rearrange("p (a b) -> p a b", b=2),
           badaT[:, l, :, 0:1].to_broadcast([128, 48, 2]), ALU.add, [psk(b), "badaT"], [("modT", l)])
        for (A, gT, m, nm) in ((A1, g1T, 1, "A1"), (A2, g2T, 4, "A2")):
            stt(A[:, l, :, :], modT[:, l, m * 8:(m + 1) * 8, :], 1.0,
                gT[:, l, :, 0:1].to_broadcast([128, 8, 2]), ALU.add, ALU.mult,
                [("modT", l), "g1T" if m == 1 else "g2T"], [(nm, l)])


    def modcol(l, m, k, g):
        c = 1 if g == 0 else 0
        return modT[:, l, m * 8 + k, c:c + 1]

    def norm_phase(l, which):
        AR.reset()
        sqb = Rot(AR.alloc("sq", [512], BF16, 3))
        lnb = Rot(AR.alloc("lnv", [512], F32, 2))
        rsb = Rot(AR.alloc("rstd", [512], F32, 2))
        tmb = Rot(AR.alloc("ntmp", [512], F32, 3))
        A = A1 if which == 1 else A2
        Ak = ("A1", l) if which == 1 else ("A2", l)
        mshift = 0 if which == 1 else 3
        sqe = Rot(["act", "act", "dve"])
        for g, (t0, n) in enumerate(TGS):
            b = g % 2
            c = 1 if g == 0 else 0
            for k in range(8):
                sq, sk = sqb.next()
                e = sqe.next()
                if e == "act":
                    act(sq[:, :n], xT[:, k, t0:t0 + n], AF.Square, [xk(g, k)], [sk])
                else:
                    tt("dve", sq[:, :n], xT[:, k, t0:t0 + n], xT[:, k, t0:t0 + n], ALU.mult, [xk(g, k)], [sk])
                mm(ps[b][:, :n], cst["ones"][:], sq[:, :n], k == 0, k == 7, [sk, "ones"], [psk(b)])
            lnv, lk = lnb.next()
            rstd, rk = rsb.next()
            act(lnv[:, :n], ps[b][:, :n], AF.Ln, [psk(b)], [lk], scale=1.0 / D, bias=EPS)
            act(rstd[:, :n], lnv[:, :n], AF.Exp, [lk], [rk], scale=-0.5)
            for k in range(8):
                tmp, tk = tmb.next()
                stt(tmp[:, :n], xT[:, k, t0:t0 + n], A[:, l, k, c:c + 1], rstd[:, :n], ALU.mult, ALU.mult,
                    [xk(g, k), Ak, rk], [tk])
                act(hT[:, k, t0:t0 + n], tmp[:, :n], AF.Identity, [tk, ("modT", l)], [("hT", g)],
                    bias=modcol(l, mshift, k, g))

    def ffn_phase(l):
        AR.reset()
        actb = Rot(AR.alloc("actT", [2, T], BF16, 2))
        sgb = Rot(AR.alloc("sg", [512], F32, 3))
        gb = Rot([0, 1])
        ub = Rot([2, 3])
        ob = Rot([4, 5, 6, 7])
        nsub = [0]

        def out_units(prev):
            (aT, ak, wo, ko, so_) = prev
            for g, (t0, n) in enumerate(TGS):
                for d in range(8):
                    def unit(g=g, t0=t0, n=n, d=d):
                        b = ob.next()
                        for j in range(2):
                            mm(ps[b][:, :n], wo[:, j, d * 128:(d + 1) * 128], aT[:, j, t0:t0 + n], j == 0, j == 1,
                               [ko, (ak, g)], [psk(b)])
                        stt(xT[:, d, t0:t0 + n], ps[b][:, :n], modcol(l, 5, d, g), xT[:, d, t0:t0 + n],
                            ALU.mult, ALU.add, [psk(b), ("modT", l), xk(g, d)], [xk(g, d)])
                    yield unit

        prev = None
        for grp in range(12):
            pend = list(out_units(prev)) if prev is not None else []
            if grp < 11:
                sg_, kg = WR.load([(0, [8, 256], wcols(w_ffn_in[l], grp * 256, 256))])
                su_, ku = WR.load([(0, [8, 256], wcols(w_ffn_in[l], DFF + grp * 256, 256))])
                so_, ko = WR.load([(0, [2, 1024], wrows(w_ffn_out[l], grp * 256, 256))])
                wg = WR.view(sg_, 0, [8, 256])
                wu = WR.view(su_, 0, [8, 256])
                wo = WR.view(so_, 0, [2, 1024])
                aT, ak = actb.next()
                if nsub[0] < 2:
                    subkeys(ak)
                    nsub[0] += 1
                for j in range(2):
                    for g, (t0, n) in enumerate(TGS):
                        bg_ = gb.next()
                        bu_ = ub.next()
                        for k in range(8):
                            mm(ps[bg_][:, :n], wg[:, k, j * 128:(j + 1) * 128], hT[:, k, t0:t0 + n], k == 0, k == 7,
                               [kg, ("hT", g)], [psk(bg_)])
                        for k in range(8):
                            mm(ps[bu_][:, :n], wu[:, k, j * 128:(j + 1) * 128], hT[:, k, t0:t0 + n], k == 0, k == 7,
                               [ku, ("hT", g)], [psk(bu_)])
                        sg, sk = sgb.next()
                        act(sg[:, :n], ps[bg_][:, :n], AF.Silu, [psk(bg_)], [sk])
                        tt("dve", aT[:, j, t0:t0 + n], sg[:, :n], ps[bu_][:, :n], ALU.mult, [sk, psk(bu_)], [(ak, g)])
                        for _ in range(4):
                            if pend:
                                pend.pop(0)()
                WR.release(sg_)
                WR.release(su_)
            while pend:
                pend.pop(0)()
            if prev is not None:
                WR.release(prev[4])
            prev = (aT, ak, wo, ko, so_) if grp < 11 else None

    def outproj_tg(l, wo_view, wkey, oT, okey, g, banks):
        t0, n = TGS[g]
        for d in range(8):
            b = banks.next()
            mm(ps[b][:, :n], wo_view[:, d * 128:(d + 1) * 128], oT[:, :n], True, True, [wkey, okey], [psk(b)])
            stt(xT[:, d, t0:t0 + n], ps[b][:, :n], modcol(l, 2, d, g), xT[:, d, t0:t0 + n], ALU.mult, ALU.add,
                [psk(b), ("modT", l), xk(g, d)], [xk(g, d)])

    def scan_head(l, kind, h, wviews, wkeys):
        j = l // 2
        dk = 128 if kind == "hgrn" else 64
        qconst = math.log(128 ** -0.5) if kind == "hgrn" else math.log(64 ** -0.5)
        AR.reset()
        vtok, vk = AR.alloc("vtok", [NT, 128], BF16, 1)[0]
        oacc, ok_ = AR.alloc("oacc", [T], F32, 1)[0]
        subkeys(ok_)
        DB = []
        for dr in range(2):
            d_ = {}
            d_["tmpF"] = {nm: Rot(AR.alloc(f"t{dr}_" + nm, [512], F32, 1)) for nm in ("a", "b", "c", "d", "e", "f")}
            d_["opB"] = {nm: Rot(AR.alloc(f"o{dr}_" + nm, [512], BF16, 1)) for nm in ("qh", "kh", "qb", "kd")}
            d_["kdt"] = Rot(AR.alloc(f"kdtok{dr}", [4, 128], BF16, 1))
            d_["attm"] = Rot(AR.alloc(f"attm{dr}", [128], BF16, 2))
            d_["S"] = AR.alloc(f"S{dr}", [128], F32, 1)[0]
            d_["Sb"] = Rot(AR.alloc(f"Sb{dr}", [128], BF16, 2))
            d_["c0"] = Rot(AR.alloc(f"c0{dr}", [8], F32, 1))
            d_["lr"] = Rot(AR.alloc(f"lrT{dr}", [512], BF16, 1))
            d_["pA"] = 0 if dr == 0 else 2
            d_["pB"] = 1 if dr == 0 else 3
            d_["ob"] = 5 if dr == 0 else 6
            d_["ab"] = 4 if dr == 0 else 7
            DB.append(d_)
        postF = {nm: Rot(AR.alloc("p_" + nm, [512], F32, 1)) for nm in ("a", "b", "c")}
        sqb = Rot(AR.alloc("psq", [512], BF16, 1))
        oTb = Rot(AR.alloc("oT", [512], BF16, 2))

        def pq(b, q):
            return ("ps", b, q)

        vb = Rot([0, 1, 2, 3])
        for i0 in range(0, NT, 4):
            nt_ = min(4, NT - i0)
            b = vb.next()
            for q in range(nt_):
                i = i0 + q
                for k in range(8):
                    mm(ps[b][:, q * 128:(q + 1) * 128], hT[:, k, i * 128:(i + 1) * 128], wviews["v"][:, k, :],
                       k == 0, k == 7, [("hT", tg_of_tile(i)), wkeys["v"]], [psk(b)])
            cp("act", vtok[:, i0:i0 + nt_, :], ps[b][:, :nt_ * 128].rearrange("p (a b) -> p a b", b=128),
               [psk(b)], [vk])

        def prep_a(dr, g):
            D_ = DB[dr]
            tmpF = D_["tmpF"]
            bA, bB = D_["pA"], D_["pB"]
            t0, n = TGS[g]
            nch = n // 64
            rd = [("hT", g)]
            for k in range(8):
                mm(ps[bA][:dk, :n], wviews["q"][:, k, :], hT[:, k, t0:t0 + n], k == 0, k == 7, rd + [wkeys["q"]], [psk(bA)])
            qf, qk_ = tmpF["a"].next()
            if kind == "hgrn":
                act(qf[:dk, :n], ps[bA][:dk, :n], AF.Sigmoid, [psk(bA)], [qk_])
                tt("dve", qf[:dk, :n], qf[:dk, :n], ps[bA][:dk, :n], ALU.mult, [qk_, psk(bA)], [qk_])
            else:
                cp("act", qf[:dk, :n], ps[bA][:dk, :n], [psk(bA)], [qk_])
            yield
            kf, kk_ = tmpF["b"].next()
            gl, gk_ = tmpF["c"].next()
            if kind == "hgrn":
                wn = "gf" if dr == 0 else "gb"
                for k in range(8):
                    mm(ps[bB][:dk, :n], wviews[wn][:, k, :], hT[:, k, t0:t0 + n], k == 0, k == 7, rd + [wkeys[wn]], [psk(bB)])
                act(kf[:dk, :n], ps[bB][:dk, :n], AF.Sigmoid, [psk(bB)], [kk_])
                ts("dve", kf[:dk, :n], kf[:dk, :n], omT[:, dr, j, h:h + 1], lbT[:, dr, j, h:h + 1], ALU.mult, ALU.add,
                   [kk_, ("omT", dr, j), ("lbT", dr, j)], [kk_])
                yield
                act(gl[:dk, :n], kf[:dk, :n], AF.Ln, [kk_], [gk_])
                act(kf[:dk, :n], kf[:dk, :n], AF.Identity, [kk_], [kk_], scale=-1.0, bias=1.0)
            else:
                for k in range(8):
                    mm(ps[bB][:dk, :n], wviews["k"][:, k, :], hT[:, k, t0:t0 + n], k == 0, k == 7, rd + [wkeys["k"]], [psk(bB)])
                cp("act", kf[:dk, :n], ps[bB][:dk, :n], [psk(bB)], [kk_])
                yield
                for k in range(8):
                    mm(ps[bA][:32, :n], wviews["lr"][:, k, :], hT[:, k, t0:t0 + n], k == 0, k == 7, rd + [wkeys["lr"]], [psk(bA)])
                lr, lrk = D_["lr"].next()
                cp("dve", lr[:32, :n], ps[bA][:32, :n], [psk(bA)], [lrk])
                mm(ps[bB][:dk, :n], wgu[:, j, dr, h * 64:(h + 1) * 64], lr[:32, :n], True, True, [lrk, "wgu"], [psk(bB)])
                act(gl[:dk, :n], ps[bB][:dk, :n], AF.Exp, [psk(bB), "bgu"], [gk_], scale=-1.0, bias=bgu[:, j, dr, h:h + 1])
                yield
                act(gl[:dk, :n], gl[:dk, :n], AF.Ln, [gk_], [gk_], bias=1.0)
                ts("dve", gl[:dk, :n], gl[:dk, :n], -1.0 / 16.0, None, ALU.mult, None, [gk_], [gk_])
            yield
            bb, bk_ = tmpF["d"].next()
            if dr == 0:
                P.op("dve", lambda h_: h_.tensor_tensor_scan(out=bb[:dk, :n], data0=cst["rmf"][:dk, :n], data1=gl[:dk, :n],
                                                             initial=0.0, op0=ALU.mult, op1=ALU.add),
                     [gk_, "rmf"], [bk_])
            else:
                P.op("dve", lambda h_: h_.tensor_tensor_scan(out=bb[:dk, :n][:, ::-1],
                                                             data0=cst["rmb"][:dk, :n][:, ::-1],
                                                             data1=gl[:dk, :n][:, ::-1],
                                                             initial=0.0, op0=ALU.mult, op1=ALU.add),
                     [gk_, "rmb"], [bk_])
            yield
            b3 = bb[:dk, :n].rearrange("p (c t) -> p c t", t=64)
            mid = 31 if dr == 0 else 32
            rr, rk_ = tmpF["c"].next()
            tt("dve", rr[:dk, :n].rearrange("p (c t) -> p c t", t=64), b3,
               b3[:, :, mid:mid + 1].to_broadcast([dk, nch, 64]), ALU.subtract, [bk_], [rk_])
            ts("dve", rr[:dk, :n], rr[:dk, :n], 40.0, -40.0, ALU.min, ALU.max, [rk_], [rk_])
            yield
            ep, epk = tmpF["e"].next()
            act(ep[:dk, :n], rr[:dk, :n], AF.Exp, [rk_], [epk], bias=qconst)
            en, enk = tmpF["f"].next()
            act(en[:dk, :n], rr[:dk, :n], AF.Exp, [rk_], [enk], scale=-1.0)
            yield
            return dict(qf=(qf, qk_), kf=(kf, kk_), bb=(bb, bk_), ep=(ep, epk), en=(en, enk))

        def prep_b(dr, g, pa):
            D_ = DB[dr]
            tmpF, opB = D_["tmpF"], D_["opB"]
            bA = D_["pA"]
            t0, n = TGS[g]
            nch = n // 64
            qf, qk_ = pa["qf"]
            kf, kk_ = pa["kf"]
            bb, bk_ = pa["bb"]
            ep, epk = pa["ep"]
            en, enk = pa["en"]
            b3 = bb[:dk, :n].rearrange("p (c t) -> p c t", t=64)
            last = 63 if dr == 0 else 0
            qh, qhk = opB["qh"].next()
            tt("dve", qh[:dk, :n], qf[:dk, :n], ep[:dk, :n], ALU.mult, [qk_, epk], [qhk])
            kh, khk = opB["kh"].next()
            tt("dve", kh[:dk, :n], kf[:dk, :n], en[:dk, :n], ALU.mult, [kk_, enk], [khk])
            eb, ebk = tmpF["e"].next()
            act(eb[:dk, :n], bb[:dk, :n], AF.Exp, [bk_], [ebk], bias=qconst)
            dd, ddk = tmpF["f"].next()
            tt("dve", dd[:dk, :n].rearrange("p (c t) -> p c t", t=64), b3,
               b3[:, :, last:last + 1].to_broadcast([dk, nch, 64]), ALU.subtract, [bk_], [ddk])
            act(dd[:dk, :n], dd[:dk, :n], AF.Exp, [ddk], [ddk], scale=-1.0)
            c0, c0k = D_["c0"].next()
            act(c0[:dk, :nch], b3[:, :, last], AF.Exp, [bk_], [c0k])
            qb, qbk = opB["qb"].next()
            tt("dve", qb[:dk, :n], qf[:dk, :n], eb[:dk, :n], ALU.mult, [qk_, ebk], [qbk])
            kd, kdk = opB["kd"].next()
            tt("dve", kd[:dk, :n], kf[:dk, :n], dd[:dk, :n], ALU.mult, [kk_, ddk], [kdk])
            kt, ktk = D_["kdt"].next()
            ntile = n // 128
            for q in range(ntile):
                tr(psb[bA][:, q * 128:q * 128 + dk], kd[:dk, q * 128:(q + 1) * 128], cst["identb"][:dk, :dk],
                   [kdk, "identb"], [psk(bA)])
            cp("dve", kt[:, :ntile, :dk], psb[bA][:, :ntile * 128].rearrange("p (a b) -> p a b", b=128)[:, :, :dk],
               [psk(bA)], [ktk])
            return dict(qh=(qh, qhk), kh=(kh, khk), qb=(qb, qbk), kt=(kt, ktk), c0=(c0, c0k))

        def scan_tile(dr, g, i, pr, state):
            D_ = DB[dr]
            Sf, Sk = D_["S"]
            t0, n = TGS[g]
            q = i - t0 // 128
            cols = slice(q * 128, (q + 1) * 128)
            qh, qhk = pr["qh"]
            kh, khk = pr["kh"]
            qb, qbk = pr["qb"]
            kt, ktk = pr["kt"]
            c0, c0k = pr["c0"]
            ab = D_["ab"]
            ac = slice(0, 128)
            uc = slice(256, 384)
            mm(ps[ab][:, ac], kh[:dk, cols], qh[:dk, cols], True, True, [khk, qhk], [psk(ab)])
            am, amk = D_["attm"].next()
            mk = "maskf" if dr == 0 else "maskb"
            tt("dve", am[:, :], ps[ab][:, ac], cst[mk][:], ALU.mult, [psk(ab), mk], [amk])
            bo = D_["ob"]
            mm(ps[bo][:, 0:128], vtok[:, i, :], am[:, :], True, False, [vk, amk], [psk(bo)])
            order = (0, 1) if dr == 0 else (1, 0)
            for ci, cch in enumerate(order):
                ccols = slice(q * 128 + cch * 64, q * 128 + (cch + 1) * 64)
                rows = slice(cch * 64, (cch + 1) * 64)
                sbf, sbk = state["sb"]
                mm(ps[bo][:, cch * 64:(cch + 1) * 64], sbf[:dk, :], qb[:dk, ccols], False, ci == 1,
                   [sbk, qbk], [psk(bo)])
                mm(ps[ab][:dk, uc], kt[rows, q, :dk], vtok[rows, i, :], True, True, [ktk, vk], [psk(ab)])
                chunk_idx = q * 2 + cch
                stt(Sf[:dk, :], Sf[:dk, :], c0[:dk, chunk_idx:chunk_idx + 1], ps[ab][:dk, uc], ALU.mult, ALU.add,
                    [Sk, c0k, psk(ab)], [Sk])
                nsb, nsk = D_["Sb"].next()
                cp("act", nsb[:dk, :], Sf[:dk, :], [Sk], [nsk])
                state["sb"] = (nsb, nsk)
                if ci == 0:
                    yield

        def tiles_of(g):
            t0, n = TGS[g]
            return list(range(t0 // 128, (t0 + n) // 128))

        gain_ap = (ognT if kind == "hgrn" else glagT)[:, j:j + 1]
        gain_key = "ognT" if kind == "hgrn" else "glagT"
        visited = set()
        done_g = {g: 0 for g in range(5)}

        def post(g, dr):
            D_ = DB[dr]
            bA, bB = D_["pA"], D_["pB"]
            t0, n = TGS[g]
            sq, sk = sqb.next()
            act(sq[:, :n], oacc[:, t0:t0 + n], AF.Square, [(ok_, g)], [sk])
            mm(ps[bA][:, :n], cst["ones"][:], sq[:, :n], True, True, [sk, "ones"], [psk(bA)])
            ra, rak = postF["a"].next()
            act(ra[:, :n], ps[bA][:, :n], AF.Ln, [psk(bA)], [rak], scale=1.0 / 128, bias=EPS)
            act(ra[:, :n], ra[:, :n], AF.Exp, [rak], [rak], scale=-0.5)
            for k in range(8):
                mm(ps[bB][:, :n], wviews["g"][:, k, :], hT[:, k, t0:t0 + n], k == 0, k == 7, [("hT", g), wkeys["g"]], [psk(bB)])
            sg, sgk = postF["b"].next()
            act(sg[:, :n], ps[bB][:, :n], AF.Silu, [psk(bB)], [sgk])
            t1, t1k = postF["c"].next()
            stt(t1[:, :n], oacc[:, t0:t0 + n], gain_ap, ra[:, :n], ALU.mult, ALU.mult, [(ok_, g), gain_key, rak], [t1k])
            oT, oTk = oTb.next()
            tt("dve", oT[:, :n], t1[:, :n], sg[:, :n], ALU.mult, [t1k, sgk], [oTk])
            outproj_tg(l, wviews["wo"], wkeys["wo"], oT, oTk, g, Rot([bA, bB]))

        def run_gen(gen):
            try:
                while True:
                    next(gen)
            except StopIteration as e_:
                return e_.value

        def step_gen(gen_box):
            if gen_box[2]:
                return
            try:
                next(gen_box[0])
            except StopIteration as e_:
                gen_box[1] = e_.value
                gen_box[2] = True

        def dirgen(dr, groups):
            D_ = DB[dr]
            Sf, Sk = D_["S"]
            P.op("pool", lambda h_: h_.memset(Sf[:, :], 0.0), [], [Sk])
            sb0, sbk0 = D_["Sb"].next()
            P.op("pool", lambda h_: h_.memset(sb0[:, :], 0.0), [], [sbk0])
            state = {"sb": (sb0, sbk0)}
            box = [prep_a(dr, groups[0]), None, False]
            while not box[2]:
                step_gen(box)
                yield
            for gi, g in enumerate(groups):
                tiles = tiles_of(g)
                if dr == 1:
                    tiles = tiles[::-1]
                pr = prep_b(dr, g, box[1])
                yield
                if gi + 1 < len(groups):
                    box = [prep_a(dr, groups[gi + 1]), None, False]
                for i in tiles:
                    for _ in scan_tile(dr, g, i, pr, state):
                        step_gen(box)
                        yield
                    bo = D_["ob"]
                    if i not in visited:
                        visited.add(i)
                        cp("act", oacc[:, i * 128:(i + 1) * 128], ps[bo][:, 0:128], [psk(bo)], [(ok_, g)])
                    else:
                        tt("dve", oacc[:, i * 128:(i + 1) * 128], ps[bo][:, 0:128], oacc[:, i * 128:(i + 1) * 128],
                           ALU.add, [psk(bo), (ok_, g)], [(ok_, g)])
                    done_g[g] += 1
                    step_gen(box)
                    if done_g[g] == 2 * len(tiles):
                        post(g, dr)
                    yield
                while not box[2]:
                    step_gen(box)
                    yield

        gens = [dirgen(0, [0, 1, 2, 3, 4]), dirgen(1, [0, 4, 3, 2, 1])]
        alive = [True, True]
        while any(alive):
            for dr in range(2):
                if alive[dr]:
                    try:
                        next(gens[dr])
                    except StopIteration:
                        alive[dr] = False

    def odd_mixer(l):
        j = l // 2
        W = w_in_odd[j]
        for h in range(8):
            s1, k1 = WR.load([(0, [8, 128], wcols(W, 0 * 1024 + h * 128, 128)),
                              (1024, [8, 128], wcols(W, 1 * 1024 + h * 128, 128))])
            s2, k2 = WR.load([(0, [8, 128], wcols(W, 2 * 1024 + h * 128, 128)),
                              (1024, [8, 128], wcols(W, 3 * 1024 + h * 128, 128))])
            s3, k3 = WR.load([(0, [8, 128], wcols(W, 4 * 1024 + h * 128, 128)),
                              (1024, [1024], w_out_odd[j][h * 128:(h + 1) * 128, :])])
            wv = dict(q=WR.view(s1, 0, [8, 128]), gf=WR.view(s1, 1024, [8, 128]),
                      gb=WR.view(s2, 0, [8, 128]), v=WR.view(s2, 1024, [8, 128]),
                      g=WR.view(s3, 0, [8, 128]), wo=WR.view(s3, 1024, [1024]))
            wk = dict(q=k1, gf=k1, gb=k2, v=k2, g=k3, wo=k3)
            scan_head(l, "hgrn", h, wv, wk)
            WR.release(s1)
            WR.release(s2)
            WR.release(s3)

    def gla_mixer(l):
        j = l // 2
        W = w_in_even[j]
        for h in range(4):
            s1, k1 = WR.load([(0, [8, 64], wcols(W, 1536 + h * 64, 64)),
                              (512, [8, 64], wcols(W, 1792 + h * 64, 64)),
                              (1024, [8, 32], wcols(W, 3072, 32))])
            s2, k2 = WR.load([(0, [8, 128], wcols(W, 2048 + h * 128, 128)),
                              (1024, [8, 128], wcols(W, 2560 + h * 128, 128))])
            s3, k3 = WR.load([(0, [1024], w_out_even[j][512 + h * 128:512 + (h + 1) * 128, :])])
            wv = dict(q=WR.view(s1, 0, [8, 64]), k=WR.view(s1, 512, [8, 64]), lr=WR.view(s1, 1024, [8, 32]),
                      v=WR.view(s2, 0, [8, 128]), g=WR.view(s2, 1024, [8, 128]), wo=WR.view(s3, 0, [1024]))
            wk = dict(q=k1, k=k1, lr=k1, v=k2, g=k2, wo=k3)
            scan_head(l, "gla", h, wv, wk)
            WR.release(s1)
            WR.release(s2)
            WR.release(s3)

    def att_mixer(l):
        j = l // 2
        W = w_in_even[j]
        AR.reset()
        kvb = []
        for i_ in range(2):
            kT_, kTk_ = AR.alloc(f"kT{i_}", [T], BF16, 1)[0]
            subkeys(kTk_)
            vt_, vk_ = AR.alloc(f"vtokA{i_}", [NT, 132], BF16, 1)[0]
            P.op("pool", lambda h_, vt_=vt_: h_.memset(vt_[:, :, 128:132], 1.0), [], [vk_])
            kvb.append((kT_, kTk_, vt_, vk_))
        qTb = Rot(AR.alloc("qT", [512], BF16, 2))
        PTb = Rot(AR.alloc("PT", [512], BF16, 4))
        sqb = Rot(AR.alloc("asq", [512], BF16, 2))
        rawb = Rot(AR.alloc("araw", [512], F32, 2))
        rsb = Rot(AR.alloc("arstd", [512], F32, 2))
        qgf = Rot(AR.alloc("aqg", [512], F32, 2))
        qgb = Rot(AR.alloc("aqgb", [512], BF16, 2))
        t1b = Rot(AR.alloc("at1", [512], F32, 1))
        t2b = Rot(AR.alloc("at2", [512], F32, 1))
        cosb = Rot(AR.alloc("cos", [512], F32, 2))
        sinb = Rot(AR.alloc("sin", [512], F32, 2))
        ptb = Rot(AR.alloc("pt", [128], F32, 2))
        pob = Rot(AR.alloc("po", [128], F32, 4))
        ponb = Rot(AR.alloc("pon", [128], BF16, 4))
        stb = Rot(AR.alloc("stat", [8], F32, 4))
        oTb = Rot(AR.alloc("oTa", [512], BF16, 2))
        sqj = Rot(AR.alloc("sqjunk", [128], F32, 1))
        zt, ztk = AR.alloc("zero512", [512], BF16, 1)[0]
        P.op("pool", lambda h_: h_.memset(zt[:, :], 0.0), [], [ztk])
        rope_cnt = [0]
        MB = 7

        def qk_prep(wview, wkey, qk, g, out, outkey, outcols):
            t0, n = TGS[g]
            for k in range(8):
                mm(ps[MB][:, :n], wview[:, k, :], hT[:, k, t0:t0 + n], k == 0, k == 7, [("hT", g), wkey], [psk(MB)])
            raw, rwk = rawb.next()
            cp("act", raw[:, :n], ps[MB][:, :n], [psk(MB)], [rwk])
            sq, sk = sqb.next()
            act(sq[:, :n], ps[MB][:, :n], AF.Square, [psk(MB)], [sk])
            yield
            mm(ps[MB][:, :n], cst["blk64"][:], sq[:, :n], True, True, [sk, "blk64"], [psk(MB)])
            rs, rk = rsb.next()
            act(rs[:, :n], ps[MB][:, :n], AF.Ln, [psk(MB)], [rk], scale=1.0 / 64, bias=EPS)
            act(rs[:, :n], rs[:, :n], AF.Exp, [rk], [rk], scale=-0.5)
            if g == 0:
                stt(out[:, outcols], raw[:, :n], qkg[:, j, qk:qk + 1], rs[:, :n], ALU.mult, ALU.mult,
                    [rwk, "qkg", rk], [outkey])
                return
            qg, qgk = qgf.next()
            stt(qg[:, :n], raw[:, :n], qkg[:, j, qk:qk + 1], rs[:, :n], ALU.mult, ALU.mult, [rwk, "qkg", rk], [qgk])
            qb_, qbk = qgb.next()
            cp("act", qb_[:, :n], qg[:, :n], [qgk], [qbk])
            cs, ck = cosb.next()
            sn, snk = sinb.next()
            l0 = t0 - CTX
            r = rope_cnt[0]
            rope_cnt[0] += 1
            P.dma("sp", f"rp{r % 2}", lambda h_: h_.dma_start(out=cs[:, :n], in_=dram["cosT"][:, l0:l0 + n]), writes=[ck])
            P.dma("sp", f"rp{r % 2}", lambda h_: h_.dma_start(out=sn[:, :n], in_=dram["sinT"][:, l0:l0 + n]), writes=[snk])
            t1, t1k = t1b.next()
            tt("dve", t1[:, :n], qg[:, :n], cs[:, :n], ALU.mult, [qgk, ck], [t1k])
            yield
            mm(ps[MB][:, :n], cst["rotm"][:], qb_[:, :n], True, True, [qbk, "rotm"], [psk(MB)])
            t2, t2k = t2b.next()
            tt("dve", t2[:, :n], ps[MB][:, :n], sn[:, :n], ALU.mult, [psk(MB), snk], [t2k])
            tt("dve", out[:, outcols], t1[:, :n], t2[:, :n], ALU.add, [t1k, t2k], [outkey])

        HW = {}

        def load_head(h):
            sa, ka = WR.load([(0, [8, 128], wcols(W, h * 128, 128)), (1024, [8, 128], wcols(W, 512 + h * 128, 128))])
            sb_, kb = WR.load([(0, [8, 128], wcols(W, 1024 + h * 128, 128)),
                               (1024, [1024], w_out_even[j][h * 128:(h + 1) * 128, :])])
            HW[h] = dict(sa=sa, sb=sb_, ka=ka, kb=kb, wq=WR.view(sa, 0, [8, 128]), wk=WR.view(sa, 1024, [8, 128]),
                         wv=WR.view(sb_, 0, [8, 128]), wo=WR.view(sb_, 1024, [1024]))

        def k_unit(h, g):
            kT_, kTk_, vt_, vk_ = kvb[h % 2]
            t0, n = TGS[g]
            return lambda: qk_prep(HW[h]["wk"], HW[h]["ka"], 1, g, kT_, (kTk_, g), slice(t0, t0 + n))

        def v_unit(h, i0):
            kT_, kTk_, vt_, vk_ = kvb[h % 2]

            def f():
                nt_ = min(4, NT - i0)
                for q in range(nt_):
                    i = i0 + q
                    for k in range(8):
                        mm(ps[MB][:, q * 128:(q + 1) * 128], hT[:, k, i * 128:(i + 1) * 128], HW[h]["wv"][:, k, :],
                           k == 0, k == 7, [("hT", tg_of_tile(i)), HW[h]["kb"]], [psk(MB)])
                cp("dve", vt_[:, i0:i0 + nt_, 0:128], ps[MB][:, :nt_ * 128].rearrange("p (a b) -> p a b", b=128),
                   [psk(MB)], [vk_])
                return
                yield
            return f

        qready = {}

        def q_unit(h, g):
            def f():
                qT, qTk = qTb.next()
                qready[(h, g)] = (qT, qTk)
                yield from qk_prep(HW[h]["wq"], HW[h]["ka"], 0, g, qT, qTk, slice(0, TGS[g][1]))
            return f

        bg = []

        def bg_add(unit):
            bg.append(unit())

        def drain(nu=1):
            for _ in range(nu):
                while bg:
                    try:
                        next(bg[0])
                        break
                    except StopIteration:
                        bg.pop(0)

        def flush():
            while bg:
                try:
                    next(bg[0])
                except StopIteration:
                    bg.pop(0)

        def run_now(unit):
            for _ in unit():
                pass

        load_head(0)
        for g in range(5):
            run_now(k_unit(0, g))
        for i0 in range(0, NT, 4):
            run_now(v_unit(0, i0))
        run_now(q_unit(0, 0))
        for h in range(4):
            kT, kTk, vtok, vk = kvb[h % 2]
            if h + 1 < 4:
                load_head(h + 1)
            plan = {0: [], 1: [], 2: [], 3: [], 4: []}
            plan[0].append(q_unit(h, 1))
            plan[1].append(q_unit(h, 2))
            plan[2].append(q_unit(h, 3))
            plan[3].append(q_unit(h, 4))
            if h + 1 < 4:
                for g_ in range(3):
                    plan[1].append(k_unit(h + 1, g_))
                for g_ in range(3, 5):
                    plan[2].append(k_unit(h + 1, g_))
                vs_ = list(range(0, NT, 4))
                for i0 in vs_[:2]:
                    plan[2].append(v_unit(h + 1, i0))
                for i0 in vs_[2:]:
                    plan[3].append(v_unit(h + 1, i0))
                plan[4].append(q_unit(h + 1, 0))
            for g in range(5):
                t0, n = TGS[g]
                nq = n // 128
                for u_ in plan[g]:
                    bg_add(u_)
                qT, qTk = qready[(h, g)]
                ktiles = list(range(2)) if g == 0 else list(range(NT))
                sbanks = Rot([4, 5, 6])
                for qt in range(nq):
                    mm(ps[qt][:, 0:512], zt[:, 0:128], zt[:, 0:512], True, False, [ztk], [psk(qt)])
                prev = None
                for ki, kt in enumerate(ktiles):
                    cur = []
                    for c in range(2):
                        bs = sbanks.next()
                        mm(ps[bs][:, :n], kT[c * 64:(c + 1) * 64, kt * 128:(kt + 1) * 128], qT[c * 64:(c + 1) * 64, :n],
                           True, True, [(kTk, tg_of_tile(kt)), qTk], [psk(bs)])
                        cur.append((c, bs))
                    its = []
                    for (c, bs) in cur:
                        pt, ptk = PTb.next()
                        act(pt[:, :n], ps[bs][:, :n], AF.Exp, [psk(bs)], [ptk], scale=0.125)
                        its.append((c, kt, pt, ptk))
                    if prev is not None:
                        for it in prev:
                            emit_av(it, nq, ktiles, vtok, vk)
                    prev = its
                    if g > 0:
                        drain(1)
                for it in prev:
                    emit_av(it, nq, ktiles, vtok, vk)
                flush()
                items = []
                for qt in range(nq):
                    acc = ps[qt]
                    sts, stk = stb.next()
                    P.op("dve", lambda h_, acc=acc, sts=sts: h_.reciprocal(out=sts[:, 0:1], in_=acc[:, 128:129]),
                         [psk(qt)], [stk])
                    P.op("dve", lambda h_, acc=acc, sts=sts: h_.reciprocal(out=sts[:, 1:2], in_=acc[:, 384:385]),
                         [psk(qt), stk], [stk])
                    tt("dve", sts[:, 2:3], sts[:, 1:2], nlam[:, j:j + 1], ALU.mult, [stk, ("nlam", j)], [stk])
                    t_, tk_ = ptb.next()
                    ts("dve", t_[:, :], acc[:, 256:384], sts[:, 2:3], None, ALU.mult, None, [psk(qt), stk], [tk_])
                    o_, ok2 = pob.next()
                    stt(o_[:, :], acc[:, 0:128], sts[:, 0:1], t_[:, :], ALU.mult, ALU.add, [psk(qt), stk, tk_], [ok2])
                    items.append((o_, ok2, sts, stk))

                def post_rest(items=items, g=g, n=n, h=h):
                    t0_ = TGS[g][0]
                    oT, oTk = oTb.next()
                    ons = []
                    for (o_, ok2, sts, stk) in items:
                        jk, jkk = sqj.next()
                        act(jk[:, :], o_[:, :], AF.Square, [ok2], [jkk, stk], accum_out=sts[:, 3:4])
                        act(sts[:, 4:5], sts[:, 3:4], AF.Ln, [stk], [stk], scale=1.0 / 128, bias=EPS)
                        act(sts[:, 5:6], sts[:, 4:5], AF.Exp, [stk], [stk], scale=-0.5)
                        on_, onk = ponb.next()
                        ts("dve", on_[:, :], o_[:, :], sts[:, 5:6], None, ALU.mult, None, [ok2, stk], [onk])
                        ons.append((on_, onk))
                    yield
                    for qt, (on_, onk) in enumerate(ons):
                        tr(psb[MB][:, qt * 128:(qt + 1) * 128], on_[:, :], cst["identb"][:], [onk, "identb"], [psk(MB)])
                    P.op("act", lambda h_, oT=oT, n=n: h_.activation(out=oT[:, :n], in_=psb[MB][:, :n], func=AF.Copy,
                                                                  scale=sublnT[:, j:j + 1]),
                         [psk(MB), "sublnT"], [oTk])
                    yield
                    wo_ = HW[h]["wo"]
                    for d in range(8):
                        mm(ps[MB][:, :n], wo_[:, d * 128:(d + 1) * 128], oT[:, :n], True, True, [HW[h]["kb"], oTk], [psk(MB)])
                        stt(xT[:, d, t0_:t0_ + n], ps[MB][:, :n], modcol(l, 2, d, g), xT[:, d, t0_:t0_ + n], ALU.mult, ALU.add,
                            [psk(MB), ("modT", l), xk(g, d)], [xk(g, d)])
                        yield

                bg_add(post_rest)
            flush()
            WR.release(HW[h]["sa"])
            WR.release(HW[h]["sb"])

    def emit_av(pend, nq, ktiles, vtok, vk):
        c, kt, pt, ptk = pend
        for qt in range(nq):
            mm(ps[qt][:, c * 256:c * 256 + 129], pt[:, qt * 128:(qt + 1) * 128], vtok[:, kt, 0:129],
               False, (kt == ktiles[-1]) and c == 1, [ptk, vk], [psk(qt)])

    ada_layer(0)
    for l in range(nlayers):
        norm_phase(l, 1)
        if l % 2 == 0:
            if do_att:
                att_mixer(l)
            if do_gla:
                gla_mixer(l)
        else:
            if do_odd:
                odd_mixer(l)
        norm_phase(l, 2)
        if l + 1 < nlayers:
            ada_layer(l + 1)
        if do_ffn:
            ffn_phase(l)

    AR.reset()
    ob = Rot(AR.alloc("otile", [1024], F32, 3))
    evac = Rot(["act", "dve"])
    cnt = 0
    for i in range(2, NT):
        ot, okk = ob.next()
        g = tg_of_tile(i)
        for half in range(2):
            b = (cnt) % 4
            cnt += 1
            for q in range(4):
                k = half * 4 + q
                tr(ps[b][:, q * 128:(q + 1) * 128], xT[:, k, i * 128:(i + 1) * 128], cst["identf"][:],
                   [xk(g, k), "identf"], [psk(b)])
            cp(evac.next(), ot[:, half * 512:(half + 1) * 512], ps[b][:, :], [psk(b)], [okk])
        P.dma("sp", f"out{i % 3}", lambda h_, ot=ot, i=i: h_.dma_start(out=y[(i - 2) * 128:(i - 1) * 128, :], in_=ot),
              reads=[okk], writes=[("y", i)])
    P.finish("sp")
    if cfg.get("verbose"):
        print("ops per engine:", {e: len(P.ops[e]) + len(P.hoisted[e]) for e in ENGS}, "arena", AR.off)
    P.emit()
    st.close()
    return nc


def host_layout(inp, b):
    f = lambda a: np.ascontiguousarray(a, dtype=np.float32)
    m = {}
    m["xin"] = f(np.concatenate([inp["ctx"][b], inp["x"][b]], axis=0))
    for nm in ("w_ada", "w_in_even", "w_out_even", "w_in_odd", "w_out_odd", "w_ffn_in", "w_ffn_out"):
        m[nm] = inp[nm]
    fm = lambda v: np.asarray(v).reshape(-1, 128).T
    cv = np.stack([fm(inp["c"][b]), fm(inp["c_ctx"])], axis=-1)
    m["cvec"] = f(cv.reshape(128, 16))
    m["badaT"] = f(np.stack([fm(inp["b_ada"][l]) for l in range(4)], axis=1).reshape(128, 4 * 48))
    m["g1T"] = f(np.stack([fm(inp["norm1_gain"][l]) for l in range(4)], axis=1).reshape(128, 32))
    m["g2T"] = f(np.stack([fm(inp["norm2_gain"][l]) for l in range(4)], axis=1).reshape(128, 32))
    qk = np.asarray(inp["qk_gain_a"])
    m["qkg"] = f(np.tile(qk.transpose(2, 0, 1), (2, 1, 1)).reshape(128, 4))
    m["lamb"] = f(np.broadcast_to(np.asarray(inp["lambda_a"]).reshape(1, 512), (128, 512)))
    m["sublnT"] = f(np.asarray(inp["subln_gain_a"]).T)
    wg = np.asarray(inp["w_gate_up_b"])
    wgp = np.zeros((32, 2, 2, 256), np.float32)
    for j in range(2):
        for dr in range(2):
            wgp[dr * 16:(dr + 1) * 16, j, dr, :] = wg[j, dr]
    m["wgu"] = f(wgp.reshape(32, 1024))
    bg = np.asarray(inp["b_gate_up_b"]).reshape(2, 2, 4, 64)
    m["bgu"] = f(bg.transpose(3, 0, 1, 2).reshape(64, 16))
    m["glagT"] = f(np.asarray(inp["onorm_gain_b"]).T)
    lb = np.asarray(inp["lb_raw_c"]).reshape(2, 4, 8, 128)
    m["lbraw"] = f(lb.transpose(3, 0, 1, 2).reshape(128, 64))
    m["ognT"] = f(np.asarray(inp["onorm_gain_c"]).T)
    return m


_CACHE = {}


def run(inputs, cfg=None, trace=False, ncores=8):
    cfg = cfg or {}
    key = tuple(sorted(cfg.items()))
    if key not in _CACHE:
        _CACHE[key] = build_program(cfg)
    nc = _CACHE[key]
    inp = {k: np.asarray(v) for k, v in inputs.items()}
    consts = host_consts()
    in_maps = []
    for b in range(ncores):
        m = host_layout(inp, b)
        m.update(consts)
        in_maps.append(m)
    res = run_bass_kernel_spmd(nc, in_maps, core_ids=list(range(ncores)), trace=trace)
    out = np.stack([np.asarray(r["y"]) for r in res.results], axis=0).astype(np.float32)
    return out, res


def kernel(**inputs):
    out, _ = run(inputs)
    return out
```

```python
import contextlib
import math
import numpy as np
import concourse.bass as bass
import concourse.mybir as mybir
from concourse.bass_utils import run_bass_kernel_spmd

F32 = mybir.dt.float32
BF16 = mybir.dt.bfloat16
ALU = mybir.AluOpType
AF = mybir.ActivationFunctionType

ENGS = ("pe", "act", "dve", "pool", "sp")
SELF_SYNC = {"pe": False, "act": True, "dve": True, "pool": True, "sp": False}

D = 1024
T = 2304
NT = 18
CTX = 256
TGS = [(0, 256), (256, 512), (768, 512), (1280, 512), (1792, 512)]
DEPTH = 4
DFF = 2816
EPS = 1e-6
NS = 6
SLOT = 2048


class Prog:
    def __init__(self, nc):
        self.nc = nc
        self.ops = {e: [] for e in ENGS}
        self.hoisted = {e: [] for e in ENGS}
        self.count = {e: 0 for e in ENGS}
        self.waited = {e: {} for e in ENGS}
        self.last_w = {}
        self.readers = {}
        self.dma_count = {}
        self.dma_sems = []
        self.used = {e: set() for e in ENGS}

    def _deps(self, reads, writes):
        deps = []
        for k in reads:
            deps.extend(self.last_w.get(k, ()))
        for k in writes:
            deps.extend(self.last_w.get(k, ()))
            deps.extend(self.readers.get(k, ()))
        return deps

    def _waits(self, eng, deps, cache=True):
        need = {}
        for (s, v) in deps:
            if s in ENGS:
                if s == eng and not SELF_SYNC[eng]:
                    continue
            else:
                v = self.dma_count[s]
            if v > need.get(s, 0):
                need[s] = v
        waits = []
        for s, v in need.items():
            if (not cache) or self.waited[eng].get(s, 0) < v:
                if cache:
                    self.waited[eng][s] = v
                waits.append((s, v))
                if s in ENGS:
                    self.used[s].add(v)
        return waits

    def _record(self, tok, reads, writes):
        for k in writes:
            self.last_w[k] = [tok]
            self.readers[k] = []
        for k in reads:
            if k not in writes:
                self.readers.setdefault(k, []).append(tok)

    def alias(self, new_keys, old_keys):
        toks = []
        for k in old_keys:
            toks.extend(self.last_w.get(k, ()))
            toks.extend(self.readers.get(k, ()))
        toks = list(set(toks))
        for k in new_keys:
            self.last_w[k] = list(toks)
            self.readers[k] = []

    def op(self, eng, fn, reads=(), writes=()):
        waits = self._waits(eng, self._deps(reads, writes))
        self.count[eng] += 1
        tok = (eng, self.count[eng])
        self.ops[eng].append((waits, fn, tok))
        self._record(tok, reads, writes)
        return tok

    def dma(self, eng, sem, fn, reads=(), writes=(), hoist_pos=None):
        if sem not in self.dma_count:
            self.dma_count[sem] = 0
            self.dma_sems.append(sem)
        waits = self._waits(eng, self._deps(reads, writes), cache=(hoist_pos is None))
        self.dma_count[sem] += 16
        tok = (sem, self.dma_count[sem])
        if hoist_pos is None:
            self.ops[eng].append((waits, fn, tok))
        else:
            self.hoisted[eng].append((hoist_pos, len(self.hoisted[eng]), (waits, fn, tok)))
        self._record(tok, reads, writes)
        return tok

    def pos(self, eng):
        return len(self.ops[eng])

    def finish(self, eng="sp"):
        waits = []
        for s in self.dma_sems:
            waits.append((s, self.dma_count[s]))
        for e in ENGS:
            if e != eng and self.count[e] > 0:
                waits.append((e, self.count[e]))
                self.used[e].add(self.count[e])
        self.ops[eng].append((waits, None, None))

    def emit(self):
        nc = self.nc
        with contextlib.ExitStack() as st:
            sems = {}
            for e in ENGS:
                sems[e] = st.enter_context(nc.semaphore("s_" + e))
            for s in self.dma_sems:
                sems[s] = st.enter_context(nc.semaphore("d_" + s))
            rank = {e: {v: i + 1 for i, v in enumerate(sorted(self.used[e]))} for e in ENGS}
            block = st.enter_context(nc.Block())

            def run(e, h):
                hoist = sorted(self.hoisted[e], key=lambda x: (x[0], x[1]))
                hi = 0
                base = self.ops[e]
                seq = []
                for i, o in enumerate(base):
                    while hi < len(hoist) and hoist[hi][0] <= i:
                        seq.append(hoist[hi][2])
                        hi += 1
                    seq.append(o)
                while hi < len(hoist):
                    seq.append(hoist[hi][2])
                    hi += 1
                for waits, fn, tok in seq:
                    for (s, v) in waits:
                        h.wait_ge(sems[s], rank[s][v] if s in ENGS else v)
                    if fn is None:
                        continue
                    ins = fn(h)
                    if tok[0] in ENGS:
                        if tok[1] in rank[tok[0]]:
                            ins.then_inc(sems[tok[0]], 1)
                    else:
                        ins.then_inc(sems[tok[0]], 16)

            @block.tensor
            def _(h):
                run("pe", h)

            @block.scalar
            def _(h):
                run("act", h)

            @block.vector
            def _(h):
                run("dve", h)

            @block.gpsimd
            def _(h):
                run("pool", h)

            @block.sync
            def _(h):
                run("sp", h)


def lambda_init(l):
    return 0.8 - 0.6 * math.exp(-0.3 * l)


def host_consts():
    c = {}
    c["identf"] = np.eye(128, dtype=np.float32)
    c["ones"] = np.ones((128, 128), np.float32)
    p = np.arange(128)
    c["blk64"] = (p[:, None] // 64 == p[None, :] // 64).astype(np.float32)
    Rm = np.zeros((128, 128), np.float32)
    for q in range(128):
        w = (q % 64) % 32
        if w < 16:
            Rm[q + 16, q] = -1.0
        else:
            Rm[q - 16, q] = 1.0
    c["rotm"] = Rm
    s = p[:, None]
    t = p[None, :]
    same = (s // 64 == t // 64)
    c["maskf"] = (same & (s <= t)).astype(np.float32)
    c["maskb"] = (same & (s >= t)).astype(np.float32)
    tt = np.arange(512)
    c["rmf"] = np.broadcast_to((tt % 64 != 0).astype(np.float32), (128, 512)).copy()
    c["rmb"] = np.broadcast_to((tt % 64 != 63).astype(np.float32), (128, 512)).copy()
    tok = np.arange(2048)
    row = (tok // 64).astype(np.float32)
    col = (tok % 64).astype(np.float32)
    inv = (np.float32(10000.0) ** (-np.arange(0, 32, 2, dtype=np.float32) / np.float32(32))).astype(np.float32)
    cosT = np.zeros((128, 2048), np.float32)
    sinT = np.zeros((128, 2048), np.float32)
    for q in range(128):
        d = q % 64
        axis = d // 32
        j = d % 16
        ang = ((row if axis == 0 else col) * inv[j]).astype(np.float32)
        cosT[q] = np.cos(ang).astype(np.float32)
        sinT[q] = np.sin(ang).astype(np.float32)
    c["cosT"] = cosT
    c["sinT"] = sinT
    return c


CONST_BF = ("ones", "blk64", "rotm", "maskf", "maskb", "identb")


def build_program(cfg):
    nlayers = cfg.get("nlayers", DEPTH)
    do_att = cfg.get("att", True)
    do_gla = cfg.get("gla", True)
    do_odd = cfg.get("odd", True)
    do_ffn = cfg.get("ffn", True)

    nc = bass.Bass("TRN2", target_bir_lowering=False)
    dram = {}

    def din(name, shape):
        dram[name] = nc.dram_tensor(name, list(shape), F32, kind="ExternalInput").ap()
        return dram[name]

    xin = din("xin", [T, D])
    y = nc.dram_tensor("y", [2048, D], F32, kind="ExternalOutput").ap()
    w_ada = din("w_ada", [DEPTH, D, 6 * D])
    w_in_even = din("w_in_even", [2, D, 3104])
    w_out_even = din("w_out_even", [2, D, D])
    w_in_odd = din("w_in_odd", [2, D, 5120])
    w_out_odd = din("w_out_odd", [2, D, D])
    w_ffn_in = din("w_ffn_in", [DEPTH, D, 2 * DFF])
    w_ffn_out = din("w_ffn_out", [DEPTH, DFF, D])
    for nm in ("identf", "ones", "blk64", "rotm", "maskf", "maskb"):
        din(nm, [128, 128])
    din("rmf", [128, 512])
    din("rmb", [128, 512])
    din("cosT", [128, 2048])
    din("sinT", [128, 2048])
    din("cvec", [128, 16])
    din("badaT", [128, 4 * 48])
    din("g1T", [128, 32])
    din("g2T", [128, 32])
    din("qkg", [128, 4])
    din("lamb", [128, 2 * 256])
    din("sublnT", [128, 2])
    din("wgu", [32, 2 * 2 * 256])
    din("bgu", [64, 2 * 2 * 4])
    din("glagT", [128, 2])
    din("lbraw", [128, 2 * 4 * 8])
    din("ognT", [128, 2])

    P = Prog(nc)
    st = contextlib.ExitStack()

    def sb(name, shape, dt):
        return st.enter_context(nc.sbuf_tensor("s_" + name, list(shape), dt))

    xT = sb("xT", [128, 8, T], F32)
    hT = sb("hT", [128, 8, T], BF16)
    wring = sb("wring", [128, NS, SLOT], BF16)
    ps = [st.enter_context(nc.psum_tensor(f"ps{i}", [128, 512], F32)) for i in range(8)]
    psb = [p_[:].bitcast(BF16) for p_ in ps]

    cst = {}
    cst["identf"] = sb("c_identf", [128, 128], F32)
    for nm in ("ones", "blk64", "rotm", "maskf", "maskb", "identb"):
        cst[nm] = sb("c_" + nm, [128, 128], BF16)
    cst["rmf"] = sb("c_rmf", [128, 512], BF16)
    cst["rmb"] = sb("c_rmb", [128, 512], BF16)
    cvec = sb("cvec", [128, 8, 2], F32)
    scT = sb("scT", [128, 8, 2], BF16)
    badaT = sb("badaT", [128, 4, 48, 1], F32)
    g1T = sb("g1T", [128, 4, 8, 1], F32)
    g2T = sb("g2T", [128, 4, 8, 1], F32)
    modT = sb("modT", [128, 4, 48, 2], F32)
    A1 = sb("A1", [128, 4, 8, 2], F32)
    A2 = sb("A2", [128, 4, 8, 2], F32)
    qkg = sb("qkg", [128, 2, 2], F32)
    lamb = sb("lamb", [128, 2, 4, 64], F32)
    lamw = sb("lamw", [128, 2, 2, 64], F32)
    lams = sb("lams", [128, 2, 4], F32)
    nlam = sb("nlam", [128, 2], F32)
    sublnT = sb("sublnT", [128, 2], F32)
    wgu = sb("wgu", [32, 2, 2, 256], BF16)
    bgu = sb("bgu", [64, 2, 2, 4], F32)
    glagT = sb("glagT", [128, 2], F32)
    lbraw = sb("lbraw", [128, 2, 4, 8], F32)
    lbe = sb("lbe", [128, 2, 4, 8], F32)
    lbs = sb("lbs", [128, 2, 8], F32)
    lbT = sb("lbT", [128, 2, 2, 8], F32)
    omT = sb("omT", [128, 2, 2, 8], F32)
    ognT = sb("ognT", [128, 2], F32)

    ARENA = 32000
    arena = sb("arena", [128, ARENA], BF16)
    arena_f = arena[:].bitcast(F32)

    class Arena:
        def __init__(self):
            self.off = 0
            self.keys = []
            self.old_keys = []

        def reset(self):
            best = {}
            toks = list(getattr(self, "summary", []))
            for k in self.keys:
                toks.extend(P.last_w.get(k, ()))
                toks.extend(P.readers.get(k, ()))
            for (s_, v_) in toks:
                if v_ > best.get(s_, 0):
                    best[s_] = v_
            self.summary = list(best.items())
            self.keys = []
            self.off = 0

        def alloc(self, name, free_shape, dt, n=1):
            size = int(np.prod(free_shape))
            outs = []
            for i in range(n):
                if dt == F32:
                    if self.off % 2:
                        self.off += 1
                    a = arena_f[:, self.off // 2: self.off // 2 + size]
                    self.off += 2 * size
                else:
                    a = arena[:, self.off: self.off + size]
                    self.off += size
                assert self.off <= ARENA, (name, self.off)
                if len(free_shape) == 2:
                    a = a.rearrange("p (a b) -> p a b", b=free_shape[1])
                self.uid = getattr(self, "uid", 0) + 1
                key = (name, i, self.uid)
                P.last_w[key] = list(getattr(self, "summary", []))
                P.readers[key] = []
                self.keys.append(key)
                outs.append((a, key))
            return outs

    AR = Arena()

    def subkeys(base, n=5):
        for g_ in range(n):
            k_ = (base, g_)
            P.last_w[k_] = list(P.last_w.get(base, ()))
            P.readers[k_] = []
            AR.keys.append(k_)

    class Rot:
        def __init__(self, items):
            self.items = items
            self.i = 0

        def next(self):
            it = self.items[self.i % len(self.items)]
            self.i += 1
            return it

    def mm(out, lhsT, rhs, start, stop, reads, writes):
        P.op("pe", lambda h: h.matmul(out, lhsT=lhsT, rhs=rhs, start=start, stop=stop), reads, writes)

    def tr(out, in_, ident, reads, writes):
        P.op("pe", lambda h: h.transpose(out, in_, ident), reads, writes)

    def act(out, in_, func, reads, writes, scale=1.0, bias=0.0, accum_out=None):
        if accum_out is None:
            P.op("act", lambda h: h.activation(out=out, in_=in_, func=func, bias=bias, scale=scale), reads, writes)
        else:
            P.op("act", lambda h: h.activation(out=out, in_=in_, func=func, bias=bias, scale=scale,
                                               accum_out=accum_out), reads, writes)

    def tt(eng, out, in0, in1, op, reads, writes):
        P.op(eng, lambda h: h.tensor_tensor(out=out, in0=in0, in1=in1, op=op), reads, writes)

    def ts(eng, out, in0, s1, s2, op0, op1, reads, writes):
        if s2 is None:
            P.op(eng, lambda h: h.tensor_scalar(out=out, in0=in0, scalar1=s1, scalar2=None, op0=op0), reads, writes)
        else:
            P.op(eng, lambda h: h.tensor_scalar(out=out, in0=in0, scalar1=s1, scalar2=s2, op0=op0, op1=op1),
                 reads, writes)

    def stt(out, in0, scalar, in1, op0, op1, reads, writes):
        P.op("dve", lambda h: h.scalar_tensor_tensor(out=out, in0=in0, scalar=scalar, in1=in1, op0=op0, op1=op1),
             reads, writes)

    def cp(eng, out, in_, reads, writes):
        if eng == "act":
            P.op("act", lambda h: h.activation(out=out, in_=in_, func=AF.Copy), reads, writes)
        else:
            P.op(eng, lambda h: h.tensor_copy(out=out, in_=in_), reads, writes)

    def psk(b):
        return ("ps", b)

    class WRing:
        def __init__(self):
            self.n = 0
            self.rel_pos = [0] * NS
            self.live = set()

        def load(self, parts):
            for _ in range(NS):
                s = self.n % NS
                self.n += 1
                if s not in self.live:
                    break
            else:
                raise RuntimeError("weight ring exhausted")
            self.live.add(s)
            key = ("w", s)
            for (off, shape, src) in parts:
                size = int(np.prod(shape))
                dst = wring[:, s, off:off + size]
                if len(shape) == 2:
                    dst = dst.rearrange("p (a b) -> p a b", b=shape[1])
                P.dma("pool", f"w{s}", lambda h, dst=dst, src=src: h.dma_start(out=dst, in_=src),
                      writes=[key], hoist_pos=self.rel_pos[s])
            return s, key

        def release(self, s):
            self.live.discard(s)
            self.rel_pos[s] = P.pos("pool")

        def view(self, s, off, shape):
            size = int(np.prod(shape))
            a = wring[:, s, off:off + size]
            if len(shape) == 2:
                a = a.rearrange("p (a b) -> p a b", b=shape[1])
            return a

    WR = WRing()

    def wcols(w2d, c0, ncols):
        return w2d.rearrange("(k p) n -> p k n", p=128)[:, :, c0:c0 + ncols]

    def wrows(w2d, r0, nrows):
        return w2d[r0:r0 + nrows, :].rearrange("(a p) n -> p a n", p=128)

    def load_small(dst, name, bfcast=False):
        src = dram[name]
        d2 = dst[:]
        if len(d2.shape) > 2:
            names = "abcdef"[: len(d2.shape) - 1]
            pat = "p " + " ".join(names) + " -> p (" + " ".join(names) + ")"
            d2 = d2.rearrange(pat)
        if bfcast:
            P.dma("pool", "cstp", lambda h: h.dma_start(out=d2, in_=src[:, :]), writes=[name])
        else:
            P.dma("sp", "cst", lambda h: h.dma_start(out=d2, in_=src[:, :]), writes=[name])

    load_small(cst["identf"], "identf")
    for nm in ("ones", "blk64", "rotm", "maskf", "maskb"):
        load_small(cst[nm], nm, bfcast=True)
    P.dma("pool", "cstp", lambda h: h.dma_start(out=cst["identb"][:], in_=dram["identf"][:, :]), writes=["identb"])
    load_small(cst["rmf"], "rmf", bfcast=True)
    load_small(cst["rmb"], "rmb", bfcast=True)
    for t_, nm in ((cvec, "cvec"), (badaT, "badaT"), (g1T, "g1T"), (g2T, "g2T"), (qkg, "qkg"), (lamb, "lamb"),
                   (sublnT, "sublnT"), (bgu, "bgu"), (glagT, "glagT"), (lbraw, "lbraw"), (ognT, "ognT")):
        load_small(t_, nm)
    load_small(wgu, "wgu", bfcast=True)

    act(scT[:], cvec[:], AF.Silu, ["cvec"], ["scT"])
    for j in range(2):
        tt("dve", lamw[:, j, :, :], lamb[:, j, 0:4:2, :], lamb[:, j, 1:4:2, :], ALU.mult, ["lamb"], [("lamw", j)])
        for i in range(2):
            P.op("dve", lambda h, j=j, i=i: h.reduce_sum(out=lams[:, j, i:i + 1], in_=lamw[:, j, i, :],
                                                          axis=mybir.AxisListType.X),
                 [("lamw", j)], [("lams", j, i)])
        act(lams[:, j, 2:4], lams[:, j, 0:2], AF.Exp, [("lams", j, 0), ("lams", j, 1)], [("lams", j, 2)])
        stt(nlam[:, j:j + 1], lams[:, j, 3:4], -lambda_init(2 * j), lams[:, j, 2:3], ALU.add, ALU.subtract,
            [("lams", j, 2)], [("nlam", j)])
        ts("dve", sublnT[:, j:j + 1], sublnT[:, j:j + 1], 1.0 - lambda_init(2 * j), None, ALU.mult, None,
           ["sublnT"], ["sublnT"])
    ts("dve", bgu[:], bgu[:], -1.0, None, ALU.mult, None, ["bgu"], ["bgu"])
    act(lbe[:], lbraw[:], AF.Exp, ["lbraw"], ["lbe"])
    for dr in range(2):
        tt("dve", lbs[:, dr, :], lbe[:, dr, 0, :], lbe[:, dr, 1, :], ALU.add, ["lbe"], [("lbs", dr)])
        tt("dve", lbs[:, dr, :], lbs[:, dr, :], lbe[:, dr, 2, :], ALU.add, ["lbe", ("lbs", dr)], [("lbs", dr)])
        tt("dve", lbs[:, dr, :], lbs[:, dr, :], lbe[:, dr, 3, :], ALU.add, ["lbe", ("lbs", dr)], [("lbs", dr)])
        P.op("dve", lambda h, dr=dr: h.reciprocal(out=lbs[:, dr, :], in_=lbs[:, dr, :]), [("lbs", dr)], [("lbs", dr)])
        tt("dve", lbT[:, dr, 0, :], lbe[:, dr, 1, :], lbs[:, dr, :], ALU.mult, ["lbe", ("lbs", dr)], [("lbT", dr, 0)])
        tt("dve", lbT[:, dr, 1, :], lbe[:, dr, 1, :], lbe[:, dr, 2, :], ALU.add, ["lbe"], [("lbT", dr, 1)])
        tt("dve", lbT[:, dr, 1, :], lbT[:, dr, 1, :], lbe[:, dr, 3, :], ALU.add, ["lbe", ("lbT", dr, 1)], [("lbT", dr, 1)])
        tt("dve", lbT[:, dr, 1, :], lbT[:, dr, 1, :], lbs[:, dr, :], ALU.mult, [("lbT", dr, 1), ("lbs", dr)],
           [("lbT", dr, 1)])
        for j in range(2):
            ts("dve", omT[:, dr, j, :], lbT[:, dr, j, :], -1.0, 1.0, ALU.mult, ALU.add, [("lbT", dr, j)], [("omT", dr, j)])

    AR.reset()
    xt_bufs = Rot(AR.alloc("xtile", [1024], F32, 3))
    evac = Rot(["act", "dve"])
    for i in range(NT):
        xt, xk = xt_bufs.next()
        P.dma("sp", f"xt{i % 3}", lambda h, xt=xt, i=i: h.dma_start(out=xt, in_=xin[i * 128:(i + 1) * 128, :]),
              writes=[xk])
        for half in range(2):
            b = (2 * i + half) % 4
            for q in range(4):
                k = half * 4 + q
                tr(ps[b][:, q * 128:(q + 1) * 128], xt[:, k * 128:(k + 1) * 128], cst["identf"][:],
                   [xk, "identf"], [psk(b)])
            e = evac.next()
            cp(e, xT[:, half * 4:half * 4 + 4, i * 128:(i + 1) * 128],
               ps[b][:, :].rearrange("p (a b) -> p a b", b=128), [psk(b)], [("xT", None)])

    def tg_of_tile(i):
        return 0 if i < 2 else 1 + (i - 2) // 4

    def xk(g, d):
        return ("xT", g, d)

    def xkall(g):
        return [("xT", g, d) for d in range(8)]

    for g in range(5):
        P.alias(xkall(g), [("xT", None)])

    def ada_layer(l):
        b = 7
        for blk in range(24):
            s, key = WR.load([(0, [8, 256], wcols(w_ada[l], blk * 256, 256))])
            wv = WR.view(s, 0, [8, 256])
            for sub in range(2):
                f = blk * 2 + sub
                for k in range(8):
                    mm(ps[b][:, 2 * f:2 * f + 2], wv[:, k, sub * 128:(sub + 1) * 128], scT[:, k, :],
                       k == 0, k == 7, [key, "scT"], [psk(b)])
            WR.release(s)
        ada_finish(l, b)

    def ada_finish(l, b):
        tt("dve", modT[:, l, :, :], ps[b][:, 0:96].rearrange("p (a b) -> p a b", b=2),
           badaT[:, l, :, 0:1].to_broadcast([128, 48, 2]), ALU.add, [psk(b), "badaT"], [("modT", l)])
        for (A, gT, m, nm) in ((A1, g1T, 1, "A1"), (A2, g2T, 4, "A2")):
            stt(A[:, l, :, :], modT[:, l, m * 8:(m + 1) * 8, :], 1.0,
                gT[:, l, :, 0:1].to_broadcast([128, 8, 2]), ALU.add, ALU.mult,
                [("modT", l), "g1T" if m == 1 else "g2T"], [(nm, l)])


    def modcol(l, m, k, g):
        c = 1 if g == 0 else 0
        return modT[:, l, m * 8 + k, c:c + 1]

    def norm_phase(l, which, skip_ctx=False):
        AR.reset()
        sqb = Rot(AR.alloc("sq", [512], BF16, 3))
        lnb = Rot(AR.alloc("lnv", [512], F32, 2))
        rsb = Rot(AR.alloc("rstd", [512], F32, 2))
        tmb = Rot(AR.alloc("ntmp", [512], F32, 3))
        A = A1 if which == 1 else A2
        Ak = ("A1", l) if which == 1 else ("A2", l)
        mshift = 0 if which == 1 else 3
        sqe = Rot(["act", "act", "dve"])
        for g, (t0, n) in enumerate(TGS):
            if skip_ctx and g == 0:
                continue
            b = g % 2
            c = 1 if g == 0 else 0
            for k in range(8):
                sq, sk = sqb.next()
                e = sqe.next()
                if e == "act":
                    act(sq[:, :n], xT[:, k, t0:t0 + n], AF.Square, [xk(g, k)], [sk])
                else:
                    tt("dve", sq[:, :n], xT[:, k, t0:t0 + n], xT[:, k, t0:t0 + n], ALU.mult, [xk(g, k)], [sk])
                mm(ps[b][:, :n], cst["ones"][:], sq[:, :n], k == 0, k == 7, [sk, "ones"], [psk(b)])
            lnv, lk = lnb.next()
            rstd, rk = rsb.next()
            act(lnv[:, :n], ps[b][:, :n], AF.Ln, [psk(b)], [lk], scale=1.0 / D, bias=EPS)
            act(rstd[:, :n], lnv[:, :n], AF.Exp, [lk], [rk], scale=-0.5)
            for k in range(8):
                tmp, tk = tmb.next()
                stt(tmp[:, :n], xT[:, k, t0:t0 + n], A[:, l, k, c:c + 1], rstd[:, :n], ALU.mult, ALU.mult,
                    [xk(g, k), Ak, rk], [tk])
                act(hT[:, k, t0:t0 + n], tmp[:, :n], AF.Identity, [tk, ("modT", l)], [("hT", g)],
                    bias=modcol(l, mshift, k, g))

    def ffn_phase(l, skip_ctx=False, ada_next=None):
        AR.reset()
        actb = Rot(AR.alloc("actT", [2, T], BF16, 2))
        sgb = Rot(AR.alloc("sg", [512], F32, 3))
        gb = Rot([0, 1])
        ub = Rot([2, 3])
        ob = Rot([4, 5, 6]) if ada_next is not None else Rot([4, 5, 6, 7])
        nsub = [0]
        ada_blocks = list(range(24)) if ada_next is not None else []

        def ada_block(blk):
            la = ada_next
            s_, key_ = WR.load([(0, [8, 256], wcols(w_ada[la], blk * 256, 256))])
            wv_ = WR.view(s_, 0, [8, 256])
            for sub in range(2):
                f = blk * 2 + sub
                for k in range(8):
                    mm(ps[7][:, 2 * f:2 * f + 2], wv_[:, k, sub * 128:(sub + 1) * 128], scT[:, k, :],
                       k == 0, k == 7, [key_, "scT"], [psk(7)])
            WR.release(s_)

        def out_units(prev):
            (aT, ak, wo, ko, so_) = prev
            for g, (t0, n) in enumerate(TGS):
                if skip_ctx and g == 0:
                    continue
                for d in range(8):
                    def unit(g=g, t0=t0, n=n, d=d):
                        b = ob.next()
                        for j in range(2):
                            mm(ps[b][:, :n], wo[:, j, d * 128:(d + 1) * 128], aT[:, j, t0:t0 + n], j == 0, j == 1,
                               [ko, (ak, g)], [psk(b)])
                        stt(xT[:, d, t0:t0 + n], ps[b][:, :n], modcol(l, 5, d, g), xT[:, d, t0:t0 + n],
                            ALU.mult, ALU.add, [psk(b), ("modT", l), xk(g, d)], [xk(g, d)])
                    yield unit

        prev = None
        for grp in range(12):
            pend = list(out_units(prev)) if prev is not None else []
            if grp < 11:
                sg_, kg = WR.load([(0, [8, 256], wcols(w_ffn_in[l], grp * 256, 256))])
                su_, ku = WR.load([(0, [8, 256], wcols(w_ffn_in[l], DFF + grp * 256, 256))])
                so_, ko = WR.load([(0, [2, 1024], wrows(w_ffn_out[l], grp * 256, 256))])
                wg = WR.view(sg_, 0, [8, 256])
                wu = WR.view(su_, 0, [8, 256])
                wo = WR.view(so_, 0, [2, 1024])
                aT, ak = actb.next()
                if nsub[0] < 2:
                    subkeys(ak)
                    nsub[0] += 1
                for j in range(2):
                    for g, (t0, n) in enumerate(TGS):
                        if skip_ctx and g == 0:
                            continue
                        bg_ = gb.next()
                        bu_ = ub.next()
                        for k in range(8):
                            mm(ps[bg_][:, :n], wg[:, k, j * 128:(j + 1) * 128], hT[:, k, t0:t0 + n], k == 0, k == 7,
                               [kg, ("hT", g)], [psk(bg_)])
                        for k in range(8):
                            mm(ps[bu_][:, :n], wu[:, k, j * 128:(j + 1) * 128], hT[:, k, t0:t0 + n], k == 0, k == 7,
                               [ku, ("hT", g)], [psk(bu_)])
                        sg, sk = sgb.next()
                        act(sg[:, :n], ps[bg_][:, :n], AF.Silu, [psk(bg_)], [sk])
                        tt("dve", aT[:, j, t0:t0 + n], sg[:, :n], ps[bu_][:, :n], ALU.mult, [sk, psk(bu_)], [(ak, g)])
                        for _ in range(4):
                            if pend:
                                pend.pop(0)()
                WR.release(sg_)
                WR.release(su_)
                for _ in range(3):
                    if ada_blocks:
                        ada_block(ada_blocks.pop(0))
            while pend:
                pend.pop(0)()
            if prev is not None:
                WR.release(prev[4])
            prev = (aT, ak, wo, ko, so_) if grp < 11 else None
        if ada_next is not None:
            while ada_blocks:
                ada_block(ada_blocks.pop(0))
            ada_finish(ada_next, 7)

    def outproj_tg(l, wo_view, wkey, oT, okey, g, banks):
        t0, n = TGS[g]
        for d in range(8):
            b = banks.next()
            mm(ps[b][:, :n], wo_view[:, d * 128:(d + 1) * 128], oT[:, :n], True, True, [wkey, okey], [psk(b)])
            stt(xT[:, d, t0:t0 + n], ps[b][:, :n], modcol(l, 2, d, g), xT[:, d, t0:t0 + n], ALU.mult, ALU.add,
                [psk(b), ("modT", l), xk(g, d)], [xk(g, d)])

    def scan_head(l, kind, h, wviews, wkeys):
        j = l // 2
        last_layer = (l == nlayers - 1) and cfg.get("ctx_skip", True)
        dk = 128 if kind == "hgrn" else 64
        qconst = math.log(128 ** -0.5) if kind == "hgrn" else math.log(64 ** -0.5)
        AR.reset()
        vtok, vk = AR.alloc("vtok", [NT, 128], BF16, 1)[0]
        oacc, ok_ = AR.alloc("oacc", [T], F32, 1)[0]
        subkeys(ok_)
        DB = []
        for dr in range(2):
            d_ = {}
            d_["tmpF"] = {nm: Rot(AR.alloc(f"t{dr}_" + nm, [512], F32, 1)) for nm in ("a", "b", "c", "d", "e", "f")}
            d_["opB"] = {nm: Rot(AR.alloc(f"o{dr}_" + nm, [512], BF16, 1)) for nm in ("qh", "kh", "qb", "kd")}
            d_["kdt"] = Rot(AR.alloc(f"kdtok{dr}", [4, 128], BF16, 1))
            d_["attm"] = Rot(AR.alloc(f"attm{dr}", [128], BF16, 2))
            d_["S"] = AR.alloc(f"S{dr}", [128], F32, 1)[0]
            d_["Sb"] = Rot(AR.alloc(f"Sb{dr}", [128], BF16, 2))
            d_["c0"] = Rot(AR.alloc(f"c0{dr}", [8], F32, 1))
            d_["lr"] = Rot(AR.alloc(f"lrT{dr}", [512], BF16, 1))
            d_["pA"] = 0 if dr == 0 else 2
            d_["pB"] = 1 if dr == 0 else 3
            d_["ob"] = 5 if dr == 0 else 6
            d_["ab"] = 4 if dr == 0 else 7
            DB.append(d_)
        postF = {nm: Rot(AR.alloc("p_" + nm, [512], F32, 1)) for nm in ("a", "b", "c")}
        sqb = Rot(AR.alloc("psq", [512], BF16, 1))
        oTb = Rot(AR.alloc("oT", [512], BF16, 2))

        def pq(b, q):
            return ("ps", b, q)

        vb = Rot([0, 1, 2, 3])
        for i0 in range(0, NT, 4):
            nt_ = min(4, NT - i0)
            b = vb.next()
            for q in range(nt_):
                i = i0 + q
                for k in range(8):
                    mm(ps[b][:, q * 128:(q + 1) * 128], hT[:, k, i * 128:(i + 1) * 128], wviews["v"][:, k, :],
                       k == 0, k == 7, [("hT", tg_of_tile(i)), wkeys["v"]], [psk(b)])
            cp("act", vtok[:, i0:i0 + nt_, :], ps[b][:, :nt_ * 128].rearrange("p (a b) -> p a b", b=128),
               [psk(b)], [vk])

        def prep_a(dr, g):
            D_ = DB[dr]
            tmpF = D_["tmpF"]
            bA, bB = D_["pA"], D_["pB"]
            t0, n = TGS[g]
            nch = n // 64
            rd = [("hT", g)]
            for k in range(8):
                mm(ps[bA][:dk, :n], wviews["q"][:, k, :], hT[:, k, t0:t0 + n], k == 0, k == 7, rd + [wkeys["q"]], [psk(bA)])
            qf, qk_ = tmpF["a"].next()
            if kind == "hgrn":
                act(qf[:dk, :n], ps[bA][:dk, :n], AF.Sigmoid, [psk(bA)], [qk_])
                tt("dve", qf[:dk, :n], qf[:dk, :n], ps[bA][:dk, :n], ALU.mult, [qk_, psk(bA)], [qk_])
            else:
                cp("act", qf[:dk, :n], ps[bA][:dk, :n], [psk(bA)], [qk_])
            yield
            kf, kk_ = tmpF["b"].next()
            gl, gk_ = tmpF["c"].next()
            if kind == "hgrn":
                wn = "gf" if dr == 0 else "gb"
                for k in range(8):
                    mm(ps[bB][:dk, :n], wviews[wn][:, k, :], hT[:, k, t0:t0 + n], k == 0, k == 7, rd + [wkeys[wn]], [psk(bB)])
                act(kf[:dk, :n], ps[bB][:dk, :n], AF.Sigmoid, [psk(bB)], [kk_])
                ts("dve", kf[:dk, :n], kf[:dk, :n], omT[:, dr, j, h:h + 1], lbT[:, dr, j, h:h + 1], ALU.mult, ALU.add,
                   [kk_, ("omT", dr, j), ("lbT", dr, j)], [kk_])
                yield
                act(gl[:dk, :n], kf[:dk, :n], AF.Ln, [kk_], [gk_])
                act(kf[:dk, :n], kf[:dk, :n], AF.Identity, [kk_], [kk_], scale=-1.0, bias=1.0)
            else:
                for k in range(8):
                    mm(ps[bB][:dk, :n], wviews["k"][:, k, :], hT[:, k, t0:t0 + n], k == 0, k == 7, rd + [wkeys["k"]], [psk(bB)])
                cp("act", kf[:dk, :n], ps[bB][:dk, :n], [psk(bB)], [kk_])
                yield
                for k in range(8):
                    mm(ps[bA][:32, :n], wviews["lr"][:, k, :], hT[:, k, t0:t0 + n], k == 0, k == 7, rd + [wkeys["lr"]], [psk(bA)])
                lr, lrk = D_["lr"].next()
                cp("dve", lr[:32, :n], ps[bA][:32, :n], [psk(bA)], [lrk])
                mm(ps[bB][:dk, :n], wgu[:, j, dr, h * 64:(h + 1) * 64], lr[:32, :n], True, True, [lrk, "wgu"], [psk(bB)])
                act(gl[:dk, :n], ps[bB][:dk, :n], AF.Exp, [psk(bB), "bgu"], [gk_], scale=-1.0, bias=bgu[:, j, dr, h:h + 1])
                yield
                act(gl[:dk, :n], gl[:dk, :n], AF.Ln, [gk_], [gk_], bias=1.0)
                ts("dve", gl[:dk, :n], gl[:dk, :n], -1.0 / 16.0, None, ALU.mult, None, [gk_], [gk_])
            yield
            bb, bk_ = tmpF["d"].next()
            if dr == 0:
                P.op("dve", lambda h_: h_.tensor_tensor_scan(out=bb[:dk, :n], data0=cst["rmf"][:dk, :n], data1=gl[:dk, :n],
                                                             initial=0.0, op0=ALU.mult, op1=ALU.add),
                     [gk_, "rmf"], [bk_])
            else:
                P.op("dve", lambda h_: h_.tensor_tensor_scan(out=bb[:dk, :n][:, ::-1],
                                                             data0=cst["rmb"][:dk, :n][:, ::-1],
                                                             data1=gl[:dk, :n][:, ::-1],
                                                             initial=0.0, op0=ALU.mult, op1=ALU.add),
                     [gk_, "rmb"], [bk_])
            yield
            b3 = bb[:dk, :n].rearrange("p (c t) -> p c t", t=64)
            mid = 31 if dr == 0 else 32
            rr, rk_ = tmpF["c"].next()
            tt("dve", rr[:dk, :n].rearrange("p (c t) -> p c t", t=64), b3,
               b3[:, :, mid:mid + 1].to_broadcast([dk, nch, 64]), ALU.subtract, [bk_], [rk_])
            ts("dve", rr[:dk, :n], rr[:dk, :n], 40.0, -40.0, ALU.min, ALU.max, [rk_], [rk_])
            yield
            ep, epk = tmpF["e"].next()
            act(ep[:dk, :n], rr[:dk, :n], AF.Exp, [rk_], [epk], bias=qconst)
            en, enk = tmpF["f"].next()
            act(en[:dk, :n], rr[:dk, :n], AF.Exp, [rk_], [enk], scale=-1.0)
            yield
            return dict(qf=(qf, qk_), kf=(kf, kk_), bb=(bb, bk_), ep=(ep, epk), en=(en, enk))

        def prep_b(dr, g, pa):
            D_ = DB[dr]
            tmpF, opB = D_["tmpF"], D_["opB"]
            bA = D_["pA"]
            t0, n = TGS[g]
            nch = n // 64
            qf, qk_ = pa["qf"]
            kf, kk_ = pa["kf"]
            bb, bk_ = pa["bb"]
            ep, epk = pa["ep"]
            en, enk = pa["en"]
            b3 = bb[:dk, :n].rearrange("p (c t) -> p c t", t=64)
            last = 63 if dr == 0 else 0
            qh, qhk = opB["qh"].next()
            tt("dve", qh[:dk, :n], qf[:dk, :n], ep[:dk, :n], ALU.mult, [qk_, epk], [qhk])
            kh, khk = opB["kh"].next()
            tt("dve", kh[:dk, :n], kf[:dk, :n], en[:dk, :n], ALU.mult, [kk_, enk], [khk])
            eb, ebk = tmpF["e"].next()
            act(eb[:dk, :n], bb[:dk, :n], AF.Exp, [bk_], [ebk], bias=qconst)
            dd, ddk = tmpF["f"].next()
            tt("dve", dd[:dk, :n].rearrange("p (c t) -> p c t", t=64), b3,
               b3[:, :, last:last + 1].to_broadcast([dk, nch, 64]), ALU.subtract, [bk_], [ddk])
            act(dd[:dk, :n], dd[:dk, :n], AF.Exp, [ddk], [ddk], scale=-1.0)
            c0, c0k = D_["c0"].next()
            act(c0[:dk, :nch], b3[:, :, last], AF.Exp, [bk_], [c0k])
            qb, qbk = opB["qb"].next()
            tt("dve", qb[:dk, :n], qf[:dk, :n], eb[:dk, :n], ALU.mult, [qk_, ebk], [qbk])
            kd, kdk = opB["kd"].next()
            tt("dve", kd[:dk, :n], kf[:dk, :n], dd[:dk, :n], ALU.mult, [kk_, ddk], [kdk])
            kt, ktk = D_["kdt"].next()
            ntile = n // 128
            for q in range(ntile):
                tr(psb[bA][:, q * 128:q * 128 + dk], kd[:dk, q * 128:(q + 1) * 128], cst["identb"][:dk, :dk],
                   [kdk, "identb"], [psk(bA)])
            cp("dve", kt[:, :ntile, :dk], psb[bA][:, :ntile * 128].rearrange("p (a b) -> p a b", b=128)[:, :, :dk],
               [psk(bA)], [ktk])
            return dict(qh=(qh, qhk), kh=(kh, khk), qb=(qb, qbk), kt=(kt, ktk), c0=(c0, c0k))

        def scan_tile(dr, g, i, pr, state):
            need_o = not (last_layer and g == 0)
            D_ = DB[dr]
            Sf, Sk = D_["S"]
            t0, n = TGS[g]
            q = i - t0 // 128
            cols = slice(q * 128, (q + 1) * 128)
            qh, qhk = pr["qh"]
            kh, khk = pr["kh"]
            qb, qbk = pr["qb"]
            kt, ktk = pr["kt"]
            c0, c0k = pr["c0"]
            ab = D_["ab"]
            ac = slice(0, 128)
            uc = slice(256, 384)
            bo = D_["ob"]
            if need_o:
                mm(ps[ab][:, ac], kh[:dk, cols], qh[:dk, cols], True, True, [khk, qhk], [psk(ab)])
                am, amk = D_["attm"].next()
                mk = "maskf" if dr == 0 else "maskb"
                tt("dve", am[:, :], ps[ab][:, ac], cst[mk][:], ALU.mult, [psk(ab), mk], [amk])
                mm(ps[bo][:, 0:128], vtok[:, i, :], am[:, :], True, False, [vk, amk], [psk(bo)])
            order = (0, 1) if dr == 0 else (1, 0)
            for ci, cch in enumerate(order):
                ccols = slice(q * 128 + cch * 64, q * 128 + (cch + 1) * 64)
                rows = slice(cch * 64, (cch + 1) * 64)
                sbf, sbk = state["sb"]
                if need_o:
                    mm(ps[bo][:, cch * 64:(cch + 1) * 64], sbf[:dk, :], qb[:dk, ccols], False, ci == 1,
                       [sbk, qbk], [psk(bo)])
                mm(ps[ab][:dk, uc], kt[rows, q, :dk], vtok[rows, i, :], True, True, [ktk, vk], [psk(ab)])
                chunk_idx = q * 2 + cch
                stt(Sf[:dk, :], Sf[:dk, :], c0[:dk, chunk_idx:chunk_idx + 1], ps[ab][:dk, uc], ALU.mult, ALU.add,
                    [Sk, c0k, psk(ab)], [Sk])
                nsb, nsk = D_["Sb"].next()
                cp("act", nsb[:dk, :], Sf[:dk, :], [Sk], [nsk])
                state["sb"] = (nsb, nsk)
                if ci == 0:
                    yield

        def tiles_of(g):
            t0, n = TGS[g]
            return list(range(t0 // 128, (t0 + n) // 128))

        gain_ap = (ognT if kind == "hgrn" else glagT)[:, j:j + 1]
        gain_key = "ognT" if kind == "hgrn" else "glagT"
        visited = set()
        done_g = {g: 0 for g in range(5)}

        def post(g, dr):
            D_ = DB[dr]
            bA, bB = D_["pA"], D_["pB"]
            t0, n = TGS[g]
            sq, sk = sqb.next()
            act(sq[:, :n], oacc[:, t0:t0 + n], AF.Square, [(ok_, g)], [sk])
            mm(ps[bA][:, :n], cst["ones"][:], sq[:, :n], True, True, [sk, "ones"], [psk(bA)])
            ra, rak = postF["a"].next()
            act(ra[:, :n], ps[bA][:, :n], AF.Ln, [psk(bA)], [rak], scale=1.0 / 128, bias=EPS)
            act(ra[:, :n], ra[:, :n], AF.Exp, [rak], [rak], scale=-0.5)
            for k in range(8):
                mm(ps[bB][:, :n], wviews["g"][:, k, :], hT[:, k, t0:t0 + n], k == 0, k == 7, [("hT", g), wkeys["g"]], [psk(bB)])
            sg, sgk = postF["b"].next()
            act(sg[:, :n], ps[bB][:, :n], AF.Silu, [psk(bB)], [sgk])
            t1, t1k = postF["c"].next()
            stt(t1[:, :n], oacc[:, t0:t0 + n], gain_ap, ra[:, :n], ALU.mult, ALU.mult, [(ok_, g), gain_key, rak], [t1k])
            oT, oTk = oTb.next()
            tt("dve", oT[:, :n], t1[:, :n], sg[:, :n], ALU.mult, [t1k, sgk], [oTk])
            outproj_tg(l, wviews["wo"], wkeys["wo"], oT, oTk, g, Rot([bA, bB]))

        def run_gen(gen):
            try:
                while True:
                    next(gen)
            except StopIteration as e_:
                return e_.value

        def step_gen(gen_box):
            if gen_box[2]:
                return
            try:
                next(gen_box[0])
            except StopIteration as e_:
                gen_box[1] = e_.value
                gen_box[2] = True

        def dirgen(dr, groups):
            D_ = DB[dr]
            Sf, Sk = D_["S"]
            P.op("pool", lambda h_: h_.memset(Sf[:, :], 0.0), [], [Sk])
            sb0, sbk0 = D_["Sb"].next()
            P.op("pool", lambda h_: h_.memset(sb0[:, :], 0.0), [], [sbk0])
            state = {"sb": (sb0, sbk0)}
            box = [prep_a(dr, groups[0]), None, False]
            while not box[2]:
                step_gen(box)
                yield
            for gi, g in enumerate(groups):
                tiles = tiles_of(g)
                if dr == 1:
                    tiles = tiles[::-1]
                pr = prep_b(dr, g, box[1])
                yield
                if gi + 1 < len(groups):
                    box = [prep_a(dr, groups[gi + 1]), None, False]
                for i in tiles:
                    for _ in scan_tile(dr, g, i, pr, state):
                        step_gen(box)
                        yield
                    bo = D_["ob"]
                    if last_layer and g == 0:
                        step_gen(box)
                        yield
                        continue
                    if i not in visited:
                        visited.add(i)
                        cp("act", oacc[:, i * 128:(i + 1) * 128], ps[bo][:, 0:128], [psk(bo)], [(ok_, g)])
                    else:
                        tt("dve", oacc[:, i * 128:(i + 1) * 128], ps[bo][:, 0:128], oacc[:, i * 128:(i + 1) * 128],
                           ALU.add, [psk(bo), (ok_, g)], [(ok_, g)])
                    done_g[g] += 1
                    step_gen(box)
                    if done_g[g] == 2 * len(tiles):
                        post(g, dr)
                    yield
                while not box[2]:
                    step_gen(box)
                    yield

        gens = [dirgen(0, [0, 1, 2, 3, 4]), dirgen(1, [0, 4, 3, 2, 1])]
        alive = [True, True]
        while any(alive):
            for dr in range(2):
                if alive[dr]:
                    try:
                        next(gens[dr])
                    except StopIteration:
                        alive[dr] = False

    def odd_mixer(l):
        j = l // 2
        W = w_in_odd[j]
        for h in range(8):
            s1, k1 = WR.load([(0, [8, 128], wcols(W, 0 * 1024 + h * 128, 128)),
                              (1024, [8, 128], wcols(W, 1 * 1024 + h * 128, 128))])
            s2, k2 = WR.load([(0, [8, 128], wcols(W, 2 * 1024 + h * 128, 128)),
                              (1024, [8, 128], wcols(W, 3 * 1024 + h * 128, 128))])
            s3, k3 = WR.load([(0, [8, 128], wcols(W, 4 * 1024 + h * 128, 128)),
                              (1024, [1024], w_out_odd[j][h * 128:(h + 1) * 128, :])])
            wv = dict(q=WR.view(s1, 0, [8, 128]), gf=WR.view(s1, 1024, [8, 128]),
                      gb=WR.view(s2, 0, [8, 128]), v=WR.view(s2, 1024, [8, 128]),
                      g=WR.view(s3, 0, [8, 128]), wo=WR.view(s3, 1024, [1024]))
            wk = dict(q=k1, gf=k1, gb=k2, v=k2, g=k3, wo=k3)
            scan_head(l, "hgrn", h, wv, wk)
            WR.release(s1)
            WR.release(s2)
            WR.release(s3)

    def gla_mixer(l):
        j = l // 2
        W = w_in_even[j]
        for h in range(4):
            s1, k1 = WR.load([(0, [8, 64], wcols(W, 1536 + h * 64, 64)),
                              (512, [8, 64], wcols(W, 1792 + h * 64, 64)),
                              (1024, [8, 32], wcols(W, 3072, 32))])
            s2, k2 = WR.load([(0, [8, 128], wcols(W, 2048 + h * 128, 128)),
                              (1024, [8, 128], wcols(W, 2560 + h * 128, 128))])
            s3, k3 = WR.load([(0, [1024], w_out_even[j][512 + h * 128:512 + (h + 1) * 128, :])])
            wv = dict(q=WR.view(s1, 0, [8, 64]), k=WR.view(s1, 512, [8, 64]), lr=WR.view(s1, 1024, [8, 32]),
                      v=WR.view(s2, 0, [8, 128]), g=WR.view(s2, 1024, [8, 128]), wo=WR.view(s3, 0, [1024]))
            wk = dict(q=k1, k=k1, lr=k1, v=k2, g=k2, wo=k3)
            scan_head(l, "gla", h, wv, wk)
            WR.release(s1)
            WR.release(s2)
            WR.release(s3)

    def att_mixer(l):
        j = l // 2
        W = w_in_even[j]
        AR.reset()
        kvb = []
        for i_ in range(2):
            kT_, kTk_ = AR.alloc(f"kT{i_}", [T], BF16, 1)[0]
            subkeys(kTk_)
            vt_, vk_ = AR.alloc(f"vtokA{i_}", [NT, 132], BF16, 1)[0]
            P.op("pool", lambda h_, vt_=vt_: h_.memset(vt_[:, :, 128:132], 1.0), [], [vk_])
            kvb.append((kT_, kTk_, vt_, vk_))
        qTb = Rot(AR.alloc("qT", [512], BF16, 2))
        PTb = Rot(AR.alloc("PT", [512], BF16, 4))
        sqb = Rot(AR.alloc("asq", [512], BF16, 2))
        rawb = Rot(AR.alloc("araw", [512], F32, 2))
        rsb = Rot(AR.alloc("arstd", [512], F32, 2))
        qgf = Rot(AR.alloc("aqg", [512], F32, 2))
        qgb = Rot(AR.alloc("aqgb", [512], BF16, 2))
        t1b = Rot(AR.alloc("at1", [512], F32, 1))
        t2b = Rot(AR.alloc("at2", [512], F32, 1))
        cosb = Rot(AR.alloc("cos", [512], F32, 2))
        sinb = Rot(AR.alloc("sin", [512], F32, 2))
        ptb = Rot(AR.alloc("pt", [128], F32, 2))
        pob = Rot(AR.alloc("po", [128], F32, 4))
        ponb = Rot(AR.alloc("pon", [128], BF16, 4))
        stb = Rot(AR.alloc("stat", [8], F32, 4))
        oTb = Rot(AR.alloc("oTa", [512], BF16, 2))
        sqj = Rot(AR.alloc("sqjunk", [128], F32, 1))
        zt, ztk = AR.alloc("zero512", [512], BF16, 1)[0]
        P.op("pool", lambda h_: h_.memset(zt[:, :], 0.0), [], [ztk])
        rope_cnt = [0]
        MB = 7

        def qk_prep(wview, wkey, qk, g, out, outkey, outcols):
            t0, n = TGS[g]
            for k in range(8):
                mm(ps[MB][:, :n], wview[:, k, :], hT[:, k, t0:t0 + n], k == 0, k == 7, [("hT", g), wkey], [psk(MB)])
            raw, rwk = rawb.next()
            cp("act", raw[:, :n], ps[MB][:, :n], [psk(MB)], [rwk])
            sq, sk = sqb.next()
            act(sq[:, :n], ps[MB][:, :n], AF.Square, [psk(MB)], [sk])
            yield
            mm(ps[MB][:, :n], cst["blk64"][:], sq[:, :n], True, True, [sk, "blk64"], [psk(MB)])
            rs, rk = rsb.next()
            act(rs[:, :n], ps[MB][:, :n], AF.Ln, [psk(MB)], [rk], scale=1.0 / 64, bias=EPS)
            act(rs[:, :n], rs[:, :n], AF.Exp, [rk], [rk], scale=-0.5)
            if g == 0:
                stt(out[:, outcols], raw[:, :n], qkg[:, j, qk:qk + 1], rs[:, :n], ALU.mult, ALU.mult,
                    [rwk, "qkg", rk], [outkey])
                return
            qg, qgk = qgf.next()
            stt(qg[:, :n], raw[:, :n], qkg[:, j, qk:qk + 1], rs[:, :n], ALU.mult, ALU.mult, [rwk, "qkg", rk], [qgk])
            qb_, qbk = qgb.next()
            cp("act", qb_[:, :n], qg[:, :n], [qgk], [qbk])
            cs, ck = cosb.next()
            sn, snk = sinb.next()
            l0 = t0 - CTX
            r = rope_cnt[0]
            rope_cnt[0] += 1
            P.dma("sp", f"rp{r % 2}", lambda h_: h_.dma_start(out=cs[:, :n], in_=dram["cosT"][:, l0:l0 + n]), writes=[ck])
            P.dma("sp", f"rp{r % 2}", lambda h_: h_.dma_start(out=sn[:, :n], in_=dram["sinT"][:, l0:l0 + n]), writes=[snk])
            t1, t1k = t1b.next()
            tt("dve", t1[:, :n], qg[:, :n], cs[:, :n], ALU.mult, [qgk, ck], [t1k])
            yield
            mm(ps[MB][:, :n], cst["rotm"][:], qb_[:, :n], True, True, [qbk, "rotm"], [psk(MB)])
            t2, t2k = t2b.next()
            tt("dve", t2[:, :n], ps[MB][:, :n], sn[:, :n], ALU.mult, [psk(MB), snk], [t2k])
            tt("dve", out[:, outcols], t1[:, :n], t2[:, :n], ALU.add, [t1k, t2k], [outkey])

        HW = {}

        def load_head(h):
            sa, ka = WR.load([(0, [8, 128], wcols(W, h * 128, 128)), (1024, [8, 128], wcols(W, 512 + h * 128, 128))])
            sb_, kb = WR.load([(0, [8, 128], wcols(W, 1024 + h * 128, 128)),
                               (1024, [1024], w_out_even[j][h * 128:(h + 1) * 128, :])])
            HW[h] = dict(sa=sa, sb=sb_, ka=ka, kb=kb, wq=WR.view(sa, 0, [8, 128]), wk=WR.view(sa, 1024, [8, 128]),
                         wv=WR.view(sb_, 0, [8, 128]), wo=WR.view(sb_, 1024, [1024]))

        def k_unit(h, g):
            kT_, kTk_, vt_, vk_ = kvb[h % 2]
            t0, n = TGS[g]
            return lambda: qk_prep(HW[h]["wk"], HW[h]["ka"], 1, g, kT_, (kTk_, g), slice(t0, t0 + n))

        def v_unit(h, i0):
            kT_, kTk_, vt_, vk_ = kvb[h % 2]

            def f():
                nt_ = min(4, NT - i0)
                for q in range(nt_):
                    i = i0 + q
                    for k in range(8):
                        mm(ps[MB][:, q * 128:(q + 1) * 128], hT[:, k, i * 128:(i + 1) * 128], HW[h]["wv"][:, k, :],
                           k == 0, k == 7, [("hT", tg_of_tile(i)), HW[h]["kb"]], [psk(MB)])
                cp("dve", vt_[:, i0:i0 + nt_, 0:128], ps[MB][:, :nt_ * 128].rearrange("p (a b) -> p a b", b=128),
                   [psk(MB)], [vk_])
                return
                yield
            return f

        qready = {}

        def q_unit(h, g):
            def f():
                qT, qTk = qTb.next()
                qready[(h, g)] = (qT, qTk)
                yield from qk_prep(HW[h]["wq"], HW[h]["ka"], 0, g, qT, qTk, slice(0, TGS[g][1]))
            return f

        bg = []

        def bg_add(unit):
            bg.append(unit())

        def drain(nu=1):
            for _ in range(nu):
                while bg:
                    try:
                        next(bg[0])
                        break
                    except StopIteration:
                        bg.pop(0)

        def flush():
            while bg:
                try:
                    next(bg[0])
                except StopIteration:
                    bg.pop(0)

        def run_now(unit):
            for _ in unit():
                pass

        load_head(0)
        for g in range(5):
            run_now(k_unit(0, g))
        for i0 in range(0, NT, 4):
            run_now(v_unit(0, i0))
        run_now(q_unit(0, 0))
        for h in range(4):
            kT, kTk, vtok, vk = kvb[h % 2]
            if h + 1 < 4:
                load_head(h + 1)
            plan = {0: [], 1: [], 2: [], 3: [], 4: []}
            plan[0].append(q_unit(h, 1))
            plan[1].append(q_unit(h, 2))
            plan[2].append(q_unit(h, 3))
            plan[3].append(q_unit(h, 4))
            if h + 1 < 4:
                for g_ in range(3):
                    plan[1].append(k_unit(h + 1, g_))
                for g_ in range(3, 5):
                    plan[2].append(k_unit(h + 1, g_))
                vs_ = list(range(0, NT, 4))
                for i0 in vs_[:2]:
                    plan[2].append(v_unit(h + 1, i0))
                for i0 in vs_[2:]:
                    plan[3].append(v_unit(h + 1, i0))
                plan[4].append(q_unit(h + 1, 0))
            for g in range(5):
                t0, n = TGS[g]
                nq = n // 128
                for u_ in plan[g]:
                    bg_add(u_)
                qT, qTk = qready[(h, g)]
                ktiles = list(range(2)) if g == 0 else list(range(NT))
                sbanks = Rot([4, 5, 6])
                for qt in range(nq):
                    mm(ps[qt][:, 0:512], zt[:, 0:128], zt[:, 0:512], True, False, [ztk], [psk(qt)])
                prev = None
                for ki, kt in enumerate(ktiles):
                    cur = []
                    for c in range(2):
                        bs = sbanks.next()
                        mm(ps[bs][:, :n], kT[c * 64:(c + 1) * 64, kt * 128:(kt + 1) * 128], qT[c * 64:(c + 1) * 64, :n],
                           True, True, [(kTk, tg_of_tile(kt)), qTk], [psk(bs)])
                        cur.append((c, bs))
                    its = []
                    for (c, bs) in cur:
                        pt, ptk = PTb.next()
                        act(pt[:, :n], ps[bs][:, :n], AF.Exp, [psk(bs)], [ptk], scale=0.125)
                        its.append((c, kt, pt, ptk))
                    if prev is not None:
                        for it in prev:
                            emit_av(it, nq, ktiles, vtok, vk)
                    prev = its
                    if g > 0:
                        drain(1)
                for it in prev:
                    emit_av(it, nq, ktiles, vtok, vk)
                flush()
                items = []
                for qt in range(nq):
                    acc = ps[qt]
                    sts, stk = stb.next()
                    P.op("dve", lambda h_, acc=acc, sts=sts: h_.reciprocal(out=sts[:, 0:1], in_=acc[:, 128:129]),
                         [psk(qt)], [stk])
                    P.op("dve", lambda h_, acc=acc, sts=sts: h_.reciprocal(out=sts[:, 1:2], in_=acc[:, 384:385]),
                         [psk(qt), stk], [stk])
                    tt("dve", sts[:, 2:3], sts[:, 1:2], nlam[:, j:j + 1], ALU.mult, [stk, ("nlam", j)], [stk])
                    t_, tk_ = ptb.next()
                    ts("dve", t_[:, :], acc[:, 256:384], sts[:, 2:3], None, ALU.mult, None, [psk(qt), stk], [tk_])
                    o_, ok2 = pob.next()
                    stt(o_[:, :], acc[:, 0:128], sts[:, 0:1], t_[:, :], ALU.mult, ALU.add, [psk(qt), stk, tk_], [ok2])
                    items.append((o_, ok2, sts, stk))

                def post_rest(items=items, g=g, n=n, h=h):
                    t0_ = TGS[g][0]
                    oT, oTk = oTb.next()
                    ons = []
                    for (o_, ok2, sts, stk) in items:
                        jk, jkk = sqj.next()
                        act(jk[:, :], o_[:, :], AF.Square, [ok2], [jkk, stk], accum_out=sts[:, 3:4])
                        act(sts[:, 4:5], sts[:, 3:4], AF.Ln, [stk], [stk], scale=1.0 / 128, bias=EPS)
                        act(sts[:, 5:6], sts[:, 4:5], AF.Exp, [stk], [stk], scale=-0.5)
                        on_, onk = ponb.next()
                        ts("dve", on_[:, :], o_[:, :], sts[:, 5:6], None, ALU.mult, None, [ok2, stk], [onk])
                        ons.append((on_, onk))
                    yield
                    for qt, (on_, onk) in enumerate(ons):
                        tr(psb[MB][:, qt * 128:(qt + 1) * 128], on_[:, :], cst["identb"][:], [onk, "identb"], [psk(MB)])
                    P.op("act", lambda h_, oT=oT, n=n: h_.activation(out=oT[:, :n], in_=psb[MB][:, :n], func=AF.Copy,
                                                                  scale=sublnT[:, j:j + 1]),
                         [psk(MB), "sublnT"], [oTk])
                    yield
                    wo_ = HW[h]["wo"]
                    for d in range(8):
                        mm(ps[MB][:, :n], wo_[:, d * 128:(d + 1) * 128], oT[:, :n], True, True, [HW[h]["kb"], oTk], [psk(MB)])
                        stt(xT[:, d, t0_:t0_ + n], ps[MB][:, :n], modcol(l, 2, d, g), xT[:, d, t0_:t0_ + n], ALU.mult, ALU.add,
                            [psk(MB), ("modT", l), xk(g, d)], [xk(g, d)])
                        yield

                bg_add(post_rest)
            flush()
            WR.release(HW[h]["sa"])
            WR.release(HW[h]["sb"])

    def emit_av(pend, nq, ktiles, vtok, vk):
        c, kt, pt, ptk = pend
        for qt in range(nq):
            mm(ps[qt][:, c * 256:c * 256 + 129], pt[:, qt * 128:(qt + 1) * 128], vtok[:, kt, 0:129],
               False, (kt == ktiles[-1]) and c == 1, [ptk, vk], [psk(qt)])

    ada_layer(0)
    for l in range(nlayers):
        norm_phase(l, 1)
        if l % 2 == 0:
            if do_att:
                att_mixer(l)
            if do_gla:
                gla_mixer(l)
        else:
            if do_odd:
                odd_mixer(l)
        last = (l == nlayers - 1) and cfg.get("ctx_skip", True)
        norm_phase(l, 2, skip_ctx=last)
        if do_ffn and cfg.get("ada_inter", False):
            ffn_phase(l, skip_ctx=last, ada_next=(l + 1 if l + 1 < nlayers else None))
        else:
            if l + 1 < nlayers:
                ada_layer(l + 1)
            if do_ffn:
                ffn_phase(l, skip_ctx=last)

    AR.reset()
    ob = Rot(AR.alloc("otile", [1024], F32, 3))
    evac = Rot(["act", "dve"])
    cnt = 0
    for i in range(2, NT):
        ot, okk = ob.next()
        g = tg_of_tile(i)
        for half in range(2):
            b = (cnt) % 4
            cnt += 1
            for q in range(4):
                k = half * 4 + q
                tr(ps[b][:, q * 128:(q + 1) * 128], xT[:, k, i * 128:(i + 1) * 128], cst["identf"][:],
                   [xk(g, k), "identf"], [psk(b)])
            cp(evac.next(), ot[:, half * 512:(half + 1) * 512], ps[b][:, :], [psk(b)], [okk])
        P.dma("sp", f"out{i % 3}", lambda h_, ot=ot, i=i: h_.dma_start(out=y[(i - 2) * 128:(i - 1) * 128, :], in_=ot),
              reads=[okk], writes=[("y", i)])
    P.finish("sp")
    if cfg.get("verbose"):
        print("ops per engine:", {e: len(P.ops[e]) + len(P.hoisted[e]) for e in ENGS}, "arena", AR.off)
    P.emit()
    st.close()
    return nc


def host_layout(inp, b):
    f = lambda a: np.ascontiguousarray(a, dtype=np.float32)
    m = {}
    m["xin"] = f(np.concatenate([inp["ctx"][b], inp["x"][b]], axis=0))
    for nm in ("w_ada", "w_in_even", "w_out_even", "w_in_odd", "w_out_odd", "w_ffn_in", "w_ffn_out"):
        m[nm] = inp[nm]
    fm = lambda v: np.asarray(v).reshape(-1, 128).T
    cv = np.stack([fm(inp["c"][b]), fm(inp["c_ctx"])], axis=-1)
    m["cvec"] = f(cv.reshape(128, 16))
    m["badaT"] = f(np.stack([fm(inp["b_ada"][l]) for l in range(4)], axis=1).reshape(128, 4 * 48))
    m["g1T"] = f(np.stack([fm(inp["norm1_gain"][l]) for l in range(4)], axis=1).reshape(128, 32))
    m["g2T"] = f(np.stack([fm(inp["norm2_gain"][l]) for l in range(4)], axis=1).reshape(128, 32))
    qk = np.asarray(inp["qk_gain_a"])
    m["qkg"] = f(np.tile(qk.transpose(2, 0, 1), (2, 1, 1)).reshape(128, 4))
    m["lamb"] = f(np.broadcast_to(np.asarray(inp["lambda_a"]).reshape(1, 512), (128, 512)))
    m["sublnT"] = f(np.asarray(inp["subln_gain_a"]).T)
    wg = np.asarray(inp["w_gate_up_b"])
    wgp = np.zeros((32, 2, 2, 256), np.float32)
    for j in range(2):
        for dr in range(2):
            wgp[dr * 16:(dr + 1) * 16, j, dr, :] = wg[j, dr]
    m["wgu"] = f(wgp.reshape(32, 1024))
    bg = np.asarray(inp["b_gate_up_b"]).reshape(2, 2, 4, 64)
    m["bgu"] = f(bg.transpose(3, 0, 1, 2).reshape(64, 16))
    m["glagT"] = f(np.asarray(inp["onorm_gain_b"]).T)
    lb = np.asarray(inp["lb_raw_c"]).reshape(2, 4, 8, 128)
    m["lbraw"] = f(lb.transpose(3, 0, 1, 2).reshape(128, 64))
    m["ognT"] = f(np.asarray(inp["onorm_gain_c"]).T)
    return m


_CACHE = {}


def run(inputs, cfg=None, trace=False, ncores=8):
    cfg = cfg or {}
    key = tuple(sorted(cfg.items()))
    if key not in _CACHE:
        _CACHE[key] = build_program(cfg)
    nc = _CACHE[key]
    inp = {k: np.asarray(v) for k, v in inputs.items()}
    consts = host_consts()
    in_maps = []
    for b in range(ncores):
        m = host_layout(inp, b)
        m.update(consts)
        in_maps.append(m)
    res = run_bass_kernel_spmd(nc, in_maps, core_ids=list(range(ncores)), trace=trace)
    out = np.stack([np.asarray(r["y"]) for r in res.results], axis=0).astype(np.float32)
    return out, res


def kernel(**inputs):
    out, _ = run(inputs)
    return out
```
